# Optimizing a Trainium2 kernel written in Bass

```python
import jax, jax.numpy as jnp
from jax import lax
import numpy as np

D_MODEL = 2048
BATCH = 2
SEQ = 4096
DEPTH = 4

N_MIXERS = 3
EPS = 1e-6
D_FF = 5504
POOL_WINDOWS = (2, 4, 8, 16)
N_POOL_GROUPS = len(POOL_WINDOWS)
POOL_GROUP = D_MODEL // N_POOL_GROUPS
RET_HEADS = 8
RET_DK = D_MODEL // RET_HEADS
RET_DV = 2 * D_MODEL // RET_HEADS
RET_CHUNK = 128
SWA_HEADS = 32
SWA_KV_HEADS = 4
SWA_HD = 64
SWA_GROUP = SWA_HEADS // SWA_KV_HEADS
SWA_WINDOW = 128
SWA_BLOCK = 128
ROPE_THETA = 10000.0
N_POOL_LAYERS = (DEPTH + 2) // 3
N_RET_LAYERS = (DEPTH + 1) // 3
N_SWA_LAYERS = DEPTH // 3

kernel_name = "hybrid_pool_retention_swa_macaron"


def rms_norm(x, g):
    xf = x.astype(jnp.float32)
    y = xf * lax.rsqrt(jnp.mean(xf * xf, axis=-1, keepdims=True) + EPS)
    return (y * g.astype(jnp.float32)).astype(x.dtype)


def apply_rotary(x, positions, inv_freq):
    ang = positions.astype(jnp.float32)[:, :, None] * inv_freq[None, None, :]
    cos = jnp.cos(ang)[:, :, None, :]
    sin = jnp.sin(ang)[:, :, None, :]
    x1, x2 = jnp.split(x.astype(jnp.float32), 2, axis=-1)
    out = jnp.concatenate([x1 * cos - x2 * sin, x2 * cos + x1 * sin], axis=-1)
    return out.astype(x.dtype)


def swiglu(x, w_in, w_out):
    gate, up = jnp.split(x @ w_in, 2, axis=-1)
    return (jax.nn.silu(gate) * up) @ w_out


def pool_mixer(x, w_group, scale):
    B, S, D = x.shape
    xf = x.astype(jnp.float32).reshape(B, S, N_POOL_GROUPS, POOL_GROUP)
    cs = jnp.pad(jnp.cumsum(xf, axis=1), ((0, 0), (1, 0), (0, 0), (0, 0)))
    t = jnp.arange(1, S + 1)[:, None]
    win = jnp.array(POOL_WINDOWS)[None, :]
    lo = jnp.maximum(t - win, 0)
    cs_lo = cs[:, lo, jnp.arange(N_POOL_GROUPS)[None, :]]
    count = jnp.minimum(t, win).astype(jnp.float32)
    pooled = (cs[:, 1:] - cs_lo) / count[None, :, :, None]
    mixed = (pooled - xf).astype(x.dtype)
    y = jnp.einsum('bsgc,gcd->bsgd', mixed, w_group).reshape(B, S, D)
    return y * scale


def retention_mixer(x, positions, w_in, gn_g, gn_b, w_out):
    B, S, D = x.shape
    N = S // RET_CHUNK
    hproj = x @ w_in
    q, k, v, g = jnp.split(hproj, [D, 2 * D, 4 * D], axis=-1)
    inv_freq = ROPE_THETA ** (-jnp.linspace(0.0, 1.0, RET_DK // 2, dtype=jnp.float32))
    q = apply_rotary(q.reshape(B, S, RET_HEADS, RET_DK), positions, inv_freq)
    k = apply_rotary(k.reshape(B, S, RET_HEADS, RET_DK), positions, inv_freq) * (RET_DK ** -0.5)
    v = v.reshape(B, S, RET_HEADS, RET_DV)

    def to_chunks(t):
        return t.astype(jnp.float32).reshape(B, N, RET_CHUNK, RET_HEADS, t.shape[-1]).transpose(1, 0, 3, 2, 4)

    log_gamma = jnp.log(1.0 - 2.0 ** (-5.0 - jnp.arange(RET_HEADS, dtype=jnp.float32)))
    idx = jnp.arange(RET_CHUNK, dtype=jnp.float32)
    rel = idx[:, None] - idx[None, :]
    inner_decay = jnp.where(rel[None] >= 0, jnp.exp(jnp.maximum(rel, 0.0)[None] * log_gamma[:, None, None]), 0.0)
    cross_decay = jnp.exp((idx + 1.0)[None, :] * log_gamma[:, None])
    state_decay = jnp.exp((RET_CHUNK - 1.0 - idx)[None, :] * log_gamma[:, None])
    chunk_decay = jnp.exp(RET_CHUNK * log_gamma)

    def step(state, qkv):
        qc, kc, vc = qkv
        scores = jnp.einsum('bhid,bhjd->bhij', qc, kc) * inner_decay[None]
        inner = jnp.einsum('bhij,bhjv->bhiv', scores, vc)
        cross = jnp.einsum('bhid,bhdv->bhiv', qc, state) * cross_decay[None, :, :, None]
        new_state = state * chunk_decay[None, :, None, None] + jnp.einsum(
            'bhjd,bhjv->bhdv', kc * state_decay[None, :, :, None], vc)
        return new_state, inner + cross

    state0 = jnp.zeros((B, RET_HEADS, RET_DK, RET_DV), jnp.float32)
    _, o = lax.scan(step, state0, (to_chunks(q), to_chunks(k), to_chunks(v)))
    o = o.transpose(1, 0, 3, 2, 4).reshape(B, S, RET_HEADS, RET_DV)
    mu = jnp.mean(o, axis=-1, keepdims=True)
    var = jnp.mean(jnp.square(o - mu), axis=-1, keepdims=True)
    o = (o - mu) * lax.rsqrt(var + EPS)
    o = o * gn_g.astype(jnp.float32).reshape(RET_HEADS, RET_DV) + gn_b.astype(jnp.float32).reshape(RET_HEADS, RET_DV)
    o = o.reshape(B, S, 2 * D).astype(x.dtype)
    return (jax.nn.silu(g) * o) @ w_out


def swa_mixer(x, positions, w_in, b_in, sinks, w_out, b_out):
    B, S, D = x.shape
    NB = S // SWA_BLOCK
    hproj = x @ w_in + b_in
    q, k, v = jnp.split(hproj, [SWA_HEADS * SWA_HD, (SWA_HEADS + SWA_KV_HEADS) * SWA_HD], axis=-1)
    inv_freq = ROPE_THETA ** (-jnp.arange(0, SWA_HD, 2, dtype=jnp.float32) / SWA_HD)
    q = apply_rotary(q.reshape(B, S, SWA_HEADS, SWA_HD), positions, inv_freq)
    k = apply_rotary(k.reshape(B, S, SWA_KV_HEADS, SWA_HD), positions, inv_freq)
    v = v.reshape(B, S, SWA_KV_HEADS, SWA_HD)

    qb = q.reshape(B, NB, SWA_BLOCK, SWA_KV_HEADS, SWA_GROUP, SWA_HD)

    def band(t):
        prev = jnp.pad(t, ((0, 0), (SWA_BLOCK, 0), (0, 0), (0, 0)))[:, :S]
        shp = (B, NB, SWA_BLOCK, SWA_KV_HEADS, SWA_HD)
        return jnp.concatenate([prev.reshape(shp), t.reshape(shp)], axis=2)

    kb, vb = band(k), band(v)
    i = jnp.arange(SWA_BLOCK)[:, None]
    j = jnp.arange(2 * SWA_BLOCK)[None, :]
    nblk = jnp.arange(NB)[:, None, None]
    diff = i + SWA_BLOCK - j
    key_pos = nblk * SWA_BLOCK - SWA_BLOCK + j
    allowed = (diff >= 0) & (diff < SWA_WINDOW) & (key_pos >= 0)

    scores = jnp.einsum('bnikgd,bnjkd->bnkgij', qb.astype(jnp.float32), kb.astype(jnp.float32)) * (SWA_HD ** -0.5)
    scores = jnp.where(allowed[None, :, None, None], scores, -jnp.inf)
    sink = jnp.broadcast_to(sinks.astype(jnp.float32).reshape(1, 1, SWA_KV_HEADS, SWA_GROUP, 1, 1),
                            scores.shape[:-1] + (1,))
    probs = jax.nn.softmax(jnp.concatenate([scores, sink], axis=-1), axis=-1)[..., :-1]
    out = jnp.einsum('bnkgij,bnjkd->bnikgd', probs, vb.astype(jnp.float32)).astype(x.dtype)
    out = out.reshape(B, S, SWA_HEADS * SWA_HD)
    return out @ w_out + b_out


def setup_inputs(seed: int = 0) -> dict:
    key = jax.random.key(seed)
    ks = jax.random.split(key, 24)
    f32 = jnp.float32

    def nrm(k, shape, fan_in):
        return jax.random.normal(k, shape, f32) * (fan_in ** -0.5)

    def gain(k, shape):
        return 1.0 + 0.05 * jax.random.normal(k, shape, f32)

    x = jax.random.normal(ks[0], (BATCH, SEQ, D_MODEL), f32)
    offsets = jax.random.randint(ks[1], (BATCH, 1), 0, 1024, dtype=jnp.int32)
    positions = offsets + jnp.arange(SEQ, dtype=jnp.int32)[None, :]
    swa_in = (SWA_HEADS + 2 * SWA_KV_HEADS) * SWA_HD
    return {
        "x": x,
        "positions": positions,
        "ln_ffn1": gain(ks[2], (DEPTH, 2, D_MODEL)),
        "ln_mix": gain(ks[3], (DEPTH, 2, D_MODEL)),
        "ln_ffn2": gain(ks[4], (DEPTH, 2, D_MODEL)),
        "ffn_w_in": nrm(ks[5], (DEPTH, 2, D_MODEL, 2 * D_FF), D_MODEL),
        "ffn_w_out": nrm(ks[6], (DEPTH, 2, D_FF, D_MODEL), D_FF),
        "pool_w": nrm(ks[7], (N_POOL_LAYERS, N_POOL_GROUPS, POOL_GROUP, POOL_GROUP), POOL_GROUP),
        "pool_scale": 1.0 + 0.1 * jax.random.normal(ks[8], (N_POOL_LAYERS, D_MODEL), f32),
        "ret_w_in": nrm(ks[9], (N_RET_LAYERS, D_MODEL, 6 * D_MODEL), D_MODEL),
        "ret_gn_g": gain(ks[10], (N_RET_LAYERS, 2 * D_MODEL)),
        "ret_gn_b": 0.02 * jax.random.normal(ks[11], (N_RET_LAYERS, 2 * D_MODEL), f32),
        "ret_w_out": nrm(ks[12], (N_RET_LAYERS, 2 * D_MODEL, D_MODEL), 2 * D_MODEL),
        "swa_w_in": nrm(ks[13], (N_SWA_LAYERS, D_MODEL, swa_in), D_MODEL),
        "swa_b_in": 0.02 * jax.random.normal(ks[14], (N_SWA_LAYERS, swa_in), f32),
        "swa_sinks": 0.5 * jax.random.normal(ks[15], (N_SWA_LAYERS, SWA_HEADS), f32),
        "swa_w_out": nrm(ks[16], (N_SWA_LAYERS, SWA_HEADS * SWA_HD, D_MODEL), SWA_HEADS * SWA_HD),
        "swa_b_out": 0.02 * jax.random.normal(ks[17], (N_SWA_LAYERS, D_MODEL), f32),
    }


def reference(x, positions, ln_ffn1, ln_mix, ln_ffn2, ffn_w_in, ffn_w_out,
              pool_w, pool_scale, ret_w_in, ret_gn_g, ret_gn_b, ret_w_out,
              swa_w_in, swa_b_in, swa_sinks, swa_w_out, swa_b_out):
    h = x
    for i in range(DEPTH):
        f = swiglu(rms_norm(h, ln_ffn1[i, 0]), ffn_w_in[i, 0], ffn_w_out[i, 0])
        h = h + 0.5 * rms_norm(f, ln_ffn1[i, 1])
        u = rms_norm(h, ln_mix[i, 0])
        kind, j = i % N_MIXERS, i // N_MIXERS
        if kind == 0:
            m = pool_mixer(u, pool_w[j], pool_scale[j])
        elif kind == 1:
            m = retention_mixer(u, positions, ret_w_in[j], ret_gn_g[j], ret_gn_b[j], ret_w_out[j])
        else:
            m = swa_mixer(u, positions, swa_w_in[j], swa_b_in[j], swa_sinks[j], swa_w_out[j], swa_b_out[j])
        h = h + rms_norm(m, ln_mix[i, 1])
        f = swiglu(rms_norm(h, ln_ffn2[i, 0]), ffn_w_in[i, 1], ffn_w_out[i, 1])
        h = h + 0.5 * rms_norm(f, ln_ffn2[i, 1])
    return h
```

```python
import numpy as np
import concourse.bass as bass
import concourse.mybir as mybir
from concourse.bass_utils import run_bass_kernel_spmd

F32 = mybir.dt.float32
BF16 = mybir.dt.bfloat16
I32 = mybir.dt.int32
ALU = mybir.AluOpType
AF = mybir.ActivationFunctionType

D_MODEL = 2048
D_FF = 5504
SEQ = 4096
BATCH = 2
NCORES = 8
NT = 1024
EPS = 1e-6


class Ev:
    __slots__ = ("sem", "val", "eng")

    def __init__(self, sem, val, eng):
        self.sem, self.val, self.eng = sem, val, eng


class Prog:
    ENGS = ("pe", "act", "dve", "pool", "sp")

    def __init__(self, n_dma_sems=32):
        self.nc = bass.Bass("TRN2", target_bir_lowering=False)
        nc = self.nc
        self.ops = {e: [] for e in self.ENGS}
        self.esem = {e: nc.alloc_semaphore("es_" + e) for e in ("pe", "act", "dve", "pool")}
        self.ecnt = {e: 0 for e in self.esem}
        self.pending = {e: [] for e in self.ENGS}
        self.dsems = [nc.alloc_semaphore("ds%d" % i) for i in range(n_dma_sems)]
        self.dcnt = [0] * n_dma_sems
        self.dnext = 0
        self.waited = {}
        self.res = {}
        self.n_sb = 0

    def sbuf(self, name, shape, dtype):
        return self.nc.alloc_sbuf_tensor(name, list(shape), dtype)

    def psum(self, name, shape, dtype=F32):
        return self.nc.alloc_psum_tensor(name, list(shape), dtype)

    def dram_in(self, name, shape, dtype):
        return self.nc.dram_tensor(name, list(shape), dtype, kind="ExternalInput").ap()

    def dram_out(self, name, shape, dtype):
        return self.nc.dram_tensor(name, list(shape), dtype, kind="ExternalOutput").ap()

    def _flush(self, eng):
        pend = self.pending[eng]
        if not pend:
            return
        last = self.ops[eng][-1]
        if not last["inc"]:
            last["inc"] = True
            self.ecnt[eng] += 1
        ev = Ev(self.esem[eng], self.ecnt[eng], eng)
        for key, is_w in pend:
            self._record(key, is_w, ev)
        self.pending[eng] = []

    def _record(self, key, is_w, ev):
        st = self.res.get(key)
        if st is None:
            st = self.res[key] = [None, []]
        if is_w:
            st[0] = ev
            st[1] = []
        else:
            rl = st[1]
            for i, o in enumerate(rl):
                if o.sem is ev.sem:
                    if o.val < ev.val:
                        rl[i] = ev
                    break
            else:
                rl.append(ev)

    def _deps(self, eng, reads, writes):
        pe_pend = self.pending["pe"]
        if pe_pend:
            keys = set(k for k, _ in pe_pend)
            if any(k in keys for k in reads) or any(k in keys for k in writes):
                if eng != "pe":
                    self._flush("pe")
        evs = []
        for r in reads:
            st = self.res.get(r)
            if st is not None and st[0] is not None:
                evs.append(st[0])
        for w in writes:
            st = self.res.get(w)
            if st is not None:
                if st[0] is not None:
                    evs.append(st[0])
                evs.extend(st[1])
        best = {}
        for ev in evs:
            if ev.eng == eng and eng == "pe":
                continue
            k = (eng, id(ev.sem))
            if self.waited.get(k, 0) >= ev.val:
                continue
            cur = best.get(id(ev.sem))
            if cur is None or cur[1] < ev.val:
                best[id(ev.sem)] = (ev.sem, ev.val)
        waits = []
        for sem, val in best.values():
            self.waited[(eng, id(sem))] = val
            waits.append((sem, val))
        return waits

    def op(self, eng, fn, reads=(), writes=(), inc=True):
        waits = self._deps(eng, reads, writes)
        rec = {"fn": fn, "waits": waits, "inc": False, "dma": None}
        self.ops[eng].append(rec)
        pend = self.pending[eng]
        for r in reads:
            pend.append((r, False))
        for w in writes:
            pend.append((w, True))
        if inc:
            self._flush(eng)
        return rec

    def dma(self, q, out_ap, in_ap, reads=(), writes=(), fn=None):
        i = self.dnext
        self.dnext = (i + 1) % len(self.dsems)
        sem = self.dsems[i]
        prev = self.dcnt[i]
        waits = self._deps(q, reads, writes)
        if prev > 0 and self.waited.get((q, id(sem)), 0) < prev:
            self.waited[(q, id(sem))] = prev
            waits = [w for w in waits if w[0] is not sem] + [(sem, prev)]
        self.dcnt[i] = prev + 16
        ev = Ev(sem, prev + 16, "dma")
        if fn is None:
            fn = (lambda e, o=out_ap, s=in_ap: e.dma_start(out=o, in_=s))
        rec = {"fn": fn, "waits": waits, "inc": False, "dma": sem}
        self.ops[q].append(rec)
        for r in reads:
            self._record(r, False, ev)
        for w in writes:
            self._record(w, True, ev)
        return ev

    def barrier(self, skip_pool=False):
        for e in self.esem:
            if self.pending[e]:
                self._flush(e)
        targets = [(self.esem[e], self.ecnt[e]) for e in self.esem
                   if self.ecnt[e] > 0 and not (skip_pool and e == "pool")]
        targets += [(s, c) for s, c in zip(self.dsems, self.dcnt) if c > 0]
        for e in self.ENGS:
            waits = []
            for sem, val in targets:
                if e in self.esem and sem is self.esem[e]:
                    continue
                if self.waited.get((e, id(sem)), 0) >= val:
                    continue
                self.waited[(e, id(sem))] = val
                waits.append((sem, val))
            if waits:
                self.ops[e].append({"fn": None, "waits": waits, "inc": False, "dma": None})

    def finish(self):
        self.barrier()
        nc = self.nc
        prog = self

        def mk(name):
            def body(eng):
                for rec in prog.ops[name]:
                    for sem, val in rec["waits"]:
                        eng.wait_ge(sem, val)
                    if rec["fn"] is None:
                        continue
                    ins = rec["fn"](eng)
                    if rec["dma"] is not None:
                        ins.then_inc(rec["dma"], 16)
                    elif rec["inc"]:
                        ins.then_inc(prog.esem[name], 1)
            return body

        with nc.Block() as block:
            block.tensor(mk("pe"))
            block.scalar(mk("act"))
            block.vector(mk("dve"))
            block.gpsimd(mk("pool"))
            block.sync(mk("sp"))
        return nc


import math

TWO_PI = 2.0 * math.pi
MAGIC = 12582912.0
PI_LO = 3.1415920
ARENA_BYTES = 200 * 1024
KC = D_MODEL // 128
FC = D_FF // 128


class Banks:
    def __init__(self, ids):
        self.ids = list(ids)
        self.i = 0

    def next(self):
        b = self.ids[self.i]
        self.i = (self.i + 1) % len(self.ids)
        return b


class Arena:
    def __init__(self, P, nbytes):
        self.t = P.sbuf("arena", [128, nbytes // 4], F32)
        self.cap = nbytes
        self.off = 0

    def reset(self, off=0):
        self.off = off

    def alloc(self, shape, dtype):
        esz = 2 if dtype is BF16 else 4
        n = 1
        for d in shape[1:]:
            n *= d
        nbytes = (n * esz + 31) // 32 * 32
        o = self.off
        self.off += nbytes
        assert self.off <= self.cap, ("arena overflow", self.off, self.cap)
        v = self.t[:, o // 4:(o + nbytes) // 4]
        if dtype is not F32:
            v = v.bitcast(dtype)
        v = v[:, 0:n]
        if len(shape) == 3:
            v = v.rearrange("p (a b) -> p a b", a=shape[1])
        elif len(shape) == 4:
            v = v.rearrange("p (a b c) -> p a b c", a=shape[1], b=shape[2])
        return v


class Ctx:
    def __init__(self, P, cst_cols):
        self.P = P
        self.ps = [P.psum("ps%d" % i, [128, 512]) for i in range(8)]
        self.ones = P.sbuf("ones_bf", [128, 128], BF16)
        self.ident = P.sbuf("ident_bf", [128, 128], BF16)
        self.perm = P.sbuf("perm_bf", [128, 128], BF16)
        self.eps = P.sbuf("eps_c", [128, 1], F32)
        self.cst = P.sbuf("cst_sb", [128, cst_cols], F32)
        P.op("dve", lambda e: e.memset(self.ones[:], 1.0), writes=[("ones",)])
        P.op("dve", lambda e: e.memset(self.eps[:], EPS), writes=[("ones",)])
        self.A = Arena(P, ARENA_BYTES)
        self.ring = []
        self.ring_i = 0
        self.uid = 0

    def key(self, name):
        self.uid += 1
        return (name, self.uid)

    def make_ring(self, nslots, elems=4096):
        self.ring = [self.A.alloc([128, elems], BF16) for _ in range(nslots)]
        self.ring_keys = [self.key("w") for _ in range(nslots)]
        self.ring_i = 0

    def next_slot(self):
        i = self.ring_i
        self.ring_i = (i + 1) % len(self.ring)
        return self.ring_keys[i], self.ring[i]


def mm(P, out, lhsT, rhs, start, stop, reads, writes, inc=None):
    if inc is None:
        inc = stop
    P.op("pe", lambda e: e.matmul(out, lhsT, rhs, start=start, stop=stop),
         reads=reads, writes=writes, inc=inc)


def emit_rstd(P, C, sq, nchunks, n, rstd, bank, dim, key_sq, key_rstd, extra=None):
    ps = C.ps[bank]
    for c in range(nchunks):
        mm(P, ps[:, 0:n], C.ones[:], sq[:, c, :], c == 0, c == nchunks - 1,
           reads=[key_sq, ("ones",)], writes=[("ps", bank)])
    P.op("act", lambda e: e.activation(rstd, ps[:, 0:n], AF.Sqrt, bias=C.eps[:, 0:1], scale=1.0 / dim),
         reads=[("ps", bank), ("ones",)], writes=[key_rstd])
    P.op("dve", lambda e: e.reciprocal(rstd, rstd), reads=[key_rstd], writes=[key_rstd])


def emit_postnorm(P, C, f, hs, g, rstd, sq, n, kf, ksq, krstd):
    P.op("act", lambda e: e.activation(sq, f, AF.Square), reads=[kf], writes=[ksq])
    emit_rstd(P, C, sq, KC, n, rstd, 7, D_MODEL, ksq, krstd)
    for c in range(KC):
        P.op("dve", lambda e, c=c: e.tensor_tensor(f[:, c, :], f[:, c, :], rstd, ALU.mult),
             reads=[kf, krstd], writes=[kf])
        P.op("dve", lambda e, c=c: e.scalar_tensor_tensor(
            hs[:, c, :], f[:, c, :], g[:, c:c + 1], hs[:, c, :], ALU.mult, ALU.add),
            reads=[kf, ("cst",), ("h",)], writes=[("h",)])


def emit_prenorm(P, C, hs, g, u, sq, rstd, n, ksq, krstd, ku):
    P.op("act", lambda e: e.activation(sq, hs, AF.Square), reads=[("h",)], writes=[ksq])
    emit_rstd(P, C, sq, KC, n, rstd, 7, D_MODEL, ksq, krstd)
    for c in range(KC):
        P.op("dve", lambda e, c=c: e.scalar_tensor_tensor(
            u[:, c, :], hs[:, c, :], g[:, c:c + 1], rstd, ALU.mult, ALU.mult),
            reads=[("h",), krstd, ("cst",)], writes=[ku])


def linear_fm(P, C, w_units, kc, tiles, rhs, rkeys, consume, banks):
    mc = 0
    for wu in w_units:
        UC = wu.shape[2]
        wk, slot = C.next_slot()
        P.dma("pool", slot[:, 0:kc * UC], wu.rearrange("p k c -> p (k c)"), writes=[wk])
        sv = slot[:, 0:kc * UC].rearrange("p (k c) -> p k c", k=kc)
        for m in range(UC // 128):
            for ti, (t0, n) in enumerate(tiles):
                b = banks.next()
                for k in range(kc):
                    mm(P, C.ps[b][:, 0:n], sv[:, k, m * 128:(m + 1) * 128], rhs(k, ti),
                       k == 0, k == kc - 1, reads=[wk] + rkeys, writes=[("ps", b)])
                consume(mc, ti, C.ps[b][:, 0:n], b)
            mc += 1


def linear_tm(P, C, w_units, kc, ntiles, lhs, lkeys, consume, banks):
    for ui, wu in enumerate(w_units):
        UC = wu.shape[2]
        wk, slot = C.next_slot()
        P.dma("pool", slot[:, 0:kc * UC], wu.rearrange("p k c -> p (k c)"), writes=[wk])
        sv = slot[:, 0:kc * UC].rearrange("p (k c) -> p k c", k=kc)
        for ti in range(ntiles):
            b = banks.next()
            for k in range(kc):
                mm(P, C.ps[b][:, 0:UC], lhs(k, ti), sv[:, k, :], k == 0, k == kc - 1,
                   reads=[wk] + lkeys, writes=[("ps", b)])
            consume(ui, ti, C.ps[b][:, 0:UC], b)


def emit_sincos(P, C, ang, tmp, sin_out, cos_out, n, kang, ktmp, ksin, kcos):
    for shift, dst, kd in ((0.0, sin_out, ksin), (0.5 * math.pi, cos_out, kcos)):
        if dst is None:
            continue
        src, ks = ang, kang
        if shift != 0.0:
            P.op("dve", lambda e, dst=dst, shift=shift: e.tensor_scalar(dst, ang, shift, None, ALU.add),
                 reads=[kang], writes=[kd])
            src, ks = dst, kd
        P.op("dve", lambda e, src=src: e.tensor_scalar(tmp, src, 1.0 / TWO_PI, MAGIC, ALU.mult, ALU.add),
             reads=[ks], writes=[ktmp])
        P.op("dve", lambda e: e.tensor_scalar(tmp, tmp, -MAGIC, None, ALU.add),
             reads=[ktmp], writes=[ktmp])
        P.op("dve", lambda e, dst=dst, src=src: e.scalar_tensor_tensor(dst, tmp, -TWO_PI, src, ALU.mult, ALU.add),
             reads=[ktmp, ks], writes=[kd])
        P.op("dve", lambda e, dst=dst: e.tensor_scalar(dst, dst, PI_LO, -PI_LO, ALU.min, ALU.max),
             reads=[kd], writes=[kd])
        P.op("act", lambda e, dst=dst: e.activation(dst, dst, AF.Sin), reads=[kd], writes=[kd])


def phase_ffn(P, C, hT, g1, g2h, win_d, wout_d):
    A = C.A
    T = 512
    A.reset(NT * KC * 4)
    u = A.alloc([128, KC, T], BF16)
    hid = A.alloc([128, FC, T], BF16)
    f = A.alloc([128, KC, T], F32)
    sg = A.alloc([128, 2, T], F32)
    rstd = A.alloc([128, T], F32)
    C.make_ring(3)
    ku, khid, kf, krstd = C.key("u"), C.key("hid"), C.key("f"), C.key("rstd")
    ksg = [C.key("sg0"), C.key("sg1")]
    for t0 in range(0, NT, T):
        hs = hT[:, :, t0:t0 + T]
        emit_prenorm(P, C, hs, g1, u, hid[:, 0:KC, :], rstd, T, khid, krstd, ku)
        for j in range(FC):
            wk, slot = C.next_slot()
            P.dma("pool", slot[:, 0:KC * 256], win_d[j].rearrange("p k c -> p (k c)"), writes=[wk])
            sv = slot[:, 0:KC * 256].rearrange("p (k c) -> p k c", k=KC)
            pair = j % 2
            bg, bu = 2 * pair, 2 * pair + 1
            for k in range(KC):
                mm(P, C.ps[bg][:, 0:T], sv[:, k, 0:128], u[:, k, :], k == 0, k == KC - 1,
                   reads=[wk, ku], writes=[("ps", bg)])
            for k in range(KC):
                mm(P, C.ps[bu][:, 0:T], sv[:, k, 128:256], u[:, k, :], k == 0, k == KC - 1,
                   reads=[wk, ku], writes=[("ps", bu)])
            P.op("act", lambda e, bg=bg, pair=pair: e.activation(sg[:, pair, :], C.ps[bg][:, 0:T], AF.Silu),
                 reads=[("ps", bg)], writes=[ksg[pair]])
            P.op("dve", lambda e, bu=bu, pair=pair, j=j: e.tensor_tensor(
                hid[:, j, :], sg[:, pair, :], C.ps[bu][:, 0:T], ALU.mult),
                reads=[ksg[pair], ("ps", bu)], writes=[khid])
        unit = 16
        for dg in range(KC // 2):
            par = dg % 2
            banks = (4 + 2 * par, 5 + 2 * par)
            k0 = 0
            while k0 < FC:
                nk = min(unit, FC - k0)
                wk, slot = C.next_slot()
                P.dma("pool", slot[:, 0:nk * 256],
                      wout_d[dg, :, k0:k0 + nk, :].rearrange("p k c -> p (k c)"), writes=[wk])
                sv = slot[:, 0:nk * 256].rearrange("p (k c) -> p k c", k=nk)
                for m in range(2):
                    for kk in range(nk):
                        kc_ = k0 + kk
                        mm(P, C.ps[banks[m]][:, 0:T], sv[:, kk, m * 128:(m + 1) * 128], hid[:, kc_, :],
                           kc_ == 0, kc_ == FC - 1, reads=[wk, khid], writes=[("ps", banks[m])],
                           inc=(kk == nk - 1))
                k0 += nk
            for m in range(2):
                c = dg * 2 + m
                P.op("act", lambda e, c=c, b=banks[m]: e.activation(f[:, c, :], C.ps[b][:, 0:T], AF.Copy),
                     reads=[("ps", banks[m])], writes=[kf])
        emit_postnorm(P, C, f, hs, g2h, rstd, u, T, kf, ku, krstd)
    P.barrier()


def phase_pool(P, C, hT, g_pre, g_post, halo_fill, poolw_d, scale, corr):
    A = C.A
    T = 512
    NX = NT + 16
    A.reset(NT * KC * 4)
    rstdx = A.alloc([128, NX], F32)
    halo = A.alloc([128, KC, 16], F32)
    mixed = A.alloc([128, KC, NT], BF16)
    f = A.alloc([128, KC, T], F32)
    sq = A.alloc([128, KC, T], BF16)
    bufsets = [[A.alloc([128, NX], F32) for _ in range(3)] for _ in range(2)]
    smalls = [A.alloc([128, 16], F32) for _ in range(2)]
    wp = A.alloc([128, 4, 4, 512], BF16)
    rstd = A.alloc([128, T], F32)
    khalo, ksq, krx, kf, kwp, krstd = (C.key(n) for n in ("halo", "sq", "rstdx", "f", "wp", "rstd"))
    kmixs = [C.key("mixed%d" % c) for c in range(KC)]
    kbs = [[C.key("b%d" % i) for i in range(3)] for _ in range(2)]
    ksms = [C.key("small0"), C.key("small1")]
    stage16 = A.alloc([128, KC, 16], F32)
    halo_fill(halo, khalo, stage16, C.key("st16"))
    for g in range(4):
        P.dma("pool", wp[:, g, :, :], poolw_d[g], writes=[kwp])
    P.op("act", lambda e: e.activation(sq[:, :, 0:16], halo, AF.Square), reads=[khalo], writes=[ksq])
    emit_rstd(P, C, sq[:, :, 0:16], KC, 16, rstdx[:, 0:16], 7, D_MODEL, ksq, krx)
    for t0 in range(0, NT, T):
        P.op("act", lambda e, t0=t0: e.activation(sq, hT[:, :, t0:t0 + T], AF.Square),
             reads=[("h",)], writes=[ksq])
        emit_rstd(P, C, sq, KC, T, rstdx[:, 16 + t0:16 + t0 + T], 7, D_MODEL, ksq, krx)
    def chunk_ops(c, ei):
        g = c // 4
        w = 2 ** (g + 1)
        a, b1, b2 = bufsets[ei]
        ka, k1, k2 = kbs[ei]
        small, ksm = smalls[ei], ksms[ei]
        kmix = kmixs[c]
        ops = []
        ops.append(lambda: P.op("dve", lambda e: e.scalar_tensor_tensor(
            a[:, 0:16], halo[:, c, :], g_pre[:, c:c + 1], rstdx[:, 0:16], ALU.mult, ALU.mult),
            reads=[khalo, krx, ("cst",)], writes=[ka]))
        ops.append(lambda: P.op("dve", lambda e: e.scalar_tensor_tensor(
            a[:, 16:NX], hT[:, c, :], g_pre[:, c:c + 1], rstdx[:, 16:NX], ALU.mult, ALU.mult),
            reads=[("h",), krx, ("cst",)], writes=[ka]))
        cur, kcur = a, ka
        nxt = [(b1, k1), (b2, k2)]
        for s_ in range(g + 1):
            sh = 2 ** s_
            lo = 2 ** (s_ + 1) - 1
            dst, kd = nxt[s_ % 2]
            ops.append(lambda cur=cur, dst=dst, sh=sh, lo=lo, kcur=kcur, kd=kd: P.op(
                "dve", lambda e: e.tensor_tensor(dst[:, lo:NX], cur[:, lo:NX], cur[:, lo - sh:NX - sh], ALU.add),
                reads=[kcur], writes=[kd]))
            cur, kcur = dst, kd
        ops.append(lambda cur=cur, kcur=kcur: P.op("dve", lambda e: e.scalar_tensor_tensor(
            mixed[:, c, 16:NT], cur[:, 32:NX], 1.0 / w, a[:, 32:NX], ALU.mult, ALU.subtract),
            reads=[kcur, ka], writes=[kmix]))
        ops.append(lambda cur=cur, kcur=kcur: P.op("dve", lambda e: e.tensor_tensor(
            small, cur[:, 16:32], corr[:, g, :], ALU.mult), reads=[kcur, ("cst",)], writes=[ksm]))
        ops.append(lambda: P.op("dve", lambda e: e.tensor_tensor(
            mixed[:, c, 0:16], small, a[:, 16:32], ALU.subtract), reads=[ksm, ka], writes=[kmix]))
        return ops
    for c in range(0, KC, 2):
        oa, ob_ = chunk_ops(c, 0), chunk_ops(c + 1, 1)
        for i in range(max(len(oa), len(ob_))):
            if i < len(oa):
                oa[i]()
            if i < len(ob_):
                ob_[i]()
    banks = Banks([0, 1, 2, 3])
    for t0 in range(0, NT, T):
        for g in range(4):
            for m in range(4):
                b = banks.next()
                c = g * 4 + m
                for k in range(4):
                    mm(P, C.ps[b][:, 0:T], wp[:, g, k, m * 128:(m + 1) * 128], mixed[:, g * 4 + k, t0:t0 + T],
                       k == 0, k == 3, reads=[kwp, kmixs[g * 4 + k]], writes=[("ps", b)])
                P.op("dve", lambda e, c=c, b=b: e.tensor_scalar(
                    f[:, c, :], C.ps[b][:, 0:T], scale[:, c:c + 1], None, ALU.mult),
                    reads=[("ps", b), ("cst",)], writes=[kf])
        emit_postnorm(P, C, f, hT[:, :, t0:t0 + T], g_post, rstd, sq, T, kf, ksq, krstd)
    P.barrier()


def rot_linear(P, C, w_units, ws_units, kc, tiles, rhs, rkeys, bias, bias_s, cosT, sinT, tmp, ktmp,
               tB, ktB, out, kout, banks, ktab, col0=0):
    mc0 = 0
    cnt = [0]
    for wu, wsu in zip(w_units, ws_units):
        nm = wu.shape[2] // 128

        def cons_a(mc, ti, ps, b, mc0=mc0):
            t0, n = tiles[ti]
            P.op("dve", lambda e: e.scalar_tensor_tensor(
                tmp[:, mc, t0:t0 + n], ps, bias[:, mc0 + mc:mc0 + mc + 1],
                cosT[:, col0 + t0:col0 + t0 + n], ALU.add, ALU.mult),
                reads=[("ps", b), ("cst",), ktab], writes=[ktmp])

        def cons_b(mc, ti, ps, b, mc0=mc0):
            t0, n = tiles[ti]
            o = out(mc0 + mc)
            bi = cnt[0] % 2
            cnt[0] += 1
            P.op("dve", lambda e: e.scalar_tensor_tensor(
                tB[:, bi, 0:n], ps, bias_s[:, mc0 + mc:mc0 + mc + 1],
                sinT[:, col0 + t0:col0 + t0 + n], ALU.add, ALU.mult),
                reads=[("ps", b), ("cst",), ktab], writes=[ktB[bi]])
            P.op("dve", lambda e: e.tensor_tensor(
                o[:, t0:t0 + n], tmp[:, mc, t0:t0 + n], tB[:, bi, 0:n], ALU.add),
                reads=[ktmp, ktB[bi]], writes=[kout])

        linear_fm(P, C, [wu], kc, tiles, rhs, rkeys, cons_a, banks)
        linear_fm(P, C, [wsu], kc, tiles, rhs, rkeys, cons_b, banks)
        mc0 += nm


def rot_linear_perm(P, C, w_units, kc, tiles, rhs, rkeys, bias, cosT, sinT, t1, kt1, qb, kqb, tB, ktB,
                    out, kout, banks, ktab, perm, col0=0):
    cnt = [0]

    def cons(mc, ti, ps, b):
        t0, n = tiles[ti]
        bi = cnt[0] % 2
        cnt[0] += 1
        o = out(mc)
        P.op("dve", lambda e: e.tensor_scalar(qb[:, bi, 0:n], ps, bias[:, mc:mc + 1], None, ALU.add),
             reads=[("ps", b), ("cst",)], writes=[kqb[bi]])
        P.op("dve", lambda e: e.scalar_tensor_tensor(
            t1[:, bi, 0:n], ps, bias[:, mc:mc + 1], cosT[:, col0 + t0:col0 + t0 + n], ALU.add, ALU.mult),
            reads=[("ps", b), ("cst",), ktab], writes=[kt1[bi]])
        b2 = banks.next()
        mm(P, C.ps[b2][:, 0:n], perm, qb[:, bi, 0:n], True, True, reads=[kqb[bi], ("ident",)],
           writes=[("ps", b2)])
        P.op("dve", lambda e: e.tensor_tensor(tB[:, bi, 0:n], C.ps[b2][:, 0:n],
                                              sinT[:, col0 + t0:col0 + t0 + n], ALU.mult),
             reads=[("ps", b2), ktab], writes=[ktB[bi]])
        P.op("dve", lambda e: e.tensor_tensor(o[:, t0:t0 + n], t1[:, bi, 0:n], tB[:, bi, 0:n], ALU.add),
             reads=[kt1[bi], ktB[bi]], writes=[kout])
    linear_fm(P, C, w_units, kc, tiles, rhs, rkeys, cons, banks)


def phase_swa(P, C, hT, g_pre, g_post, D):
    A = C.A
    T = 512
    NX = NT + 128
    tiles_x = [(0, 512), (512, 512), (1024, 128)]
    tiles_o = [(0, 512), (512, 512)]
    A.reset(NT * KC * 4)
    u = A.alloc([128, KC, NX], BF16)
    QA = A.alloc([128, KC, NT], BF16)
    cosT = A.alloc([128, NX], F32)
    sinT = A.alloc([128, NX], F32)
    kv_flat = A.alloc([128, 4 * NX + 9 * 512], BF16)
    KT = kv_flat[:, 0:4 * NX].rearrange("p (a b) -> p a b", a=4)
    V = kv_flat[:, 4 * NX:4 * NX + 9 * 512].rearrange("p (a b) -> p a b", a=9)
    sq2 = kv_flat[:, 0:KC * T].rearrange("p (a b) -> p a b", a=KC)
    Pm = A.alloc([128, 2, 2, 512], BF16)
    rden = A.alloc([128, 2, 512], F32)
    masks = A.alloc([128, 3, 4, 128], BF16)
    esink = A.alloc([128, 32], F32)
    tab = A.alloc([128, 512 + 384 + 32], F32)
    rstd = A.alloc([128, T], F32)
    t1 = A.alloc([128, 2, 512], F32)
    qb = A.alloc([128, 2, 512], BF16)
    tB = A.alloc([128, 2, 512], F32)
    ktB = [C.key("tB"), C.key("tB")]
    kt1 = [C.key("t1"), C.key("t1")]
    kqb = [C.key("qb"), C.key("qb")]
    C.make_ring(2, 2048)
    ku, kqa, ktab, kkt, kv, krstd, ktq, kmask, ktb = (C.key(n) for n in
        ("u", "qa", "tab", "kt", "v", "rstd", "tq", "mask", "tb"))
    kpm = [[C.key("pm"), C.key("pm")], [C.key("pm"), C.key("pm")]]
    krd = [C.key("rd"), C.key("rd")]
    mark = A.off
    A.reset(NT * KC * 4 + NX * KC * 2)
    halo = A.alloc([128, KC, 128], F32)
    posi = A.alloc([128, NX], I32)
    ang = A.alloc([128, NX], F32)
    tmp = A.alloc([128, NX], F32)
    sq = A.alloc([128, KC, 128], BF16)
    A.reset(mark)
    khalo, kpos, kang, ktmp, ksq = (C.key(n) for n in ("halo", "posi", "ang", "tmp", "sq"))
    stage128 = kv_flat.bitcast(F32)[:, 0:KC * 128].rearrange("p (a b) -> p a b", a=KC)
    D["halo_fill"](halo, khalo, stage128, kkt)
    P.dma("sp", posi, D["posx"], writes=[kpos])
    P.dma("sp", tab, D["tab"], writes=[ktb])
    P.op("act", lambda e: e.activation(sq, halo, AF.Square), reads=[khalo], writes=[ksq])
    emit_rstd(P, C, sq, KC, 128, rstd[:, 0:128], 7, D_MODEL, ksq, krstd)
    for c in range(KC):
        P.op("dve", lambda e, c=c: e.scalar_tensor_tensor(
            u[:, c, 0:128], halo[:, c, :], g_pre[:, c:c + 1], rstd[:, 0:128], ALU.mult, ALU.mult),
            reads=[khalo, krstd, ("cst",)], writes=[ku])
    for t0 in range(0, NT, T):
        emit_prenorm(P, C, hT[:, :, t0:t0 + T], g_pre, u[:, :, 128 + t0:128 + t0 + T], sq2, rstd, T,
                     kkt, krstd, ku)
    P.op("dve", lambda e: e.tensor_copy(ang, posi), reads=[kpos], writes=[kang])
    P.op("dve", lambda e: e.tensor_scalar(ang, ang, D["invf"], None, ALU.mult),
         reads=[kang, ("cst",)], writes=[kang])
    emit_sincos(P, C, ang, tmp, sinT, cosT, NX, kang, ktmp, ktab, ktab)
    P.op("dve", lambda e: e.tensor_scalar(sinT, sinT, D["sgn"], None, ALU.mult),
         reads=[ktab, ("cst",)], writes=[ktab])
    for mi_ in range(3):
        P.op("dve", lambda e, mi_=mi_: e.tensor_scalar(
            masks[:, mi_, :, :], tab[:, 512 + 128 * mi_:640 + 128 * mi_].unsqueeze(1).to_broadcast([128, 4, 128]),
            -1.0, 30000.0, ALU.add, ALU.mult), reads=[ktb], writes=[kmask])
    P.op("act", lambda e: e.activation(esink, tab[:, 896:928], AF.Exp), reads=[ktb], writes=[kmask])
    banks = Banks([0, 1, 2, 3])
    base_ring, base_keys = list(C.ring), list(C.ring_keys)
    proj_slots = [Pm.rearrange("p a b c -> p (a b c)")[:, 0:2048],
                  rden.rearrange("p a b -> p (a b)").bitcast(BF16)[:, 0:2048]]
    C.ring = base_ring + proj_slots
    C.ring_keys = base_keys + [C.key("w") for _ in proj_slots]
    rot_linear_perm(P, C, D["wk"], KC, tiles_x,
                    lambda k, ti: u[:, k, tiles_x[ti][0]:tiles_x[ti][0] + tiles_x[ti][1]],
                    [ku], D["bk"], cosT, sinT, t1, kt1, qb, kqb, tB, ktB, lambda mc: KT[:, mc, :], kkt, banks, ktab,
                    D["perm"])
    def cons_v(ui, ti, ps, b):
        P.op("dve", lambda e: e.tensor_tensor(V[:, ti, ui * 128:(ui + 1) * 128], ps,
                                              tab[:, ui * 128:(ui + 1) * 128], ALU.add),
             reads=[("ps", b), ktb], writes=[kv])
    linear_tm(P, C, D["wv"], KC, 9, lambda k, ti: u[:, k, ti * 128:(ti + 1) * 128], [ku], cons_v, banks)
    P.barrier()
    rot_linear_perm(P, C, D["wq"], KC, tiles_o,
                    lambda k, ti: u[:, k, 128 + tiles_o[ti][0]:128 + tiles_o[ti][0] + 512],
                    [ku], D["bq"], cosT, sinT, t1, kt1, qb, kqb, tB, ktB, lambda mc: QA[:, mc, :], kqa, banks, ktab,
                    D["perm"], col0=128)
    C.ring, C.ring_keys, C.ring_i = base_ring, base_keys, 0
    P.barrier()
    def stage1(kh, n, par):
        r0 = par * 64
        bS = [4 * par, 4 * par + 1]
        for wi, kt in enumerate((n, n + 1)):
            mi = (2 if n == 0 else 1) if wi == 0 else 0
            mm(P, C.ps[bS[wi]][:, 0:512], KT[r0:r0 + 64, kh, kt * 128:(kt + 1) * 128],
               QA[r0:r0 + 64, 4 * kh:4 * kh + 4, n * 128:(n + 1) * 128], True, False,
               reads=[kkt, kqa, ("qa", kh, n, par)], writes=[("ps", bS[wi])], inc=False)
            mm(P, C.ps[bS[wi]][:, 0:512], C.ident[:], masks[:, mi, :, :], False, True,
               reads=[("ident",), kmask], writes=[("ps", bS[wi])])
            P.op("act", lambda e, wi=wi, par=par, b=bS[wi]: e.activation(
                Pm[:, par, wi, :], C.ps[b][:, 0:512], AF.Exp, scale=0.125),
                reads=[("ps", bS[wi])], writes=[kpm[par][wi]])

    def stage2(kh, n, par):
        r0 = par * 64
        bO, bD = 4 * par + 2, 4 * par + 3
        for wi, kt in enumerate((n, n + 1)):
            mm(P, C.ps[bO][:, 0:512], V[:, kt, kh * 128:(kh + 1) * 128], Pm[:, par, wi, :],
               wi == 0, wi == 1, reads=[kv, kpm[par][wi]], writes=[("ps", bO)])
        for wi in range(2):
            mm(P, C.ps[bD][:, 0:512], C.ones[:], Pm[:, par, wi, :],
               wi == 0, wi == 1, reads=[("ones",), kpm[par][wi]], writes=[("ps", bD)])
        c0 = kh * 8 + par
        P.op("dve", lambda e: e.tensor_tensor(
            rden[:, par, :].rearrange("p (a b) -> p a b", a=4),
            C.ps[bD][:, 0:512].rearrange("p (a b) -> p a b", a=4),
            esink[:, c0:c0 + 7:2].unsqueeze(2).to_broadcast([128, 4, 128]), ALU.add),
            reads=[("ps", bD), kmask], writes=[krd[par]])
        P.op("dve", lambda e: e.reciprocal(rden[:, par, :], rden[:, par, :]),
             reads=[krd[par]], writes=[krd[par]])
        P.op("dve", lambda e: e.tensor_tensor(
            QA[r0:r0 + 64, 4 * kh:4 * kh + 4, n * 128:(n + 1) * 128],
            C.ps[bO][r0:r0 + 64, 0:512].rearrange("p (a b) -> p a b", a=4),
            rden[r0:r0 + 64, par, :].rearrange("p (a b) -> p a b", a=4), ALU.mult),
            reads=[("ps", bO), krd[par]], writes=[("qa", kh, n, par)])

    its = [(kh, n, par) for kh in range(4) for n in range(8) for par in range(2)]
    stage1(*its[0])
    for i in range(len(its)):
        if i + 1 < len(its):
            stage1(*its[i + 1])
        stage2(*its[i])
    P.barrier()
    A2 = A.off
    A.reset(NT * KC * 4)
    f = A.alloc([128, KC, T], F32)
    A.reset(A2)
    kf = C.key("f")
    sqo = sq2
    extra_slots = [cosT.bitcast(BF16)[:, 0:2048], sinT.bitcast(BF16)[:, 0:2048],
                   t1.rearrange("p a b -> p (a b)").bitcast(BF16)[:, 0:2048],
                   tB.rearrange("p a b -> p (a b)").bitcast(BF16)[:, 0:2048]]
    C.ring = list(C.ring) + extra_slots
    C.ring_keys = list(C.ring_keys) + [C.key("w") for _ in extra_slots]
    for t0 in range(0, NT, T):
        def cons_o(mc, ti, ps, b):
            P.op("act", lambda e: e.activation(f[:, mc, :], ps, AF.Identity, bias=D["bo"][:, mc:mc + 1]),
                 reads=[("ps", b), ("cst",)], writes=[kf])
        linear_fm(P, C, D["wo"], KC, [(t0, T)], lambda k, ti, t0=t0: QA[:, k, t0:t0 + T], [kqa], cons_o, banks)
        emit_postnorm(P, C, f, hT[:, :, t0:t0 + T], g_post, rstd, sqo, T, kf, kkt, krstd)
    P.barrier()


RET_H = 8
GAMMAS = [1.0 - 2.0 ** (-5.0 - h) for h in range(RET_H)]


def ret_tables(P, C, D, cosT, sinT, ang, tmp, posi, ktab):
    kpos, kang, ktmp = C.key("posi"), C.key("ang"), C.key("tmp")
    P.dma("sp", posi, D["posr"], writes=[kpos])
    P.op("dve", lambda e: e.tensor_copy(ang, posi), reads=[kpos], writes=[kang])
    P.op("dve", lambda e: e.tensor_scalar(ang, ang, D["invf_ret"], None, ALU.mult),
         reads=[kang, ("cst",)], writes=[kang])
    emit_sincos(P, C, ang, tmp, sinT, cosT, NT, kang, ktmp, ktab, ktab)


def ret_rotary(P, C, scr, kscr, cosT, sinT, ktab, scale, out, kout):
    x1, x2, t1, t2 = scr[:, 0, :], scr[:, 1, :], scr[:, 2, :], scr[:, 3, :]
    for (a, b, o, op) in ((x1, x2, out[:, 0, :], ALU.subtract), (x2, x1, out[:, 1, :], ALU.add)):
        P.op("dve", lambda e, a=a: e.scalar_tensor_tensor(t1, a, scale, cosT, ALU.mult, ALU.mult),
             reads=[kscr, ktab], writes=[kscr])
        P.op("dve", lambda e, b=b: e.scalar_tensor_tensor(t2, b, scale, sinT, ALU.mult, ALU.mult),
             reads=[kscr, ktab], writes=[kscr])
        P.op("dve", lambda e, o=o, op=op: e.tensor_tensor(o, t1, t2, op), reads=[kscr], writes=[kout])


def ret_head_kv(P, C, D, h, u, ku, scr, kscr, cosT, sinT, ktab, KT, kkt, Ktm, kktm, V, kv, dcol, banks,
                load=None):
    tiles = [(0, 512), (512, 512)]
    W = D["ret_w"]
    if load is not None:
        load(h, KT, kkt, V, kv)
    else:
        def cons_raw(mc, ti, ps, b):
            t0, n = tiles[ti]
            P.op("act", lambda e: e.activation(scr[:, mc, t0:t0 + n], ps, AF.Copy),
                 reads=[("ps", b)], writes=[kscr])
        linear_fm(P, C, [W[h, 1]], KC, tiles, lambda k, ti: u[:, k, tiles[ti][0]:tiles[ti][0] + 512], [ku],
                  cons_raw, banks)
        ret_rotary(P, C, scr, kscr, cosT, sinT, ktab, 256.0 ** -0.5, KT, kkt)
    if load is None:
        def cons_v(ui, ti, ps, b):
            P.op("act", lambda e: e.activation(V[:, ti, ui * 256:(ui + 1) * 256], ps, AF.Copy),
                 reads=[("ps", b)], writes=[kv])
        linear_tm(P, C, [W[h, 2], W[h, 3]], KC, 8, lambda k, ti: u[:, k, ti * 128:(ti + 1) * 128], [ku],
                  cons_v, banks)
    for ti in range(8):
        b = banks.next()
        for a in range(2):
            mm(P, C.ps[b][:, a * 128:(a + 1) * 128], KT[:, a, ti * 128:(ti + 1) * 128], C.ident[:],
               True, True, reads=[kkt, ("ident",)], writes=[("ps", b)], inc=(a == 1))
        P.op("dve", lambda e, ti=ti, b=b: e.tensor_scalar(
            Ktm[:, ti, :], C.ps[b][:, 0:256], dcol(ti), None, ALU.mult),
            reads=[("ps", b), ("cst",)], writes=[kktm])


def phase_retA(P, C, hT, g_pre, D):
    A = C.A
    T = 512
    A.reset(NT * KC * 4)
    u = A.alloc([128, KC, NT], BF16)
    cosT = A.alloc([128, NT], F32)
    sinT = A.alloc([128, NT], F32)
    scr = A.alloc([128, 4, NT], F32)
    KTs = [A.alloc([128, 2, NT], BF16) for _ in range(2)]
    Ktm = A.alloc([128, 8, 256], BF16)
    Vs = [A.alloc([128, 8, 512], BF16) for _ in range(2)]
    Sst = A.alloc([128, 2, 512], F32)
    rstd = A.alloc([128, T], F32)
    sq = A.alloc([128, KC, T], BF16)
    posi = A.alloc([128, NT], I32)
    C.make_ring(2)
    ku, ktab, kscr, kktm, kS, krstd, ksq = (C.key(n) for n in
        ("u", "tab", "scr", "ktm", "S", "rstd", "sq"))
    kkts = [C.key("kt0"), C.key("kt1")]
    kvs = [C.key("v0"), C.key("v1")]
    ret_tables(P, C, D, cosT, sinT, scr[:, 0, :], scr[:, 1, :], posi, ktab)
    for t0 in range(0, NT, T):
        emit_prenorm(P, C, hT[:, :, t0:t0 + T], g_pre, u[:, :, t0:t0 + T], sq, rstd, T, ksq, krstd, ku)
    banks = Banks([0, 1, 2, 3])
    pending_coll = None
    for h in range(RET_H):
        KT, kkt, V, kv = KTs[h % 2], kkts[h % 2], Vs[h % 2], kvs[h % 2]
        ret_head_kv(P, C, D, h, u, ku, scr, kscr, cosT, sinT, ktab, KT, kkt, Ktm, kktm, V, kv,
                    lambda ti, h=h: D["dloc"][:, ti, h:h + 1], banks)
        for a in range(2):
            b = 4 + a
            for ti in range(8):
                mm(P, C.ps[b][:, 0:512], Ktm[:, ti, a * 128:(a + 1) * 128], V[:, ti, :], ti == 0, ti == 7,
                   reads=[kktm, kv], writes=[("ps", b)])
            P.op("act", lambda e, a=a, b=b: e.activation(Sst[:, a, :], C.ps[b][:, 0:512], AF.Copy),
                 reads=[("ps", b)], writes=[kS])
        D["sloc_store"](h, Sst, kS)
        if D.get("kv_store") is not None:
            D["kv_store"](h, KT, kkt, V, kv)
        if pending_coll is not None:
            pending_coll()
        pending_coll = (lambda h=h: D["sloc_coll"](h)) if D.get("sloc_coll") is not None else None
    if pending_coll is not None:
        pending_coll()
    D["retA_state"] = {"u": u, "ku": ku, "cosT": cosT, "sinT": sinT, "ktab": ktab}
    P.barrier(skip_pool=bool(D.get("kv_store")))


def phase_retB(P, C, hT, g_pre, g_post, D, load_h):
    A = C.A
    T = 512
    GOFF = ARENA_BYTES - 2 * KC * NT * 2
    A.reset(GOFF)
    gated_flat = A.alloc([128, 2 * KC * NT], BF16)
    gated = gated_flat.rearrange("p (a b) -> p a b", a=2 * KC)
    hin = gated_flat.bitcast(F32).rearrange("p (a b) -> p a b", a=KC)
    reuse = D.get("reuse")
    if reuse is not None:
        A.reset(NT * KC * 4)
    else:
        A.reset(0)
    u = A.alloc([128, KC, NT], BF16)
    cosT = A.alloc([128, NT], F32)
    sinT = A.alloc([128, NT], F32)
    scr_flat = A.alloc([128, 4 * NT], F32)
    scr = scr_flat.rearrange("p (a b) -> p a b", a=4)
    sq_tmp = scr_flat.bitcast(BF16)[:, 0:KC * T].rearrange("p (a b) -> p a b", a=KC)
    ob = scr_flat.bitcast(BF16)[:, 0:4 * T].rearrange("p (a b) -> p a b", a=4)
    osq = scr_flat.bitcast(BF16)[:, 4 * T:8 * T].rearrange("p (a b) -> p a b", a=4)
    posi_t = scr_flat.bitcast(I32)[:, 2 * NT:3 * NT]
    C.make_ring(2)
    if reuse is not None:
        assert A.off <= GOFF, (A.off, GOFF)
        A.reset(0)
    QT = A.alloc([128, 2, NT], BF16)
    KT = A.alloc([128, 2, NT], BF16)
    Ktm = A.alloc([128, 8, 256], BF16)
    V = A.alloc([128, 8, 512], BF16)
    G = A.alloc([128, 4, NT], BF16)
    oh = A.alloc([128, 4, T], F32)
    S = A.alloc([128, 2, 512], F32)
    Sbf2 = A.alloc([128, 2, 2, 512], BF16)
    dtab = A.alloc([128, 2, 128], F32)
    Sd = A.alloc([128, 2, 128], BF16)
    Qc = A.alloc([128, 2, 2, 128], BF16)
    gnt = A.alloc([128, 4, T], F32)
    kgs = kscr_gn = None
    if reuse is not None:
        ob = A.alloc([128, 4, T], BF16)
        osq = A.alloc([128, 4, T], BF16)
        rstd = None
        assert A.off <= NT * KC * 4, A.off
    else:
        rstd = A.alloc([128, T], F32)
        assert A.off <= GOFF, (A.off, GOFF)
    ku, ktab, kscr, kqt, kkt, kktm, kv, kg, koh, kS, kSb, kdt, kgn, krstd, kgated = (C.key(n) for n in
        ("u", "tab", "scr", "qt", "kt", "ktm", "v", "g", "oh", "S", "Sb", "dt", "gn", "rstd", "gated"))
    ksd = [C.key("sd"), C.key("sd")]
    kqc = [C.key("qc"), C.key("qc")]
    kSbs = [C.key("Sb0"), C.key("Sb1")]
    kob = C.key("ob") if reuse is not None else kscr
    if reuse is None:
        load_h(hin)
        ret_tables(P, C, D, cosT, sinT, scr[:, 0, :], scr[:, 1, :], posi_t, ktab)
        P.barrier()
        for t0 in range(0, NT, T):
            emit_prenorm(P, C, hin[:, :, t0:t0 + T], g_pre, u[:, :, t0:t0 + T], sq_tmp, rstd, T, kscr, krstd, ku)
        P.barrier()
    banks = Banks([0, 1, 2, 3])
    tiles = [(0, 512), (512, 512)]
    W = D["ret_w"]
    stage = scr[:, 2:4, 0:512]
    for h in range(RET_H):
        gam = GAMMAS[h]
        for c in range(4):
            P.dma("sp", stage, D["sall_ap"](c, h), reads=D["sall_keys"](h), writes=[kscr])
            if c == 0:
                P.op("dve", lambda e, c=c, h=h: e.tensor_scalar(
                    S, stage, D["coef"][:, c * 8 + h:c * 8 + h + 1], None, ALU.mult),
                    reads=[kscr, ("cst",)], writes=[kS])
            else:
                P.op("dve", lambda e, c=c, h=h: e.scalar_tensor_tensor(
                    S, stage, D["coef"][:, c * 8 + h:c * 8 + h + 1], S, ALU.mult, ALU.add),
                    reads=[kscr, ("cst",), kS], writes=[kS])
        P.op("act", lambda e: e.activation(Sbf2[:, 0, :, :], S, AF.Copy), reads=[kS], writes=[kSbs[0]])
        P.dma("sp", dtab, D["dtab"][h], writes=[kdt])
        def cons_raw(mc, ti, ps, b):
            t0, n = tiles[ti]
            P.op("act", lambda e: e.activation(scr[:, mc, t0:t0 + n], ps, AF.Copy),
                 reads=[("ps", b)], writes=[kscr])
        linear_fm(P, C, [W[h, 0]], KC, tiles, lambda k, ti: u[:, k, tiles[ti][0]:tiles[ti][0] + 512], [ku],
                  cons_raw, banks)
        ret_rotary(P, C, scr, kscr, cosT, sinT, ktab, 1.0, QT, kqt)
        ret_head_kv(P, C, D, h, u, ku, scr, kscr, cosT, sinT, ktab, KT, kkt, Ktm, kktm, V, kv,
                    lambda ti, h=h: D["sdcol"][:, h:h + 1], banks, load=D.get("kv_load"))

        def cons_g(mc, ti, ps, b):
            t0, n = tiles[ti]
            P.op("act", lambda e: e.activation(G[:, mc, t0:t0 + n], ps, AF.Silu),
                 reads=[("ps", b)], writes=[kg])
        linear_fm(P, C, [W[h, 4], W[h, 5]], KC, tiles, lambda k, ti: u[:, k, tiles[ti][0]:tiles[ti][0] + 512],
                  [ku], cons_g, banks)
        SB = (4, 3)

        def scores(n):
            cs = slice(n * 128, (n + 1) * 128)
            for a in range(2):
                mm(P, C.ps[SB[n % 2]][:, 0:128], KT[:, a, cs], QT[:, a, cs], a == 0, a == 1,
                   reads=[kkt, kqt], writes=[("ps", SB[n % 2])])
        scores(0)
        for n in range(8):
            cs = slice(n * 128, (n + 1) * 128)
            pp = n % 2
            Sbf = Sbf2[:, pp, :, :]
            P.op("dve", lambda e, pp=pp, n=n: e.tensor_tensor(Sd[:, pp, :], C.ps[SB[n % 2]][:, 0:128],
                                                              dtab[:, 0, :], ALU.mult),
                 reads=[("ps", SB[n % 2]), kdt], writes=[ksd[pp]])
            P.op("dve", lambda e, pp=pp, cs=cs: e.tensor_tensor(
                Qc[:, pp, :, :], QT[:, :, cs], dtab[:, 1, :].unsqueeze(1).to_broadcast([128, 2, 128]), ALU.mult),
                reads=[kqt, kdt], writes=[kqc[pp]])
            if n < 7:
                for a in range(2):
                    mm(P, C.ps[6 + a][:, 0:512], Ktm[:, n, a * 128:(a + 1) * 128], V[:, n, :], True, True,
                       reads=[kktm, kv], writes=[("ps", 6 + a)])
                scores(n + 1)
            for m in range(4):
                ms = slice(m * 128, (m + 1) * 128)
                mm(P, C.ps[5][:, ms], V[:, n, ms], Sd[:, pp, :], True, False,
                   reads=[kv, ksd[pp]], writes=[("ps", 5)], inc=False)
                for a in range(2):
                    mm(P, C.ps[5][:, ms], Sbf[:, a, ms], Qc[:, pp, a, :], False, a == 1,
                       reads=[kSbs[pp], kqc[pp]], writes=[("ps", 5)], inc=(a == 1 and m == 3))
            if n < 7:
                for a in range(2):
                    P.op("dve", lambda e, a=a, gam=gam: e.scalar_tensor_tensor(
                        S[:, a, :], S[:, a, :], gam ** 128, C.ps[6 + a][:, 0:512], ALU.mult, ALU.add),
                        reads=[kS, ("ps", 6 + a)], writes=[kS])
                P.op("act", lambda e, pp=pp: e.activation(Sbf2[:, 1 - pp, :, :], S, AF.Copy),
                     reads=[kS], writes=[kSbs[1 - pp]])
            half = n // 4
            P.op("act", lambda e, n=n: e.activation(
                oh[:, :, (n % 4) * 128:(n % 4 + 1) * 128],
                C.ps[5][:, 0:512].rearrange("p (a b) -> p a b", a=4), AF.Copy),
                reads=[("ps", 5)], writes=[koh])
            if n % 4 == 3:
                t0 = half * T
                P.op("act", lambda e: e.activation(ob, oh, AF.Copy), reads=[koh], writes=[kob])
                P.op("act", lambda e: e.activation(osq, oh, AF.Square), reads=[koh], writes=[kob])
                for m in range(4):
                    mm(P, C.ps[6][:, 0:T], C.ones[:], ob[:, m, :], m == 0, m == 3,
                       reads=[kob, ("ones",)], writes=[("ps", 6)])
                for m in range(4):
                    mm(P, C.ps[7][:, 0:T], C.ones[:], osq[:, m, :], m == 0, m == 3,
                       reads=[kob, ("ones",)], writes=[("ps", 7)])
                mean, var, tt_ = gnt[:, 0, :], gnt[:, 1, :], gnt[:, 2, :]
                P.op("dve", lambda e: e.tensor_scalar(mean, C.ps[6][:, 0:T], 1.0 / 512, None, ALU.mult),
                     reads=[("ps", 6)], writes=[kgn])
                P.op("dve", lambda e: e.tensor_tensor(tt_, mean, mean, ALU.mult), reads=[kgn], writes=[kgn])
                P.op("dve", lambda e: e.scalar_tensor_tensor(var, C.ps[7][:, 0:T], 1.0 / 512, tt_,
                                                             ALU.mult, ALU.subtract),
                     reads=[("ps", 7), kgn], writes=[kgn])
                P.op("act", lambda e: e.activation(var, var, AF.Sqrt, bias=C.eps[:, 0:1]),
                     reads=[kgn, ("ones",)], writes=[kgn])
                P.op("dve", lambda e: e.reciprocal(var, var), reads=[kgn], writes=[kgn])
                for m in range(4):
                    c = h * 4 + m
                    P.op("dve", lambda e, m=m: e.tensor_tensor(tt_, oh[:, m, :], mean, ALU.subtract),
                         reads=[koh, kgn], writes=[kgn])
                    P.op("dve", lambda e: e.tensor_tensor(tt_, tt_, var, ALU.mult), reads=[kgn], writes=[kgn])
                    P.op("dve", lambda e, c=c: e.tensor_scalar(
                        tt_, tt_, D["gn_g"][:, c:c + 1], D["gn_b"][:, c:c + 1], ALU.mult, ALU.add),
                        reads=[kgn, ("cst",)], writes=[kgn])
                    P.op("dve", lambda e, m=m, c=c, t0=t0: e.tensor_tensor(
                        gated[:, c, t0:t0 + T], tt_, G[:, m, t0:t0 + T], ALU.mult),
                        reads=[kgn, kg], writes=[kgated])
    P.barrier()
    A.reset(0)
    hT2 = A.alloc([128, KC, NT], F32)
    f = A.alloc([128, KC, T], F32)
    sqo = A.alloc([128, KC, T], BF16)
    rstd2 = A.alloc([128, T], F32)
    C.make_ring(2)
    assert A.off <= GOFF, (A.off, GOFF)
    load_h(hT2)
    kf, ksqo, kr2 = C.key("f"), C.key("sqo"), C.key("r2")
    for t0 in range(0, NT, T):
        def cons_o(mc, ti, ps, b):
            P.op("act", lambda e: e.activation(f[:, mc, :], ps, AF.Copy), reads=[("ps", b)], writes=[kf])
        linear_fm(P, C, D["ret_wo"], 2 * KC, [(t0, T)], lambda k, ti, t0=t0: gated[:, k, t0:t0 + T], [kgated],
                  cons_o, banks)
        emit_postnorm(P, C, f, hT2[:, :, t0:t0 + T], g_post, rstd2, sqo, T, kf, ksqo, kr2)
    P.barrier()
    return hT2


def phase_ffn_seq(P, C, hT, ffns, hook=None):
    A = C.A
    T = 512
    P.barrier()
    A.reset(NT * KC * 4)
    u = A.alloc([128, KC, T], BF16)
    hid = A.alloc([128, FC, T], BF16)
    f = A.alloc([128, KC, T], F32)
    sg = A.alloc([128, 2, T], F32)
    rstdA = A.alloc([128, T], F32)
    rstdB = A.alloc([128, T], F32)
    sqA = A.alloc([128, 2, 2, T], BF16)
    sqB = A.alloc([128, 2, 2, T], BF16)
    C.make_ring(3)
    ku, khid, krA, krB = C.key("u"), C.key("hid"), C.key("rA"), C.key("rB")
    kf = [C.key("f%d" % c) for c in range(KC)]
    ksg = [C.key("sg0"), C.key("sg1")]
    ksqA = [C.key("sqA0"), C.key("sqA1")]
    ksqB = [C.key("sqB0"), C.key("sqB1")]
    items = [(fi, ti) for fi in range(len(ffns)) for ti in (1, 0)]
    K_ = len(items)
    NB = 7

    def hkey(ti):
        return ("h", ti)

    def hs_of(k):
        return hT[:, :, items[k][1] * T:(items[k][1] + 1) * T]

    def stats_pre_ops(k):
        ops = []
        hs = hs_of(k)
        hk = hkey(items[k][1])
        for r in range(KC // 2):
            pp = r % 2

            def sq_op(r=r, pp=pp):
                P.op("act", lambda e: e.activation(sqA[:, pp, :, :], hs[:, 2 * r:2 * r + 2, :], AF.Square),
                     reads=[hk], writes=[ksqA[pp]])

            def mm_op(r=r, pp=pp):
                for i in range(2):
                    c = 2 * r + i
                    mm(P, C.ps[NB][:, 0:T], C.ones[:], sqA[:, pp, i, :], c == 0, c == KC - 1,
                       reads=[ksqA[pp], ("ones",)], writes=[("ps", NB)], inc=(i == 1))
            ops += [sq_op, mm_op]

        def fin():
            P.op("act", lambda e: e.activation(rstdA, C.ps[NB][:, 0:T], AF.Sqrt, bias=C.eps[:, 0:1],
                                               scale=1.0 / D_MODEL),
                 reads=[("ps", NB), ("ones",)], writes=[krA])
            P.op("dve", lambda e: e.reciprocal(rstdA, rstdA), reads=[krA], writes=[krA])
        ops.append(fin)
        return ops

    def write_u(k):
        hs = hs_of(k)
        g1 = ffns[items[k][0]][0]
        for c in range(KC):
            P.op("dve", lambda e, c=c: e.scalar_tensor_tensor(
                u[:, c, :], hs[:, c, :], g1[:, c:c + 1], rstdA, ALU.mult, ALU.mult),
                reads=[hkey(items[k][1]), krA, ("cst",)], writes=[ku])

    def post_apply_ops(k):
        hs = hs_of(k)
        hk = hkey(items[k][1])
        g2h = ffns[items[k][0]][1]
        ops = []

        def fin():
            P.op("act", lambda e: e.activation(rstdB, C.ps[NB][:, 0:T], AF.Sqrt, bias=C.eps[:, 0:1],
                                               scale=1.0 / D_MODEL),
                 reads=[("ps", NB), ("ones",)], writes=[krB])
            P.op("dve", lambda e: e.reciprocal(rstdB, rstdB), reads=[krB], writes=[krB])
        ops.append(fin)
        for c in range(KC):
            def ap(c=c):
                P.op("dve", lambda e: e.tensor_tensor(f[:, c, :], f[:, c, :], rstdB, ALU.mult),
                     reads=[kf[c], krB], writes=[kf[c]])
                P.op("dve", lambda e: e.scalar_tensor_tensor(
                    hs[:, c, :], f[:, c, :], g2h[:, c:c + 1], hs[:, c, :], ALU.mult, ALU.add),
                    reads=[kf[c], ("cst",), hk], writes=[hk])
            ops.append(ap)
        return ops

    for op_ in stats_pre_ops(0):
        op_()
    write_u(0)
    for k in range(K_):
        fi, ti = items[k]
        g1, g2h, win_d, wout_d = ffns[fi]
        extras = []
        if k >= 1:
            extras += post_apply_ops(k - 1)
            if hook is not None and k == K_ - 1:
                extras.append(lambda: hook(hkey(items[k - 1][1])))
        if k + 1 < K_:
            extras += stats_pre_ops(k + 1)
        ei = 0
        for j in range(FC):
            wk, slot = C.next_slot()
            P.dma("pool", slot[:, 0:KC * 256], win_d[j].rearrange("p k c -> p (k c)"), writes=[wk])
            sv = slot[:, 0:KC * 256].rearrange("p (k c) -> p k c", k=KC)
            pair = j % 2
            bg, bu = 2 * pair, 2 * pair + 1
            for kk in range(KC):
                mm(P, C.ps[bg][:, 0:T], sv[:, kk, 0:128], u[:, kk, :], kk == 0, kk == KC - 1,
                   reads=[wk, ku], writes=[("ps", bg)])
            for kk in range(KC):
                mm(P, C.ps[bu][:, 0:T], sv[:, kk, 128:256], u[:, kk, :], kk == 0, kk == KC - 1,
                   reads=[wk, ku], writes=[("ps", bu)])
            P.op("act", lambda e, bg=bg, pair=pair: e.activation(sg[:, pair, :], C.ps[bg][:, 0:T], AF.Silu),
                 reads=[("ps", bg)], writes=[ksg[pair]])
            P.op("dve", lambda e, bu=bu, pair=pair, j=j: e.tensor_tensor(
                hid[:, j, :], sg[:, pair, :], C.ps[bu][:, 0:T], ALU.mult),
                reads=[ksg[pair], ("ps", bu)], writes=[khid])
            for _ in range(2):
                if ei < len(extras) and j >= 1:
                    extras[ei]()
                    ei += 1
        while ei < len(extras):
            extras[ei]()
            ei += 1
        if k + 1 < K_:
            write_u(k + 1)
        unit = 16
        pend = None
        for dg in range(KC // 2):
            par = dg % 2
            banks = (4, 5) if par == 0 else (6, 3)
            k0 = 0
            first = True
            while k0 < FC:
                nk = min(unit, FC - k0)
                wk, slot = C.next_slot()
                P.dma("pool", slot[:, 0:nk * 256],
                      wout_d[dg, :, k0:k0 + nk, :].rearrange("p k c -> p (k c)"), writes=[wk])
                sv = slot[:, 0:nk * 256].rearrange("p (k c) -> p k c", k=nk)
                for m in range(2):
                    for kk in range(nk):
                        kc_ = k0 + kk
                        mm(P, C.ps[banks[m]][:, 0:T], sv[:, kk, m * 128:(m + 1) * 128], hid[:, kc_, :],
                           kc_ == 0, kc_ == FC - 1, reads=[wk, khid], writes=[("ps", banks[m])],
                           inc=(kk == nk - 1))
                k0 += nk
                if first and pend is not None:
                    pend()
                    pend = None
                first = False
            pp = dg % 2
            for m in range(2):
                c = dg * 2 + m
                P.op("act", lambda e, c=c, b=banks[m]: e.activation(f[:, c, :], C.ps[b][:, 0:T], AF.Copy),
                     reads=[("ps", banks[m])], writes=[kf[c]])
                P.op("act", lambda e, m=m, pp=pp, b=banks[m]: e.activation(sqB[:, pp, m, :], C.ps[b][:, 0:T],
                                                                           AF.Square),
                     reads=[("ps", banks[m])], writes=[ksqB[pp]])

            def ones_mm(dg=dg, pp=pp):
                for m in range(2):
                    c = dg * 2 + m
                    mm(P, C.ps[NB][:, 0:T], C.ones[:], sqB[:, pp, m, :], c == 0, c == KC - 1,
                       reads=[ksqB[pp], ("ones",)], writes=[("ps", NB)], inc=(m == 1))
            pend = ones_mm
        pend()
    for op_ in post_apply_ops(K_ - 1):
        op_()
    P.barrier()


def linear_fm_pieces(P, C, wu, kc, tiles, rhs, rkeys, consume, banks, mc0=0):
    UC = wu.shape[2]
    state = {}

    def load():
        wk, slot = C.next_slot()
        P.dma("pool", slot[:, 0:kc * UC], wu.rearrange("p k c -> p (k c)"), writes=[wk])
        state["wk"] = wk
        state["sv"] = slot[:, 0:kc * UC].rearrange("p (k c) -> p k c", k=kc)
    pieces = []
    first = True
    for m in range(UC // 128):
        for ti, (t0, n) in enumerate(tiles):
            def piece(m=m, ti=ti, n=n, first=first):
                if first:
                    load()
                b = banks.next()
                for k in range(kc):
                    mm(P, C.ps[b][:, 0:n], state["sv"][:, k, m * 128:(m + 1) * 128], rhs(k, ti),
                       k == 0, k == kc - 1, reads=[state["wk"]] + rkeys, writes=[("ps", b)])
                consume(mc0 + m, ti, C.ps[b][:, 0:n], b)
            pieces.append(piece)
            first = False
    return pieces


def phase_retB2(P, C, hT, g_post, D, load_h):
    A = C.A
    T = 512
    A.reset(NT * KC * 4)
    u = A.alloc([128, KC, NT], BF16)
    cosT = A.alloc([128, NT], F32)
    sinT = A.alloc([128, NT], F32)
    scr = A.alloc([128, 4, NT], F32)
    C.make_ring(2)
    GOFF = A.off

    def alloc_set():
        return {"QT": A.alloc([128, 2, NT], BF16), "KT": A.alloc([128, 2, NT], BF16),
                "Ktm": A.alloc([128, 8, 256], BF16), "V": A.alloc([128, 8, 512], BF16),
                "G": A.alloc([128, 4, NT], BF16),
                "kqt": C.key("qt"), "kkt": C.key("kt"), "kktm": C.key("ktm"), "kv": C.key("v"), "kg": C.key("g")}
    sets = [None, alloc_set()]
    gst = A.alloc([128, 2, 4, T], BF16)
    ob = A.alloc([128, 4, T], BF16)
    osq = A.alloc([128, 4, T], BF16)
    A.reset(0)
    sets[0] = alloc_set()
    oh = A.alloc([128, 4, T], F32)
    Ss = [A.alloc([128, 2, 512], F32) for _ in range(2)]
    Sbf2 = A.alloc([128, 2, 2, 512], BF16)
    dtabs = [A.alloc([128, 2, 128], F32) for _ in range(2)]
    Sd = A.alloc([128, 2, 128], BF16)
    Qc = A.alloc([128, 2, 2, 128], BF16)
    gnt = A.alloc([128, 3, T], F32)
    assert A.off <= NT * KC * 4, A.off
    ku, ktab, kscr, koh, kgn, kob = (C.key(n) for n in ("u", "tab", "scr", "oh", "gn", "ob"))
    kSs = [C.key("S0"), C.key("S1")]
    kdts = [C.key("dt0"), C.key("dt1")]
    ksd = [C.key("sd"), C.key("sd")]
    kqc = [C.key("qc"), C.key("qc")]
    kSbs = [C.key("Sb0"), C.key("Sb1")]
    kgst = [C.key("gst0"), C.key("gst1")]
    banks = Banks([0, 1, 2])
    tiles = [(0, 512), (512, 512)]
    W = D["ret_w"]
    stage = scr[:, 2:4, 0:512]
    gated_d = D["gated_d"]
    rhs_u = lambda k, ti: u[:, k, tiles[ti][0]:tiles[ti][0] + 512]

    def state_in(h):
        S, kS, dtab, kdt = Ss[h % 2], kSs[h % 2], dtabs[h % 2], kdts[h % 2]
        for c in range(4):
            P.dma("sp", stage, D["sall_ap"](c, h), reads=D["sall_keys"](h), writes=[kscr])
            if c == 0:
                P.op("dve", lambda e, c=c: e.tensor_scalar(
                    S, stage, D["coef"][:, c * 8 + h:c * 8 + h + 1], None, ALU.mult),
                    reads=[kscr, ("cst",)], writes=[kS])
            else:
                P.op("dve", lambda e, c=c: e.scalar_tensor_tensor(
                    S, stage, D["coef"][:, c * 8 + h:c * 8 + h + 1], S, ALU.mult, ALU.add),
                    reads=[kscr, ("cst",), kS], writes=[kS])
        P.dma("sp", dtab, D["dtab"][h], writes=[kdt])

    def proj_pieces(h):
        st = sets[h % 2]
        pieces = [lambda: state_in(h)]

        def cons_raw(mc, ti, ps, b):
            t0, n = tiles[ti]
            P.op("act", lambda e: e.activation(scr[:, mc, t0:t0 + n], ps, AF.Copy),
                 reads=[("ps", b)], writes=[kscr])
        pieces += linear_fm_pieces(P, C, W[h, 0], KC, tiles, rhs_u, [ku], cons_raw, banks)
        pieces.append(lambda: D["kv_load"](h, st["KT"], st["kkt"], st["V"], st["kv"]))

        def cons_g(mc, ti, ps, b):
            t0, n = tiles[ti]
            P.op("act", lambda e: e.activation(st["G"][:, mc, t0:t0 + n], ps, AF.Silu),
                 reads=[("ps", b)], writes=[st["kg"]])
        pieces += linear_fm_pieces(P, C, W[h, 4], KC, tiles, rhs_u, [ku], cons_g, banks, mc0=0)
        pieces += linear_fm_pieces(P, C, W[h, 5], KC, tiles, rhs_u, [ku], cons_g, banks, mc0=2)

        def ktm_piece(t_lo):
            for ti in range(t_lo, t_lo + 4):
                b = banks.next()
                for a in range(2):
                    mm(P, C.ps[b][:, a * 128:(a + 1) * 128], st["KT"][:, a, ti * 128:(ti + 1) * 128], C.ident[:],
                       True, True, reads=[st["kkt"], ("ident",)], writes=[("ps", b)], inc=(a == 1))
                P.op("dve", lambda e, ti=ti, b=b: e.tensor_scalar(
                    st["Ktm"][:, ti, :], C.ps[b][:, 0:256], D["sdcol"][:, h:h + 1], None, ALU.mult),
                    reads=[("ps", b), ("cst",)], writes=[st["kktm"]])
        pieces.append(lambda: ktm_piece(0))
        pieces.append(lambda: ktm_piece(4))
        pieces.append(lambda: ret_rotary(P, C, scr, kscr, cosT, sinT, ktab, 1.0, st["QT"], st["kqt"]))
        return pieces

    for pc in proj_pieces(0):
        pc()
    SB = (4, 3)
    for h in range(RET_H):
        st = sets[h % 2]
        QT, KT, Ktm, V, G = st["QT"], st["KT"], st["Ktm"], st["V"], st["G"]
        kqt, kkt, kktm, kv, kg = st["kqt"], st["kkt"], st["kktm"], st["kv"], st["kg"]
        gam = GAMMAS[h]
        nxt = proj_pieces(h + 1) if h + 1 < RET_H else []
        pi = [0]

        def emit_piece(cnt=1):
            for _ in range(cnt):
                if pi[0] < len(nxt):
                    nxt[pi[0]]()
                    pi[0] += 1
        S, kS, dtab, kdt = Ss[h % 2], kSs[h % 2], dtabs[h % 2], kdts[h % 2]
        P.op("act", lambda e, S=S: e.activation(Sbf2[:, 0, :, :], S, AF.Copy), reads=[kS], writes=[kSbs[0]])

        def scores(n):
            cs = slice(n * 128, (n + 1) * 128)
            for a in range(2):
                mm(P, C.ps[SB[n % 2]][:, 0:128], KT[:, a, cs], QT[:, a, cs], a == 0, a == 1,
                   reads=[kkt, kqt], writes=[("ps", SB[n % 2])])
        scores(0)
        for n in range(8):
            cs = slice(n * 128, (n + 1) * 128)
            pp = n % 2
            Sbf = Sbf2[:, pp, :, :]
            P.op("dve", lambda e, pp=pp, n=n, dtab=dtab: e.tensor_tensor(Sd[:, pp, :], C.ps[SB[n % 2]][:, 0:128],
                                                              dtab[:, 0, :], ALU.mult),
                 reads=[("ps", SB[n % 2]), kdt], writes=[ksd[pp]])
            P.op("dve", lambda e, pp=pp, cs=cs, QT=QT, dtab=dtab: e.tensor_tensor(
                Qc[:, pp, :, :], QT[:, :, cs], dtab[:, 1, :].unsqueeze(1).to_broadcast([128, 2, 128]), ALU.mult),
                reads=[kqt, kdt], writes=[kqc[pp]])
            if n < 7:
                for a in range(2):
                    mm(P, C.ps[6 + a][:, 0:512], Ktm[:, n, a * 128:(a + 1) * 128], V[:, n, :], True, True,
                       reads=[kktm, kv], writes=[("ps", 6 + a)])
                scores(n + 1)
            emit_piece(2 if n == 0 else 1)
            for m in range(4):
                ms = slice(m * 128, (m + 1) * 128)
                mm(P, C.ps[5][:, ms], V[:, n, ms], Sd[:, pp, :], True, False,
                   reads=[kv, ksd[pp]], writes=[("ps", 5)], inc=False)
                for a in range(2):
                    mm(P, C.ps[5][:, ms], Sbf[:, a, ms], Qc[:, pp, a, :], False, a == 1,
                       reads=[kSbs[pp], kqc[pp]], writes=[("ps", 5)], inc=(a == 1 and m == 3))
            if n < 7:
                for a in range(2):
                    P.op("dve", lambda e, a=a, gam=gam, S=S: e.scalar_tensor_tensor(
                        S[:, a, :], S[:, a, :], gam ** 128, C.ps[6 + a][:, 0:512], ALU.mult, ALU.add),
                        reads=[kS, ("ps", 6 + a)], writes=[kS])
                P.op("act", lambda e, pp=pp, S=S: e.activation(Sbf2[:, 1 - pp, :, :], S, AF.Copy),
                     reads=[kS], writes=[kSbs[1 - pp]])
            P.op("act", lambda e, n=n: e.activation(
                oh[:, :, (n % 4) * 128:(n % 4 + 1) * 128],
                C.ps[5][:, 0:512].rearrange("p (a b) -> p a b", a=4), AF.Copy),
                reads=[("ps", 5)], writes=[koh])
            if n % 4 == 3:
                half = n // 4
                t0 = half * T
                gi = (2 * h + half) % 2
                P.op("act", lambda e: e.activation(ob, oh, AF.Copy), reads=[koh], writes=[kob])
                P.op("act", lambda e: e.activation(osq, oh, AF.Square), reads=[koh], writes=[kob])
                for m in range(4):
                    mm(P, C.ps[6][:, 0:T], C.ones[:], ob[:, m, :], m == 0, m == 3,
                       reads=[kob, ("ones",)], writes=[("ps", 6)])
                for m in range(4):
                    mm(P, C.ps[7][:, 0:T], C.ones[:], osq[:, m, :], m == 0, m == 3,
                       reads=[kob, ("ones",)], writes=[("ps", 7)])
                mean, var, tt_ = gnt[:, 0, :], gnt[:, 1, :], gnt[:, 2, :]
                P.op("dve", lambda e: e.tensor_scalar(mean, C.ps[6][:, 0:T], 1.0 / 512, None, ALU.mult),
                     reads=[("ps", 6)], writes=[kgn])
                P.op("dve", lambda e: e.tensor_tensor(tt_, mean, mean, ALU.mult), reads=[kgn], writes=[kgn])
                P.op("dve", lambda e: e.scalar_tensor_tensor(var, C.ps[7][:, 0:T], 1.0 / 512, tt_,
                                                             ALU.mult, ALU.subtract),
                     reads=[("ps", 7), kgn], writes=[kgn])
                P.op("act", lambda e: e.activation(var, var, AF.Sqrt, bias=C.eps[:, 0:1]),
                     reads=[kgn, ("ones",)], writes=[kgn])
                P.op("dve", lambda e: e.reciprocal(var, var), reads=[kgn], writes=[kgn])
                for m in range(4):
                    c = h * 4 + m
                    P.op("dve", lambda e, m=m: e.tensor_tensor(tt_, oh[:, m, :], mean, ALU.subtract),
                         reads=[koh, kgn], writes=[kgn])
                    P.op("dve", lambda e: e.tensor_tensor(tt_, tt_, var, ALU.mult), reads=[kgn], writes=[kgn])
                    P.op("dve", lambda e, c=c: e.tensor_scalar(
                        tt_, tt_, D["gn_g"][:, c:c + 1], D["gn_b"][:, c:c + 1], ALU.mult, ALU.add),
                        reads=[kgn, ("cst",)], writes=[kgn])
                    P.op("dve", lambda e, m=m, t0=t0, gi=gi, G=G: e.tensor_tensor(
                        gst[:, gi, m, :], tt_, G[:, m, t0:t0 + T], ALU.mult),
                        reads=[kgn, kg], writes=[kgst[gi]])
                    emit_piece()
                P.dma("sp", gated_d[half][:, 4 * h:4 * h + 4, :], gst[:, gi, :, :],
                      reads=[kgst[gi]], writes=[("gd", half)])
        emit_piece(len(nxt))
    P.barrier()
    A.reset(0)
    hT2 = A.alloc([128, KC, NT], F32)
    f = A.alloc([128, KC, T], F32)
    sqo = A.alloc([128, KC, T], BF16)
    rstd2 = A.alloc([128, T], F32)
    C.make_ring(6)
    gt1 = A.alloc([128, 2 * KC, T], BF16)
    gts = [gt1, gt1]
    kf, ksqo, kr2 = C.key("f"), C.key("sqo"), C.key("r2")
    kg1 = C.key("gt")
    kgt = [kg1, kg1]
    banks = Banks([0, 1, 2, 3])
    P.dma("sp", gts[0].rearrange("p a b -> p (a b)"), gated_d[0].rearrange("p a b -> p (a b)"),
          reads=[("gd", 0)], writes=[kgt[0]])
    load_h(hT2)
    for half in range(2):
        t0 = half * T
        if half == 1:
            P.dma("sp", gts[1].rearrange("p a b -> p (a b)"), gated_d[1].rearrange("p a b -> p (a b)"),
                  reads=[("gd", 1)], writes=[kgt[1]])

        def cons_o(mc, ti, ps, b):
            P.op("act", lambda e: e.activation(f[:, mc, :], ps, AF.Copy), reads=[("ps", b)], writes=[kf])
        linear_fm(P, C, D["ret_wo"], 2 * KC, [(0, T)], lambda k, ti, half=half: gts[half][:, k, :], [kgt[half]],
                  cons_o, banks)
        emit_postnorm(P, C, f, hT2[:, :, t0:t0 + T], g_post, rstd2, sqo, T, kf, ksqo, kr2)
    P.barrier()
    return hT2


def phase_pool2(P, C, hT, g_pre, g_post, halo_fill, poolw_d, scale, band_d):
    A = C.A
    T = 512
    NX = NT + 16
    A.reset(NT * KC * 4)
    rstdx = A.alloc([128, NX], F32)
    halo = A.alloc([128, KC, 16], F32)
    stage16 = A.alloc([128, KC, 16], F32)
    ub_flat = A.alloc([128, KC * NX], BF16)
    ub = ub_flat.rearrange("p (a b) -> p a b", a=KC)
    mixed = ub_flat[:, 0:KC * NT].rearrange("p (a b) -> p a b", a=KC)
    utm_flat = A.alloc([128, KC * NT], BF16)
    utm = utm_flat.rearrange("p (c j f) -> p c j f", c=KC, j=8)
    sq = utm_flat[:, 0:KC * T].rearrange("p (a b) -> p a b", a=KC)
    f = utm_flat.bitcast(F32).rearrange("p (a b) -> p a b", a=KC)
    uhtm = A.alloc([128, KC, 128], BF16)
    wp = A.alloc([128, 4, 4, 512], BF16)
    bt = A.alloc([128, 4, 4, 128], BF16)
    rstd = A.alloc([128, T], F32)
    sqp = A.alloc([128, KC, T], BF16)
    khalo, krx, kub, kutm, kuh, kwp, kbt, krstd, ksqp = (C.key(n) for n in
        ("halo", "rstdx", "ub", "utm", "uh", "wp", "bt", "rstd", "sqp"))
    halo_fill(halo, khalo, stage16, C.key("st16"))
    for g in range(4):
        P.dma("pool", wp[:, g, :, :], poolw_d[g], writes=[kwp])
    P.dma("pool", bt.rearrange("p a b c -> p (a b c)"), band_d.rearrange("p a b c -> p (a b c)"), writes=[kbt])
    P.op("act", lambda e: e.activation(sq[:, :, 0:16], halo, AF.Square), reads=[khalo], writes=[kutm])
    emit_rstd(P, C, sq[:, :, 0:16], KC, 16, rstdx[:, 0:16], 7, D_MODEL, kutm, krx)
    for t0 in range(0, NT, T):
        P.op("act", lambda e, t0=t0: e.activation(sq, hT[:, :, t0:t0 + T], AF.Square),
             reads=[("h",)], writes=[kutm])
        emit_rstd(P, C, sq, KC, T, rstdx[:, 16 + t0:16 + t0 + T], 7, D_MODEL, kutm, krx)
    for c in range(KC):
        P.op("dve", lambda e, c=c: e.scalar_tensor_tensor(
            ub[:, c, 0:16], halo[:, c, :], g_pre[:, c:c + 1], rstdx[:, 0:16], ALU.mult, ALU.mult),
            reads=[khalo, krx, ("cst",)], writes=[kub])
        P.op("dve", lambda e, c=c: e.scalar_tensor_tensor(
            ub[:, c, 16:NX], hT[:, c, :], g_pre[:, c:c + 1], rstdx[:, 16:NX], ALU.mult, ALU.mult),
            reads=[("h",), krx, ("cst",)], writes=[kub])
    banks = Banks([0, 1, 2, 3, 4, 5])
    ev = [0]

    def evac(dst, src, reads, writes):
        eng = "act" if ev[0] % 2 == 0 else "dve"
        ev[0] += 1
        if eng == "act":
            P.op("act", lambda e: e.activation(dst, src, AF.Copy), reads=reads, writes=writes)
        else:
            P.op("dve", lambda e: e.tensor_copy(dst, src), reads=reads, writes=writes)
    for cg in range(KC // 4):
        b = banks.next()
        for q in range(4):
            c = cg * 4 + q
            mm(P, C.ps[b][0:16, q * 128:(q + 1) * 128], ub[:, c, 0:16], C.ident[:], True, True,
               reads=[kub, ("ident",)], writes=[("ps", b)], inc=(q == 3))
        evac(uhtm[0:16, cg * 4:cg * 4 + 4, :], C.ps[b][0:16, 0:512].rearrange("p (a b) -> p a b", a=4),
             [("ps", b)], [kuh])
    for c in range(KC):
        for jg in range(2):
            b = banks.next()
            for q in range(4):
                j = jg * 4 + q
                mm(P, C.ps[b][:, q * 128:(q + 1) * 128], ub[:, c, 16 + j * 128:16 + (j + 1) * 128], C.ident[:],
                   True, True, reads=[kub, ("ident",)], writes=[("ps", b)], inc=(q == 3))
            evac(utm[:, c, jg * 4:jg * 4 + 4, :], C.ps[b][:, 0:512].rearrange("p (a b) -> p a b", a=4),
                 [("ps", b)], [kutm])
    for c in range(KC):
        g = c // 4
        for jg in range(2):
            b = banks.next()
            for q in range(4):
                j = jg * 4 + q
                o = C.ps[b][:, q * 128:(q + 1) * 128]
                cur = bt[:, g, 2, :] if j == 0 else bt[:, g, 0, :]
                mm(P, o, utm[:, c, j, :], cur, True, False, reads=[kutm, kbt], writes=[("ps", b)], inc=False)
                if j == 0:
                    mm(P, o, uhtm[0:16, c, :], bt[0:16, g, 3, :], False, True,
                       reads=[kuh, kbt], writes=[("ps", b)], inc=(q == 3))
                else:
                    mm(P, o, utm[:, c, j - 1, :], bt[:, g, 1, :], False, True,
                       reads=[kutm, kbt], writes=[("ps", b)], inc=(q == 3))
            evac(mixed[:, c, jg * 512:(jg + 1) * 512], C.ps[b][:, 0:512], [("ps", b)], [kub])
    banks = Banks([0, 1, 2, 3])
    for t0 in range(0, NT, T):
        for g in range(4):
            for m in range(4):
                b = banks.next()
                c = g * 4 + m
                for k in range(4):
                    mm(P, C.ps[b][:, 0:T], wp[:, g, k, m * 128:(m + 1) * 128], mixed[:, g * 4 + k, t0:t0 + T],
                       k == 0, k == 3, reads=[kwp, kub], writes=[("ps", b)])
                P.op("dve", lambda e, c=c, b=b: e.tensor_scalar(
                    f[:, c, :], C.ps[b][:, 0:T], scale[:, c:c + 1], None, ALU.mult),
                    reads=[("ps", b), ("cst",)], writes=[kutm])
        emit_postnorm(P, C, f, hT[:, :, t0:t0 + T], g_post, rstd, sqp, T, kutm, ksqp, krstd)
    P.barrier()


CST_ITEMS = [
    ("ln", (4, 6, KC)), ("pool_scale", (2, KC)), ("pool_corr", (4, 16)),
    ("bq", (KC,)), ("bqs", (KC,)), ("bk", (4,)), ("bks", (4,)), ("bo", (KC,)),
    ("invf_swa", (1,)), ("sgn", (1,)),
    ("gn_g", (2 * KC,)), ("gn_b", (2 * KC,)), ("invf_ret", (1,)), ("sdcol", (8,)),
    ("dloc", (8, 8)), ("coef", (64,)), ("sel", (4,)),
]
CST_OFF = {}
_o = 0
for _n, _s in CST_ITEMS:
    _sz = int(np.prod(_s))
    CST_OFF[_n] = (_o, _s)
    _o += _sz
NCST = _o


def cst_view(C, name):
    o, shp = CST_OFF[name]
    n = int(np.prod(shp))
    v = C.cst[:, o:o + n]
    if len(shp) == 2:
        v = v.rearrange("p (a b) -> p a b", a=shp[0])
    elif len(shp) == 3:
        v = v.rearrange("p (a b c) -> p a b c", a=shp[0], b=shp[1])
    return v


def prologue(P, C):
    cst_d = P.dram_in("cst", [128, NCST], F32)
    P.dma("sp", C.cst[:], cst_d, writes=[("cst",)])
    ident_d = P.dram_in("ident", [128, 128], F32)
    P.dma("pool", C.ident[:], ident_d, writes=[("ident",)])
    perm_d = P.dram_in("perm", [128, 128], F32)
    P.dma("pool", C.perm[:], perm_d, writes=[("ident",)])
    ln = cst_view(C, "ln")
    for i in range(4):
        for s in (1, 5):
            P.op("dve", lambda e, i=i, s=s: e.tensor_scalar(ln[:, i, s, :], ln[:, i, s, :], 0.5, None, ALU.mult),
                 reads=[("cst",)], writes=[("cst",)])
    return ln


GROUPS = [[0, 1, 2, 3], [4, 5, 6, 7]]


def build_fused():
    P = Prog()
    C = Ctx(P, NCST)
    nc = P.nc
    ln = prologue(P, C)
    hT_d = P.dram_in("hT", [KC * 128, NT], F32)
    out_d = P.dram_out("outT", [KC * 128, NT], F32)
    hT = C.A.t[:, 0:KC * NT].rearrange("p (a b) -> p a b", a=KC)
    sel = cst_view(C, "sel")
    P.dma("sp", hT, hT_d.rearrange("(c p) t -> p c t", p=128), writes=[("h",)])

    def ffn_seq(*specs, hook=None):
        ffns = []
        for (i, s) in specs:
            win_d = P.dram_in("win_%d_%d" % (i, s), [FC, 128, KC, 256], F32)
            wout_d = P.dram_in("wout_%d_%d" % (i, s), [KC // 2, 128, FC, 256], F32)
            base = 0 if s == 0 else 4
            ffns.append((ln[:, i, base, :], ln[:, i, base + 1, :], win_d, wout_d))
        phase_ffn_seq(P, C, hT, ffns, hook=hook)

    xbuf = {}

    def ex_send(w, tag):
        src = nc.dram_tensor("xs_" + tag, [KC * 128, w], F32).ap()
        dst = nc.dram_tensor("xd_" + tag, [4 * KC * 128, w], F32).ap()
        xbuf[tag] = dst

        def hook(hk):
            P.dma("sp", src.rearrange("(c p) t -> p c t", p=128), hT[:, :, NT - w:NT],
                  reads=[hk], writes=[("xs", tag)])
            P.op("pool", lambda e: e.collective_compute("AllGather", ALU.bypass, replica_groups=GROUPS,
                                                        ins=[src.opt()], outs=[dst.opt()]),
                 reads=[("xs", tag)], writes=[("xd", tag)])
        return hook

    def exchange(w, tag):
        dst = xbuf[tag]

        def fill(halo, kh, stage, kst):
            for r in range(4):
                P.dma("sp", stage, dst[r * KC * 128:(r + 1) * KC * 128, :].rearrange("(c p) t -> p c t", p=128),
                      reads=[("xd", tag)], writes=[kst])
                if r == 0:
                    P.op("dve", lambda e: e.tensor_scalar(halo, stage, sel[:, 0:1], None, ALU.mult),
                         reads=[kst, ("cst",)], writes=[kh])
                else:
                    P.op("dve", lambda e, r=r: e.scalar_tensor_tensor(halo, stage, sel[:, r:r + 1], halo,
                                                                      ALU.mult, ALU.add),
                         reads=[kst, ("cst",), kh], writes=[kh])
        return fill

    def pool(i, j, tag):
        fill = exchange(16, tag)
        pw_d = P.dram_in("poolw%d" % j, [4, 128, 4, 512], F32)
        if "band" not in xbuf:
            xbuf["band"] = P.dram_in("pool_band", [128, 4, 4, 128], F32)
        phase_pool2(P, C, hT, ln[:, i, 2, :], ln[:, i, 3, :], fill, pw_d,
                    cst_view(C, "pool_scale")[:, j, :], xbuf["band"])

    ffn_seq((0, 0), hook=ex_send(16, "a"))
    pool(0, 0, "a")
    ffn_seq((0, 1), (1, 0))
    posr_d = P.dram_in("posr", [128, NT], I32)
    retw_d = P.dram_in("ret_w", [8, 6, 128, KC, 256], F32)
    sl_src = nc.dram_tensor("sl_src", [8, 256, 512], F32).ap()
    sl_dst = nc.dram_tensor("sl_dst", [8, 4 * 256, 512], F32).ap()

    def sloc_store(hh, Sst, kS):
        P.dma("sp", sl_src[hh].rearrange("(a p) v -> p a v", p=128), Sst, reads=[kS], writes=[("sls", hh)])

    def sloc_coll(hh):
        P.op("pool", lambda e: e.collective_compute("AllGather", ALU.bypass, replica_groups=GROUPS,
                                                    ins=[sl_src[hh].opt()], outs=[sl_dst[hh].opt()]),
             reads=[("sls", hh)], writes=[("sld", hh)])
    kt_s = nc.dram_tensor("kt_scr", [8, 128, 2 * NT], BF16).ap()
    v_s = nc.dram_tensor("v_scr", [8, 128, 8 * 512], BF16).ap()

    def kv_store(hh, KT, kkt, V, kv):
        P.dma("sp", kt_s[hh], KT.rearrange("p a b -> p (a b)"), reads=[kkt], writes=[("kts", hh)])
        P.dma("sp", v_s[hh], V.rearrange("p a b -> p (a b)"), reads=[kv], writes=[("vs", hh)])

    def kv_load(hh, KT, kkt, V, kv):
        P.dma("sp", KT.rearrange("p a b -> p (a b)"), kt_s[hh], reads=[("kts", hh)], writes=[kkt])
        P.dma("sp", V.rearrange("p a b -> p (a b)"), v_s[hh], reads=[("vs", hh)], writes=[kv])
    DA = {"posr": posr_d, "ret_w": retw_d, "sloc_store": sloc_store, "sloc_coll": sloc_coll, "kv_store": kv_store,
          "invf_ret": cst_view(C, "invf_ret"), "dloc": cst_view(C, "dloc")}
    phase_retA(P, C, hT, ln[:, 1, 2, :], DA)
    hsp = nc.dram_tensor("hspill", [KC * 128, NT], F32).ap()
    P.dma("sp", hsp.rearrange("(c p) t -> p c t", p=128), hT, reads=[("h",)], writes=[("hsp",)])
    P.barrier(skip_pool=True)

    def load_h2(dst):
        P.dma("sp", dst, hsp.rearrange("(c p) t -> p c t", p=128), reads=[("hsp",)], writes=[("h",)])
    rwo_d = P.dram_in("ret_wo", [KC, 128, 2 * KC, 128], F32)
    DB = {"posr": posr_d, "ret_w": retw_d, "kv_load": kv_load,
          "dtab": P.dram_in("dtab", [8, 128, 2, 128], F32),
          "ret_wo": [rwo_d[m] for m in range(KC)],
          "invf_ret": cst_view(C, "invf_ret"), "sdcol": cst_view(C, "sdcol"),
          "coef": cst_view(C, "coef"), "gn_g": cst_view(C, "gn_g"), "gn_b": cst_view(C, "gn_b"),
          "sall_ap": (lambda c, hh: sl_dst[hh][c * 256:(c + 1) * 256, :].rearrange("(a p) v -> p a v", p=128)),
          "sall_keys": (lambda hh: [("sld", hh)])}
    DB["reuse"] = DA["retA_state"]
    DB["gated_d"] = nc.dram_tensor("gated_scr", [2, 128, 2 * KC, 512], BF16).ap()
    phase_retB2(P, C, hT, ln[:, 1, 3, :], DB, load_h2)
    ffn_seq((1, 1), (2, 0), hook=ex_send(128, "b"))
    def units(name, n):
        d = P.dram_in(name, [n, 128, KC, 128], F32)
        return [d[m] for m in range(n)]
    DS = {"halo_fill": exchange(128, "b"),
          "posx": P.dram_in("posx", [128, NT + 128], I32),
          "tab": P.dram_in("swa_tab", [128, 928], F32),
          "wq": units("swa_wq", 16), "wk": units("swa_wk", 4),
          "wv": units("swa_wv", 4), "wo": units("swa_wo", 16),
          "bq": cst_view(C, "bq"), "bqs": cst_view(C, "bqs"), "bk": cst_view(C, "bk"),
          "bks": cst_view(C, "bks"), "bo": cst_view(C, "bo"),
          "invf": cst_view(C, "invf_swa"), "sgn": cst_view(C, "sgn"), "perm": C.perm[:]}
    phase_swa(P, C, hT, ln[:, 2, 2, :], ln[:, 2, 3, :], DS)
    ffn_seq((2, 1), (3, 0), hook=ex_send(16, "c"))
    pool(3, 1, "c")
    ffn_seq((3, 1))
    P.dma("sp", out_d.rearrange("(c p) t -> p c t", p=128), hT, reads=[("h",)])
    print("sem counts", P.ecnt, max(P.dcnt))
    return P.finish()


FUSED_INPUTS = (["ident", "perm"] + ["win_%d_%d" % (i, s) for i in range(4) for s in range(2)]
                + ["wout_%d_%d" % (i, s) for i in range(4) for s in range(2)]
                + ["poolw0", "poolw1", "pool_band", "posr", "ret_w", "dtab", "ret_wo", "posx", "swa_tab",
                   "swa_wq", "swa_wk", "swa_wv", "swa_wo"])


def kernel_fused(**inputs):
    host = Host(inputs)
    x = host.inp["x"]
    nc = build_fused()
    in_maps = []
    for c in range(NCORES):
        b, s0 = host.core_info(c)
        m = {"hT": np.ascontiguousarray(x[b, s0:s0 + NT, :].T), "cst": host.cst(c)}
        for n in FUSED_INPUTS:
            m[n] = host.const(n, c if n in PER_CORE else None)
        in_maps.append(m)
    res = run_bass_kernel_spmd(nc, in_maps, core_ids=list(range(NCORES)))
    out = np.empty((BATCH, SEQ, D_MODEL), np.float32)
    for c in range(NCORES):
        b, s0 = host.core_info(c)
        out[b, s0:s0 + NT, :] = np.asarray(res.results[c]["outT"]).T
    return out


def build_launch(phases):
    P = Prog()
    C = Ctx(P, NCST)
    ln = prologue(P, C)
    hT_d = P.dram_in("hT", [KC * 128, NT], F32)
    out_d = P.dram_out("outT", [KC * 128, NT], F32)
    hT = C.A.t[:, 0:KC * NT].rearrange("p (a b) -> p a b", a=KC)

    def load_h(dst):
        P.dma("sp", dst, hT_d.rearrange("(c p) t -> p c t", p=128), writes=[("h",)])

    for pi, ph in enumerate(phases):
        kind = ph[0]
        if pi == 0 and kind != "retB":
            load_h(hT)
        if kind == "ffn":
            i, s = ph[1], ph[2]
            win_d = P.dram_in("win_%d_%d" % (i, s), [FC, 128, KC, 256], F32)
            wout_d = P.dram_in("wout_%d_%d" % (i, s), [KC // 2, 128, FC, 256], F32)
            base = 0 if s == 0 else 4
            phase_ffn(P, C, hT, ln[:, i, base, :], ln[:, i, base + 1, :], win_d, wout_d)
        elif kind == "pool":
            i, j = ph[1], ph[2]
            halo_d = P.dram_in("halo16", [128, KC, 16], F32)
            pw_d = P.dram_in("poolw", [4, 128, 4, 512], F32)
            band_d = P.dram_in("pool_band", [128, 4, 4, 128], F32)
            phase_pool2(P, C, hT, ln[:, i, 2, :], ln[:, i, 3, :],
                        (lambda halo, kh, st, kst, halo_d=halo_d: P.dma("sp", halo, halo_d, writes=[kh])), pw_d,
                        cst_view(C, "pool_scale")[:, j, :], band_d)
        elif kind == "retA":
            i = ph[1]
            D = {"posr": P.dram_in("posr", [128, NT], I32),
                 "ret_w": P.dram_in("ret_w", [8, 6, 128, KC, 256], F32),
                 "invf_ret": cst_view(C, "invf_ret"), "dloc": cst_view(C, "dloc")}
            sloc_d = P.dram_out("sloc", [8, 2, 128, 512], F32)
            D["sloc_store"] = (lambda hh, Sst, kS, sloc_d=sloc_d:
                               P.dma("sp", sloc_d[hh].rearrange("a p v -> p a v"), Sst, reads=[kS]))
            phase_retA(P, C, hT, ln[:, i, 2, :], D)
        elif kind == "retB":
            i = ph[1]
            rwo_d = P.dram_in("ret_wo", [KC, 128, 2 * KC, 128], F32)
            D = {"posr": P.dram_in("posr", [128, NT], I32),
                 "ret_w": P.dram_in("ret_w", [8, 6, 128, KC, 256], F32),
                 "sall": P.dram_in("sall", [4, 8, 2, 128, 512], F32),
                 "dtab": P.dram_in("dtab", [8, 128, 2, 128], F32),
                 "ret_wo": [rwo_d[m] for m in range(KC)],
                 "invf_ret": cst_view(C, "invf_ret"), "sdcol": cst_view(C, "sdcol"),
                 "coef": cst_view(C, "coef"), "gn_g": cst_view(C, "gn_g"), "gn_b": cst_view(C, "gn_b")}
            D["sall_ap"] = (lambda c, hh, D=D: D["sall"][c, hh].rearrange("a p v -> p a v"))
            D["sall_keys"] = (lambda hh: [])
            phase_retB(P, C, hT, ln[:, i, 2, :], ln[:, i, 3, :], D, load_h)
        elif kind == "swa":
            i = ph[1]
            def units(name, n):
                d = P.dram_in(name, [n, 128, KC, 128], F32)
                return [d[m] for m in range(n)]
            halo_d = P.dram_in("halo128", [128, KC, 128], F32)
            D = {"halo_fill": (lambda halo, kh, st, kst, halo_d=halo_d: P.dma("sp", halo, halo_d, writes=[kh])),
                 "posx": P.dram_in("posx", [128, NT + 128], I32),
                 "tab": P.dram_in("swa_tab", [128, 928], F32),
                 "wq": units("swa_wq", 16), "wk": units("swa_wk", 4),
                 "wv": units("swa_wv", 4), "wo": units("swa_wo", 16),
                 "bq": cst_view(C, "bq"), "bqs": cst_view(C, "bqs"), "bk": cst_view(C, "bk"),
                 "bks": cst_view(C, "bks"), "bo": cst_view(C, "bo"),
                 "invf": cst_view(C, "invf_swa"), "sgn": cst_view(C, "sgn"), "perm": C.perm[:]}
            phase_swa(P, C, hT, ln[:, i, 2, :], ln[:, i, 3, :], D)
        else:
            raise ValueError(kind)
    P.dma("sp", out_d.rearrange("(c p) t -> p c t", p=128), hT, reads=[("h",)])
    return P.finish()


def fm_vec(v):
    v = np.asarray(v, np.float32)
    return np.ascontiguousarray(v.reshape(-1, 128).T)


def tile_w(w, UC):
    K_, N_ = w.shape
    return np.ascontiguousarray(w.reshape(K_ // 128, 128, N_ // UC, UC).transpose(2, 1, 0, 3))


def tile_win(w_in):
    g = w_in[:, :D_FF].reshape(KC, 128, FC, 128)
    u = w_in[:, D_FF:].reshape(KC, 128, FC, 128)
    out = np.empty((FC, 128, KC, 256), np.float32)
    out[:, :, :, :128] = g.transpose(2, 1, 0, 3)
    out[:, :, :, 128:] = u.transpose(2, 1, 0, 3)
    return out


def tile_wout(w_out):
    w = w_out.reshape(FC, 128, KC // 2, 256)
    return np.ascontiguousarray(w.transpose(2, 1, 0, 3))


def swap_halves(x, hd=64):
    s = x.shape
    y = x.reshape(s[:-1] + (s[-1] // hd, 2, hd // 2))[..., ::-1, :]
    return np.ascontiguousarray(y).reshape(s)


def dup_heads(x, hd=64):
    s = x.shape
    y = x.reshape(s[:-1] + (s[-1] // hd, 1, hd))
    y = np.repeat(y, 2, axis=-2)
    return np.ascontiguousarray(y).reshape(s[:-1] + (2 * s[-1],))


def hT_to_halo(hT, w):
    return np.ascontiguousarray(hT[:, NT - w:].reshape(KC, 128, w).transpose(1, 0, 2))


class Host:
    def __init__(self, inp):
        self.inp = {k: np.asarray(v) for k, v in inp.items()}
        self.cache = {}

    def core_info(self, c):
        return c // 4, (c % 4) * NT

    def cst(self, c):
        I = self.inp
        b, s0 = self.core_info(c)
        out = np.zeros((128, NCST), np.float32)

        def put(name, arr):
            o, shp = CST_OFF[name]
            n = int(np.prod(shp))
            out[:, o:o + n] = np.asarray(arr, np.float32).reshape(128, n)
        ln = np.zeros((128, 4, 6, KC), np.float32)
        for i in range(4):
            for k, nm in enumerate(("ln_ffn1", "ln_mix", "ln_ffn2")):
                for s in range(2):
                    ln[:, i, 2 * k + s, :] = fm_vec(I[nm][i, s])
        put("ln", ln)
        put("pool_scale", np.stack([fm_vec(I["pool_scale"][j]) for j in range(2)], axis=1))
        t = np.arange(16) + s0 + 1
        corr = np.stack([1.0 / np.minimum(t, w) for w in (2, 4, 8, 16)], axis=0)
        put("pool_corr", np.broadcast_to(corr[None], (128, 4, 16)))
        bi = I["swa_b_in"][0]
        put("bq", fm_vec(bi[:2048]))
        put("bqs", fm_vec(swap_halves(bi[:2048])))
        put("bk", fm_vec(dup_heads(bi[2048:2304])))
        put("bks", fm_vec(dup_heads(swap_halves(bi[2048:2304]))))
        put("bo", fm_vec(I["swa_b_out"][0]))
        p = np.arange(128)
        put("invf_swa", (10000.0 ** (-(2.0 * (p % 32)) / 64.0)).astype(np.float32)[:, None])
        put("sgn", np.where((p % 64) < 32, -1.0, 1.0)[:, None])
        put("gn_g", fm_vec(I["ret_gn_g"][0]))
        put("gn_b", fm_vec(I["ret_gn_b"][0]))
        put("invf_ret", (10000.0 ** (-np.linspace(0.0, 1.0, 128, dtype=np.float32))).astype(np.float32)[:, None])
        gam = np.array(GAMMAS, np.float64)
        put("sdcol", gam[None, :] ** (127.0 - p[:, None]))
        tt = (np.arange(8)[None, :, None] * 128 + p[:, None, None])
        put("dloc", gam[None, None, :] ** (1023.0 - tt))
        coef = np.zeros((8, 8))
        r = c % 4
        for r2 in range(4):
            if r2 < r:
                coef[r2] = gam ** (1024.0 * (r - 1 - r2))
        put("coef", np.broadcast_to(coef.reshape(1, 64), (128, 64)))
        sel = np.zeros(4)
        if r > 0:
            sel[r - 1] = 1.0
        put("sel", np.broadcast_to(sel.reshape(1, 4), (128, 4)))
        return out

    def const(self, name, c=None):
        I = self.inp
        key = (name, c)
        if key in self.cache:
            return self.cache[key]
        if name == "ident":
            v = np.eye(128, dtype=np.float32)
        elif name == "perm":
            v = np.zeros((128, 128), np.float32)
            pp = np.arange(128)
            v[pp, pp ^ 32] = 1.0
        elif name.startswith("win_"):
            _, i, s = name.split("_")
            v = tile_win(I["ffn_w_in"][int(i), int(s)])
        elif name.startswith("wout_"):
            _, i, s = name.split("_")
            v = tile_wout(I["ffn_w_out"][int(i), int(s)])
        elif name.startswith("poolw"):
            j = int(name[5:])
            v = np.ascontiguousarray(I["pool_w"][j].reshape(4, 4, 128, 512).transpose(0, 2, 1, 3))
        elif name == "pool_band":
            b, s0 = self.core_info(c)
            v = np.zeros((128, 4, 4, 128), np.float32)
            s_ = np.arange(128)[:, None]
            t_ = np.arange(128)[None, :]
            for g, w in enumerate((2, 4, 8, 16)):
                cur = ((t_ - s_ >= 0) & (t_ - s_ < w)) / float(w)
                prev = ((t_ - (s_ - 128)) < w) / float(w)
                if s0 == 0:
                    cur0 = ((t_ - s_ >= 0) & (t_ - s_ < w)) / np.minimum(t_ + 1.0, float(w))
                else:
                    cur0 = cur
                v[:, g, 0, :] = cur - np.eye(128)
                v[:, g, 1, :] = prev
                v[:, g, 2, :] = cur0 - np.eye(128)
                v[0:16, g, 3, :] = prev[112:128, :]
        elif name == "ret_w":
            w = I["ret_w_in"][0]
            v = np.empty((8, 6, 128, KC, 256), np.float32)
            for h in range(8):
                cols = [w[:, h * 256:(h + 1) * 256], w[:, 2048 + h * 256:2048 + (h + 1) * 256],
                        w[:, 4096 + h * 512:4096 + h * 512 + 256], w[:, 4096 + h * 512 + 256:4096 + (h + 1) * 512],
                        w[:, 8192 + h * 512:8192 + h * 512 + 256], w[:, 8192 + h * 512 + 256:8192 + (h + 1) * 512]]
                for ui, x in enumerate(cols):
                    v[h, ui] = x.reshape(KC, 128, 256).transpose(1, 0, 2)
        elif name == "ret_wo":
            v = tile_w(I["ret_w_out"][0], 128)
        elif name == "dtab":
            gam = np.array(GAMMAS, np.float64)
            i_ = np.arange(128)
            rel = i_[None, :] - i_[:, None]
            v = np.zeros((8, 128, 2, 128), np.float32)
            for h in range(8):
                v[h, :, 0, :] = np.where(rel >= 0, gam[h] ** np.maximum(rel, 0), 0.0)
                v[h, :, 1, :] = (gam[h] ** (i_ + 1.0))[None, :]
        elif name in ("swa_wq", "swa_wqs", "swa_wk", "swa_wks", "swa_wv", "swa_wo"):
            w = I["swa_w_in"][0]
            if name == "swa_wq":
                v = tile_w(w[:, :2048], 128)
            elif name == "swa_wqs":
                v = tile_w(swap_halves(w[:, :2048]), 128)
            elif name == "swa_wk":
                v = tile_w(dup_heads(w[:, 2048:2304]), 128)
            elif name == "swa_wks":
                v = tile_w(dup_heads(swap_halves(w[:, 2048:2304])), 128)
            elif name == "swa_wv":
                v = tile_w(dup_heads(w[:, 2304:2560]), 128)
            else:
                v = tile_w(I["swa_w_out"][0], 128)
        elif name == "posr":
            b, s0 = self.core_info(c)
            v = np.ascontiguousarray(np.broadcast_to(I["positions"][b, s0:s0 + NT].astype(np.int32)[None], (128, NT)))
        elif name == "posx":
            b, s0 = self.core_info(c)
            pos = np.zeros(NT + 128, np.int32)
            pos[128:] = I["positions"][b, s0:s0 + NT]
            if s0 > 0:
                pos[:128] = I["positions"][b, s0 - 128:s0]
            v = np.ascontiguousarray(np.broadcast_to(pos[None], (128, NT + 128)))
        elif name == "swa_tab":
            b, s0 = self.core_info(c)
            tab = np.zeros((128, 928), np.float32)
            tab[:, 0:512] = dup_heads(I["swa_b_in"][0][2304:2560])[None, :]
            j = np.arange(128)[:, None]
            i_ = np.arange(128)[None, :]
            tab[:, 512:640] = (j <= i_)
            tab[:, 640:768] = (j > i_)
            tab[:, 768:896] = (j > i_) if s0 > 0 else 0.0
            tab[:, 896:928] = I["swa_sinks"][0][None, :]
            v = tab
        else:
            raise KeyError(name)
        self.cache[key] = v
        return v


PER_CORE = ("posr", "posx", "swa_tab", "pool_band")


def run_launch(host, phases, hTs, extra):
    nc = build_launch(phases)
    names = ["ident", "perm"]
    rename = {}
    for ph in phases:
        if ph[0] == "ffn":
            names += ["win_%d_%d" % (ph[1], ph[2]), "wout_%d_%d" % (ph[1], ph[2])]
        elif ph[0] == "pool":
            rename["poolw"] = "poolw%d" % ph[2]
            names += ["poolw", "pool_band"]
        elif ph[0] == "retA":
            names += ["posr", "ret_w"]
        elif ph[0] == "retB":
            names += ["posr", "ret_w", "dtab", "ret_wo"]
        elif ph[0] == "swa":
            names += ["posx", "swa_tab", "swa_wq", "swa_wk", "swa_wv", "swa_wo"]
    in_maps = []
    for c in range(NCORES):
        m = {"hT": hTs[c], "cst": host.cst(c)}
        for n in names:
            src = rename.get(n, n)
            m[n] = host.const(src, c if src in PER_CORE else None)
        for k, v in extra.items():
            m[k] = v[c]
        in_maps.append(m)
    res = run_bass_kernel_spmd(nc, in_maps, core_ids=list(range(NCORES)))
    return res.results


LAUNCHES = [
    [("ffn", 0, 0)],
    [("pool", 0, 0), ("ffn", 0, 1), ("ffn", 1, 0), ("retA", 1)],
    [("retB", 1), ("ffn", 1, 1), ("ffn", 2, 0)],
    [("swa", 2), ("ffn", 2, 1), ("ffn", 3, 0)],
    [("pool", 3, 1), ("ffn", 3, 1)],
]


def halos(hTs, w):
    out = []
    for c in range(NCORES):
        if c % 4 == 0:
            out.append(np.zeros((128, KC, w), np.float32))
        else:
            out.append(hT_to_halo(hTs[c - 1], w))
    return out


def kernel(**inputs):
    return kernel_fused(**inputs)


def kernel_unfused(**inputs):
    host = Host(inputs)
    x = host.inp["x"]
    hTs = []
    for c in range(NCORES):
        b, s0 = host.core_info(c)
        hTs.append(np.ascontiguousarray(x[b, s0:s0 + NT, :].T))
    sall = None
    for li, phases in enumerate(LAUNCHES):
        extra = {}
        kinds = [p[0] for p in phases]
        if "pool" in kinds:
            extra["halo16"] = halos(hTs, 16)
        if "swa" in kinds:
            extra["halo128"] = halos(hTs, 128)
        if "retB" in kinds:
            extra["sall"] = [np.ascontiguousarray(sall[4 * (c // 4):4 * (c // 4) + 4]) for c in range(NCORES)]
        res = run_launch(host, phases, hTs, extra)
        hTs = [np.asarray(r["outT"]) for r in res]
        if "retA" in kinds:
            sall = np.ascontiguousarray(np.stack([np.asarray(r["sloc"]) for r in res], axis=0))
    out = np.empty((BATCH, SEQ, D_MODEL), np.float32)
    for c in range(NCORES):
        b, s0 = host.core_info(c)
        out[b, s0:s0 + NT, :] = hTs[c].T
    return out
```

```python
import numpy as np
import concourse.bass as bass
import concourse.mybir as mybir
from concourse.bass_utils import run_bass_kernel_spmd

F32 = mybir.dt.float32
BF16 = mybir.dt.bfloat16
I32 = mybir.dt.int32
ALU = mybir.AluOpType
AF = mybir.ActivationFunctionType

D_MODEL = 2048
D_FF = 5504
SEQ = 4096
BATCH = 2
NCORES = 8
NT = 1024
EPS = 1e-6


class Ev:
    __slots__ = ("sem", "val", "eng")

    def __init__(self, sem, val, eng):
        self.sem, self.val, self.eng = sem, val, eng


class Prog:
    ENGS = ("pe", "act", "dve", "pool", "sp")

    def __init__(self, n_dma_sems=32):
        self.nc = bass.Bass("TRN2", target_bir_lowering=False)
        nc = self.nc
        self.ops = {e: [] for e in self.ENGS}
        self.esem = {e: nc.alloc_semaphore("es_" + e) for e in ("pe", "act", "dve", "pool")}
        self.ecnt = {e: 0 for e in self.esem}
        self.pending = {e: [] for e in self.ENGS}
        self.dsems = [nc.alloc_semaphore("ds%d" % i) for i in range(n_dma_sems)]
        self.dcnt = [0] * n_dma_sems
        self.dnext = 0
        self.waited = {}
        self.res = {}
        self.n_sb = 0

    def sbuf(self, name, shape, dtype):
        return self.nc.alloc_sbuf_tensor(name, list(shape), dtype)

    def psum(self, name, shape, dtype=F32):
        return self.nc.alloc_psum_tensor(name, list(shape), dtype)

    def dram_in(self, name, shape, dtype):
        return self.nc.dram_tensor(name, list(shape), dtype, kind="ExternalInput").ap()

    def dram_out(self, name, shape, dtype):
        return self.nc.dram_tensor(name, list(shape), dtype, kind="ExternalOutput").ap()

    def _flush(self, eng):
        pend = self.pending[eng]
        if not pend:
            return
        last = self.ops[eng][-1]
        if not last["inc"]:
            last["inc"] = True
            self.ecnt[eng] += 1
        ev = Ev(self.esem[eng], self.ecnt[eng], eng)
        for key, is_w in pend:
            self._record(key, is_w, ev)
        self.pending[eng] = []

    def _record(self, key, is_w, ev):
        st = self.res.get(key)
        if st is None:
            st = self.res[key] = [None, []]
        if is_w:
            st[0] = ev
            st[1] = []
        else:
            rl = st[1]
            for i, o in enumerate(rl):
                if o.sem is ev.sem:
                    if o.val < ev.val:
                        rl[i] = ev
                    break
            else:
                rl.append(ev)

    def _deps(self, eng, reads, writes):
        pe_pend = self.pending["pe"]
        if pe_pend:
            keys = set(k for k, _ in pe_pend)
            if any(k in keys for k in reads) or any(k in keys for k in writes):
                if eng != "pe":
                    self._flush("pe")
        evs = []
        for r in reads:
            st = self.res.get(r)
            if st is not None and st[0] is not None:
                evs.append(st[0])
        for w in writes:
            st = self.res.get(w)
            if st is not None:
                if st[0] is not None:
                    evs.append(st[0])
                evs.extend(st[1])
        best = {}
        for ev in evs:
            if ev.eng == eng and eng == "pe":
                continue
            k = (eng, id(ev.sem))
            if self.waited.get(k, 0) >= ev.val:
                continue
            cur = best.get(id(ev.sem))
            if cur is None or cur[1] < ev.val:
                best[id(ev.sem)] = (ev.sem, ev.val)
        waits = []
        for sem, val in best.values():
            self.waited[(eng, id(sem))] = val
            waits.append((sem, val))
        return waits

    def op(self, eng, fn, reads=(), writes=(), inc=True):
        waits = self._deps(eng, reads, writes)
        rec = {"fn": fn, "waits": waits, "inc": False, "dma": None}
        self.ops[eng].append(rec)
        pend = self.pending[eng]
        for r in reads:
            pend.append((r, False))
        for w in writes:
            pend.append((w, True))
        if inc:
            self._flush(eng)
        return rec

    def dma(self, q, out_ap, in_ap, reads=(), writes=(), fn=None):
        i = self.dnext
        self.dnext = (i + 1) % len(self.dsems)
        sem = self.dsems[i]
        prev = self.dcnt[i]
        waits = self._deps(q, reads, writes)
        if prev > 0 and self.waited.get((q, id(sem)), 0) < prev:
            self.waited[(q, id(sem))] = prev
            waits = [w for w in waits if w[0] is not sem] + [(sem, prev)]
        self.dcnt[i] = prev + 16
        ev = Ev(sem, prev + 16, "dma")
        if fn is None:
            fn = (lambda e, o=out_ap, s=in_ap: e.dma_start(out=o, in_=s))
        rec = {"fn": fn, "waits": waits, "inc": False, "dma": sem}
        self.ops[q].append(rec)
        for r in reads:
            self._record(r, False, ev)
        for w in writes:
            self._record(w, True, ev)
        return ev

    def barrier(self, skip_pool=False):
        for e in self.esem:
            if self.pending[e]:
                self._flush(e)
        targets = [(self.esem[e], self.ecnt[e]) for e in self.esem
                   if self.ecnt[e] > 0 and not (skip_pool and e == "pool")]
        targets += [(s, c) for s, c in zip(self.dsems, self.dcnt) if c > 0]
        for e in self.ENGS:
            waits = []
            for sem, val in targets:
                if e in self.esem and sem is self.esem[e]:
                    continue
                if self.waited.get((e, id(sem)), 0) >= val:
                    continue
                self.waited[(e, id(sem))] = val
                waits.append((sem, val))
            if waits:
                self.ops[e].append({"fn": None, "waits": waits, "inc": False, "dma": None})

    def finish(self):
        self.barrier()
        nc = self.nc
        prog = self

        def mk(name):
            def body(eng):
                for rec in prog.ops[name]:
                    for sem, val in rec["waits"]:
                        eng.wait_ge(sem, val)
                    if rec["fn"] is None:
                        continue
                    ins = rec["fn"](eng)
                    if rec["dma"] is not None:
                        ins.then_inc(rec["dma"], 16)
                    elif rec["inc"]:
                        ins.then_inc(prog.esem[name], 1)
            return body

        with nc.Block() as block:
            block.tensor(mk("pe"))
            block.scalar(mk("act"))
            block.vector(mk("dve"))
            block.gpsimd(mk("pool"))
            block.sync(mk("sp"))
        return nc


import math

TWO_PI = 2.0 * math.pi
MAGIC = 12582912.0
PI_LO = 3.1415920
ARENA_BYTES = 200 * 1024
KC = D_MODEL // 128
FC = D_FF // 128


class Banks:
    def __init__(self, ids):
        self.ids = list(ids)
        self.i = 0

    def next(self):
        b = self.ids[self.i]
        self.i = (self.i + 1) % len(self.ids)
        return b


class Arena:
    def __init__(self, P, nbytes):
        self.t = P.sbuf("arena", [128, nbytes // 4], F32)
        self.cap = nbytes
        self.off = 0

    def reset(self, off=0):
        self.off = off

    def alloc(self, shape, dtype):
        esz = 2 if dtype is BF16 else 4
        n = 1
        for d in shape[1:]:
            n *= d
        nbytes = (n * esz + 31) // 32 * 32
        o = self.off
        self.off += nbytes
        assert self.off <= self.cap, ("arena overflow", self.off, self.cap)
        v = self.t[:, o // 4:(o + nbytes) // 4]
        if dtype is not F32:
            v = v.bitcast(dtype)
        v = v[:, 0:n]
        if len(shape) == 3:
            v = v.rearrange("p (a b) -> p a b", a=shape[1])
        elif len(shape) == 4:
            v = v.rearrange("p (a b c) -> p a b c", a=shape[1], b=shape[2])
        return v


class Ctx:
    def __init__(self, P, cst_cols):
        self.P = P
        self.ps = [P.psum("ps%d" % i, [128, 512]) for i in range(8)]
        self.ones = P.sbuf("ones_bf", [128, 128], BF16)
        self.ident = P.sbuf("ident_bf", [128, 128], BF16)
        self.perm = P.sbuf("perm_bf", [128, 128], BF16)
        self.eps = P.sbuf("eps_c", [128, 1], F32)
        self.cst = P.sbuf("cst_sb", [128, cst_cols], F32)
        P.op("dve", lambda e: e.memset(self.ones[:], 1.0), writes=[("ones",)])
        P.op("dve", lambda e: e.memset(self.eps[:], EPS), writes=[("ones",)])
        self.A = Arena(P, ARENA_BYTES)
        self.ring = []
        self.ring_i = 0
        self.uid = 0

    def key(self, name):
        self.uid += 1
        return (name, self.uid)

    def make_ring(self, nslots, elems=4096):
        self.ring = [self.A.alloc([128, elems], BF16) for _ in range(nslots)]
        self.ring_keys = [self.key("w") for _ in range(nslots)]
        self.ring_i = 0

    def next_slot(self):
        i = self.ring_i
        self.ring_i = (i + 1) % len(self.ring)
        return self.ring_keys[i], self.ring[i]


def mm(P, out, lhsT, rhs, start, stop, reads, writes, inc=None):
    if inc is None:
        inc = stop
    P.op("pe", lambda e: e.matmul(out, lhsT, rhs, start=start, stop=stop),
         reads=reads, writes=writes, inc=inc)


def emit_rstd(P, C, sq, nchunks, n, rstd, bank, dim, key_sq, key_rstd, extra=None):
    ps = C.ps[bank]
    for c in range(nchunks):
        mm(P, ps[:, 0:n], C.ones[:], sq[:, c, :], c == 0, c == nchunks - 1,
           reads=[key_sq, ("ones",)], writes=[("ps", bank)])
    P.op("act", lambda e: e.activation(rstd, ps[:, 0:n], AF.Sqrt, bias=C.eps[:, 0:1], scale=1.0 / dim),
         reads=[("ps", bank), ("ones",)], writes=[key_rstd])
    P.op("dve", lambda e: e.reciprocal(rstd, rstd), reads=[key_rstd], writes=[key_rstd])


def emit_postnorm(P, C, f, hs, g, rstd, sq, n, kf, ksq, krstd):
    P.op("act", lambda e: e.activation(sq, f, AF.Square), reads=[kf], writes=[ksq])
    emit_rstd(P, C, sq, KC, n, rstd, 7, D_MODEL, ksq, krstd)
    for c in range(KC):
        P.op("dve", lambda e, c=c: e.tensor_tensor(f[:, c, :], f[:, c, :], rstd, ALU.mult),
             reads=[kf, krstd], writes=[kf])
        P.op("dve", lambda e, c=c: e.scalar_tensor_tensor(
            hs[:, c, :], f[:, c, :], g[:, c:c + 1], hs[:, c, :], ALU.mult, ALU.add),
            reads=[kf, ("cst",), ("h",)], writes=[("h",)])


def emit_prenorm(P, C, hs, g, u, sq, rstd, n, ksq, krstd, ku):
    P.op("act", lambda e: e.activation(sq, hs, AF.Square), reads=[("h",)], writes=[ksq])
    emit_rstd(P, C, sq, KC, n, rstd, 7, D_MODEL, ksq, krstd)
    for c in range(KC):
        P.op("dve", lambda e, c=c: e.scalar_tensor_tensor(
            u[:, c, :], hs[:, c, :], g[:, c:c + 1], rstd, ALU.mult, ALU.mult),
            reads=[("h",), krstd, ("cst",)], writes=[ku])


def linear_fm(P, C, w_units, kc, tiles, rhs, rkeys, consume, banks):
    mc = 0
    for wu in w_units:
        UC = wu.shape[2]
        wk, slot = C.next_slot()
        P.dma("pool", slot[:, 0:kc * UC], wu.rearrange("p k c -> p (k c)"), writes=[wk])
        sv = slot[:, 0:kc * UC].rearrange("p (k c) -> p k c", k=kc)
        for m in range(UC // 128):
            for ti, (t0, n) in enumerate(tiles):
                b = banks.next()
                for k in range(kc):
                    mm(P, C.ps[b][:, 0:n], sv[:, k, m * 128:(m + 1) * 128], rhs(k, ti),
                       k == 0, k == kc - 1, reads=[wk] + rkeys, writes=[("ps", b)])
                consume(mc, ti, C.ps[b][:, 0:n], b)
            mc += 1


def linear_tm(P, C, w_units, kc, ntiles, lhs, lkeys, consume, banks):
    for ui, wu in enumerate(w_units):
        UC = wu.shape[2]
        wk, slot = C.next_slot()
        P.dma("pool", slot[:, 0:kc * UC], wu.rearrange("p k c -> p (k c)"), writes=[wk])
        sv = slot[:, 0:kc * UC].rearrange("p (k c) -> p k c", k=kc)
        for ti in range(ntiles):
            b = banks.next()
            for k in range(kc):
                mm(P, C.ps[b][:, 0:UC], lhs(k, ti), sv[:, k, :], k == 0, k == kc - 1,
                   reads=[wk] + lkeys, writes=[("ps", b)])
            consume(ui, ti, C.ps[b][:, 0:UC], b)


def emit_sincos(P, C, ang, tmp, sin_out, cos_out, n, kang, ktmp, ksin, kcos):
    for shift, dst, kd in ((0.0, sin_out, ksin), (0.5 * math.pi, cos_out, kcos)):
        if dst is None:
            continue
        src, ks = ang, kang
        if shift != 0.0:
            P.op("dve", lambda e, dst=dst, shift=shift: e.tensor_scalar(dst, ang, shift, None, ALU.add),
                 reads=[kang], writes=[kd])
            src, ks = dst, kd
        P.op("dve", lambda e, src=src: e.tensor_scalar(tmp, src, 1.0 / TWO_PI, MAGIC, ALU.mult, ALU.add),
             reads=[ks], writes=[ktmp])
        P.op("dve", lambda e: e.tensor_scalar(tmp, tmp, -MAGIC, None, ALU.add),
             reads=[ktmp], writes=[ktmp])
        P.op("dve", lambda e, dst=dst, src=src: e.scalar_tensor_tensor(dst, tmp, -TWO_PI, src, ALU.mult, ALU.add),
             reads=[ktmp, ks], writes=[kd])
        P.op("dve", lambda e, dst=dst: e.tensor_scalar(dst, dst, PI_LO, -PI_LO, ALU.min, ALU.max),
             reads=[kd], writes=[kd])
        P.op("act", lambda e, dst=dst: e.activation(dst, dst, AF.Sin), reads=[kd], writes=[kd])


def phase_ffn(P, C, hT, g1, g2h, win_d, wout_d):
    A = C.A
    T = 512
    A.reset(NT * KC * 4)
    u = A.alloc([128, KC, T], BF16)
    hid = A.alloc([128, FC, T], BF16)
    f = A.alloc([128, KC, T], F32)
    sg = A.alloc([128, 2, T], F32)
    rstd = A.alloc([128, T], F32)
    C.make_ring(3)
    ku, khid, kf, krstd = C.key("u"), C.key("hid"), C.key("f"), C.key("rstd")
    ksg = [C.key("sg0"), C.key("sg1")]
    for t0 in range(0, NT, T):
        hs = hT[:, :, t0:t0 + T]
        emit_prenorm(P, C, hs, g1, u, hid[:, 0:KC, :], rstd, T, khid, krstd, ku)
        for j in range(FC):
            wk, slot = C.next_slot()
            P.dma("pool", slot[:, 0:KC * 256], win_d[j].rearrange("p k c -> p (k c)"), writes=[wk])
            sv = slot[:, 0:KC * 256].rearrange("p (k c) -> p k c", k=KC)
            pair = j % 2
            bg, bu = 2 * pair, 2 * pair + 1
            for k in range(KC):
                mm(P, C.ps[bg][:, 0:T], sv[:, k, 0:128], u[:, k, :], k == 0, k == KC - 1,
                   reads=[wk, ku], writes=[("ps", bg)])
            for k in range(KC):
                mm(P, C.ps[bu][:, 0:T], sv[:, k, 128:256], u[:, k, :], k == 0, k == KC - 1,
                   reads=[wk, ku], writes=[("ps", bu)])
            P.op("act", lambda e, bg=bg, pair=pair: e.activation(sg[:, pair, :], C.ps[bg][:, 0:T], AF.Silu),
                 reads=[("ps", bg)], writes=[ksg[pair]])
            P.op("dve", lambda e, bu=bu, pair=pair, j=j: e.tensor_tensor(
                hid[:, j, :], sg[:, pair, :], C.ps[bu][:, 0:T], ALU.mult),
                reads=[ksg[pair], ("ps", bu)], writes=[khid])
        unit = 16
        for dg in range(KC // 2):
            par = dg % 2
            banks = (4 + 2 * par, 5 + 2 * par)
            k0 = 0
            while k0 < FC:
                nk = min(unit, FC - k0)
                wk, slot = C.next_slot()
                P.dma("pool", slot[:, 0:nk * 256],
                      wout_d[dg, :, k0:k0 + nk, :].rearrange("p k c -> p (k c)"), writes=[wk])
                sv = slot[:, 0:nk * 256].rearrange("p (k c) -> p k c", k=nk)
                for m in range(2):
                    for kk in range(nk):
                        kc_ = k0 + kk
                        mm(P, C.ps[banks[m]][:, 0:T], sv[:, kk, m * 128:(m + 1) * 128], hid[:, kc_, :],
                           kc_ == 0, kc_ == FC - 1, reads=[wk, khid], writes=[("ps", banks[m])],
                           inc=(kk == nk - 1))
                k0 += nk
            for m in range(2):
                c = dg * 2 + m
                P.op("act", lambda e, c=c, b=banks[m]: e.activation(f[:, c, :], C.ps[b][:, 0:T], AF.Copy),
                     reads=[("ps", banks[m])], writes=[kf])
        emit_postnorm(P, C, f, hs, g2h, rstd, u, T, kf, ku, krstd)
    P.barrier()


def phase_pool(P, C, hT, g_pre, g_post, halo_fill, poolw_d, scale, corr):
    A = C.A
    T = 512
    NX = NT + 16
    A.reset(NT * KC * 4)
    rstdx = A.alloc([128, NX], F32)
    halo = A.alloc([128, KC, 16], F32)
    mixed = A.alloc([128, KC, NT], BF16)
    f = A.alloc([128, KC, T], F32)
    sq = A.alloc([128, KC, T], BF16)
    bufsets = [[A.alloc([128, NX], F32) for _ in range(3)] for _ in range(2)]
    smalls = [A.alloc([128, 16], F32) for _ in range(2)]
    wp = A.alloc([128, 4, 4, 512], BF16)
    rstd = A.alloc([128, T], F32)
    khalo, ksq, krx, kf, kwp, krstd = (C.key(n) for n in ("halo", "sq", "rstdx", "f", "wp", "rstd"))
    kmixs = [C.key("mixed%d" % c) for c in range(KC)]
    kbs = [[C.key("b%d" % i) for i in range(3)] for _ in range(2)]
    ksms = [C.key("small0"), C.key("small1")]
    stage16 = A.alloc([128, KC, 16], F32)
    halo_fill(halo, khalo, stage16, C.key("st16"))
    for g in range(4):
        P.dma("pool", wp[:, g, :, :], poolw_d[g], writes=[kwp])
    P.op("act", lambda e: e.activation(sq[:, :, 0:16], halo, AF.Square), reads=[khalo], writes=[ksq])
    emit_rstd(P, C, sq[:, :, 0:16], KC, 16, rstdx[:, 0:16], 7, D_MODEL, ksq, krx)
    for t0 in range(0, NT, T):
        P.op("act", lambda e, t0=t0: e.activation(sq, hT[:, :, t0:t0 + T], AF.Square),
             reads=[("h",)], writes=[ksq])
        emit_rstd(P, C, sq, KC, T, rstdx[:, 16 + t0:16 + t0 + T], 7, D_MODEL, ksq, krx)
    def chunk_ops(c, ei):
        g = c // 4
        w = 2 ** (g + 1)
        a, b1, b2 = bufsets[ei]
        ka, k1, k2 = kbs[ei]
        small, ksm = smalls[ei], ksms[ei]
        kmix = kmixs[c]
        ops = []
        ops.append(lambda: P.op("dve", lambda e: e.scalar_tensor_tensor(
            a[:, 0:16], halo[:, c, :], g_pre[:, c:c + 1], rstdx[:, 0:16], ALU.mult, ALU.mult),
            reads=[khalo, krx, ("cst",)], writes=[ka]))
        ops.append(lambda: P.op("dve", lambda e: e.scalar_tensor_tensor(
            a[:, 16:NX], hT[:, c, :], g_pre[:, c:c + 1], rstdx[:, 16:NX], ALU.mult, ALU.mult),
            reads=[("h",), krx, ("cst",)], writes=[ka]))
        cur, kcur = a, ka
        nxt = [(b1, k1), (b2, k2)]
        for s_ in range(g + 1):
            sh = 2 ** s_
            lo = 2 ** (s_ + 1) - 1
            dst, kd = nxt[s_ % 2]
            ops.append(lambda cur=cur, dst=dst, sh=sh, lo=lo, kcur=kcur, kd=kd: P.op(
                "dve", lambda e: e.tensor_tensor(dst[:, lo:NX], cur[:, lo:NX], cur[:, lo - sh:NX - sh], ALU.add),
                reads=[kcur], writes=[kd]))
            cur, kcur = dst, kd
        ops.append(lambda cur=cur, kcur=kcur: P.op("dve", lambda e: e.scalar_tensor_tensor(
            mixed[:, c, 16:NT], cur[:, 32:NX], 1.0 / w, a[:, 32:NX], ALU.mult, ALU.subtract),
            reads=[kcur, ka], writes=[kmix]))
        ops.append(lambda cur=cur, kcur=kcur: P.op("dve", lambda e: e.tensor_tensor(
            small, cur[:, 16:32], corr[:, g, :], ALU.mult), reads=[kcur, ("cst",)], writes=[ksm]))
        ops.append(lambda: P.op("dve", lambda e: e.tensor_tensor(
            mixed[:, c, 0:16], small, a[:, 16:32], ALU.subtract), reads=[ksm, ka], writes=[kmix]))
        return ops
    for c in range(0, KC, 2):
        oa, ob_ = chunk_ops(c, 0), chunk_ops(c + 1, 1)
        for i in range(max(len(oa), len(ob_))):
            if i < len(oa):
                oa[i]()
            if i < len(ob_):
                ob_[i]()
    banks = Banks([0, 1, 2, 3])
    for t0 in range(0, NT, T):
        for g in range(4):
            for m in range(4):
                b = banks.next()
                c = g * 4 + m
                for k in range(4):
                    mm(P, C.ps[b][:, 0:T], wp[:, g, k, m * 128:(m + 1) * 128], mixed[:, g * 4 + k, t0:t0 + T],
                       k == 0, k == 3, reads=[kwp, kmixs[g * 4 + k]], writes=[("ps", b)])
                P.op("dve", lambda e, c=c, b=b: e.tensor_scalar(
                    f[:, c, :], C.ps[b][:, 0:T], scale[:, c:c + 1], None, ALU.mult),
                    reads=[("ps", b), ("cst",)], writes=[kf])
        emit_postnorm(P, C, f, hT[:, :, t0:t0 + T], g_post, rstd, sq, T, kf, ksq, krstd)
    P.barrier()


def rot_linear(P, C, w_units, ws_units, kc, tiles, rhs, rkeys, bias, bias_s, cosT, sinT, tmp, ktmp,
               tB, ktB, out, kout, banks, ktab, col0=0):
    mc0 = 0
    cnt = [0]
    for wu, wsu in zip(w_units, ws_units):
        nm = wu.shape[2] // 128

        def cons_a(mc, ti, ps, b, mc0=mc0):
            t0, n = tiles[ti]
            P.op("dve", lambda e: e.scalar_tensor_tensor(
                tmp[:, mc, t0:t0 + n], ps, bias[:, mc0 + mc:mc0 + mc + 1],
                cosT[:, col0 + t0:col0 + t0 + n], ALU.add, ALU.mult),
                reads=[("ps", b), ("cst",), ktab], writes=[ktmp])

        def cons_b(mc, ti, ps, b, mc0=mc0):
            t0, n = tiles[ti]
            o = out(mc0 + mc)
            bi = cnt[0] % 2
            cnt[0] += 1
            P.op("dve", lambda e: e.scalar_tensor_tensor(
                tB[:, bi, 0:n], ps, bias_s[:, mc0 + mc:mc0 + mc + 1],
                sinT[:, col0 + t0:col0 + t0 + n], ALU.add, ALU.mult),
                reads=[("ps", b), ("cst",), ktab], writes=[ktB[bi]])
            P.op("dve", lambda e: e.tensor_tensor(
                o[:, t0:t0 + n], tmp[:, mc, t0:t0 + n], tB[:, bi, 0:n], ALU.add),
                reads=[ktmp, ktB[bi]], writes=[kout])

        linear_fm(P, C, [wu], kc, tiles, rhs, rkeys, cons_a, banks)
        linear_fm(P, C, [wsu], kc, tiles, rhs, rkeys, cons_b, banks)
        mc0 += nm


def rot_linear_perm(P, C, w_units, kc, tiles, rhs, rkeys, bias, cosT, sinT, t1, kt1, qb, kqb, tB, ktB,
                    out, kout, banks, ktab, perm, col0=0):
    cnt = [0]

    def cons(mc, ti, ps, b):
        t0, n = tiles[ti]
        bi = cnt[0] % 2
        cnt[0] += 1
        o = out(mc)
        P.op("dve", lambda e: e.tensor_scalar(qb[:, bi, 0:n], ps, bias[:, mc:mc + 1], None, ALU.add),
             reads=[("ps", b), ("cst",)], writes=[kqb[bi]])
        P.op("dve", lambda e: e.scalar_tensor_tensor(
            t1[:, bi, 0:n], ps, bias[:, mc:mc + 1], cosT[:, col0 + t0:col0 + t0 + n], ALU.add, ALU.mult),
            reads=[("ps", b), ("cst",), ktab], writes=[kt1[bi]])
        b2 = banks.next()
        mm(P, C.ps[b2][:, 0:n], perm, qb[:, bi, 0:n], True, True, reads=[kqb[bi], ("ident",)],
           writes=[("ps", b2)])
        P.op("dve", lambda e: e.tensor_tensor(tB[:, bi, 0:n], C.ps[b2][:, 0:n],
                                              sinT[:, col0 + t0:col0 + t0 + n], ALU.mult),
             reads=[("ps", b2), ktab], writes=[ktB[bi]])
        P.op("dve", lambda e: e.tensor_tensor(o[:, t0:t0 + n], t1[:, bi, 0:n], tB[:, bi, 0:n], ALU.add),
             reads=[kt1[bi], ktB[bi]], writes=[kout])
    linear_fm(P, C, w_units, kc, tiles, rhs, rkeys, cons, banks)


def phase_swa(P, C, hT, g_pre, g_post, D):
    A = C.A
    T = 512
    NX = NT + 128
    tiles_x = [(0, 512), (512, 512), (1024, 128)]
    tiles_o = [(0, 512), (512, 512)]
    A.reset(NT * KC * 4)
    u = A.alloc([128, KC, NX], BF16)
    QA = A.alloc([128, KC, NT], BF16)
    cosT = A.alloc([128, NX], F32)
    sinT = A.alloc([128, NX], F32)
    kv_flat = A.alloc([128, 4 * NX + 9 * 512], BF16)
    KT = kv_flat[:, 0:4 * NX].rearrange("p (a b) -> p a b", a=4)
    V = kv_flat[:, 4 * NX:4 * NX + 9 * 512].rearrange("p (a b) -> p a b", a=9)
    sq2 = kv_flat[:, 0:KC * T].rearrange("p (a b) -> p a b", a=KC)
    Pm = A.alloc([128, 2, 2, 512], BF16)
    rden = A.alloc([128, 2, 512], F32)
    masks = A.alloc([128, 3, 4, 128], BF16)
    esink = A.alloc([128, 32], F32)
    tab = A.alloc([128, 512 + 384 + 32], F32)
    rstd = A.alloc([128, T], F32)
    t1 = A.alloc([128, 2, 512], F32)
    qb = A.alloc([128, 2, 512], BF16)
    tB = A.alloc([128, 2, 512], F32)
    ktB = [C.key("tB"), C.key("tB")]
    kt1 = [C.key("t1"), C.key("t1")]
    kqb = [C.key("qb"), C.key("qb")]
    C.make_ring(2, 2048)
    ku, kqa, ktab, kkt, kv, krstd, ktq, kmask, ktb = (C.key(n) for n in
        ("u", "qa", "tab", "kt", "v", "rstd", "tq", "mask", "tb"))
    kpm = [[C.key("pm"), C.key("pm")], [C.key("pm"), C.key("pm")]]
    krd = [C.key("rd"), C.key("rd")]
    mark = A.off
    A.reset(NT * KC * 4 + NX * KC * 2)
    halo = A.alloc([128, KC, 128], F32)
    posi = A.alloc([128, NX], I32)
    ang = A.alloc([128, NX], F32)
    tmp = A.alloc([128, NX], F32)
    sq = A.alloc([128, KC, 128], BF16)
    A.reset(mark)
    khalo, kpos, kang, ktmp, ksq = (C.key(n) for n in ("halo", "posi", "ang", "tmp", "sq"))
    stage128 = kv_flat.bitcast(F32)[:, 0:KC * 128].rearrange("p (a b) -> p a b", a=KC)
    D["halo_fill"](halo, khalo, stage128, kkt)
    P.dma("sp", posi, D["posx"], writes=[kpos])
    P.dma("sp", tab, D["tab"], writes=[ktb])
    P.op("dve", lambda e: e.tensor_copy(ang, posi), reads=[kpos], writes=[kang])
    P.op("dve", lambda e: e.tensor_scalar(ang, ang, D["invf"], None, ALU.mult),
         reads=[kang, ("cst",)], writes=[kang])
    emit_sincos(P, C, ang, tmp, sinT, cosT, NX, kang, ktmp, ktab, ktab)
    P.op("dve", lambda e: e.tensor_scalar(sinT, sinT, D["sgn"], None, ALU.mult),
         reads=[ktab, ("cst",)], writes=[ktab])
    for mi_ in range(3):
        P.op("dve", lambda e, mi_=mi_: e.tensor_scalar(
            masks[:, mi_, :, :], tab[:, 512 + 128 * mi_:640 + 128 * mi_].unsqueeze(1).to_broadcast([128, 4, 128]),
            -1.0, 30000.0, ALU.add, ALU.mult), reads=[ktb], writes=[kmask])
    P.op("act", lambda e: e.activation(esink, tab[:, 896:928], AF.Exp), reads=[ktb], writes=[kmask])
    P.op("act", lambda e: e.activation(sq, halo, AF.Square), reads=[khalo], writes=[ksq])
    emit_rstd(P, C, sq, KC, 128, rstd[:, 0:128], 7, D_MODEL, ksq, krstd)
    for c in range(KC):
        P.op("dve", lambda e, c=c: e.scalar_tensor_tensor(
            u[:, c, 0:128], halo[:, c, :], g_pre[:, c:c + 1], rstd[:, 0:128], ALU.mult, ALU.mult),
            reads=[khalo, krstd, ("cst",)], writes=[ku])
    for t0 in range(0, NT, T):
        emit_prenorm(P, C, hT[:, :, t0:t0 + T], g_pre, u[:, :, 128 + t0:128 + t0 + T], sq2, rstd, T,
                     kkt, krstd, ku)
    banks = Banks([0, 1, 2, 3])
    base_ring, base_keys = list(C.ring), list(C.ring_keys)
    proj_slots = [Pm.rearrange("p a b c -> p (a b c)")[:, 0:2048],
                  rden.rearrange("p a b -> p (a b)").bitcast(BF16)[:, 0:2048]]
    C.ring = base_ring + proj_slots
    C.ring_keys = base_keys + [C.key("w") for _ in proj_slots]
    rot_linear_perm(P, C, D["wk"], KC, tiles_x,
                    lambda k, ti: u[:, k, tiles_x[ti][0]:tiles_x[ti][0] + tiles_x[ti][1]],
                    [ku], D["bk"], cosT, sinT, t1, kt1, qb, kqb, tB, ktB, lambda mc: KT[:, mc, :], kkt, banks, ktab,
                    D["perm"])
    def cons_v(ui, ti, ps, b):
        P.op("dve", lambda e: e.tensor_tensor(V[:, ti, ui * 128:(ui + 1) * 128], ps,
                                              tab[:, ui * 128:(ui + 1) * 128], ALU.add),
             reads=[("ps", b), ktb], writes=[kv])
    linear_tm(P, C, D["wv"], KC, 9, lambda k, ti: u[:, k, ti * 128:(ti + 1) * 128], [ku], cons_v, banks)
    P.barrier()
    rot_linear_perm(P, C, D["wq"], KC, tiles_o,
                    lambda k, ti: u[:, k, 128 + tiles_o[ti][0]:128 + tiles_o[ti][0] + 512],
                    [ku], D["bq"], cosT, sinT, t1, kt1, qb, kqb, tB, ktB, lambda mc: QA[:, mc, :], kqa, banks, ktab,
                    D["perm"], col0=128)
    C.ring, C.ring_keys, C.ring_i = base_ring, base_keys, 0
    P.barrier()
    def stage1(kh, n, par):
        r0 = par * 64
        bS = [4 * par, 4 * par + 1]
        for wi, kt in enumerate((n, n + 1)):
            mi = (2 if n == 0 else 1) if wi == 0 else 0
            mm(P, C.ps[bS[wi]][:, 0:512], KT[r0:r0 + 64, kh, kt * 128:(kt + 1) * 128],
               QA[r0:r0 + 64, 4 * kh:4 * kh + 4, n * 128:(n + 1) * 128], True, False,
               reads=[kkt, kqa, ("qa", kh, n, par)], writes=[("ps", bS[wi])], inc=False)
            mm(P, C.ps[bS[wi]][:, 0:512], C.ident[:], masks[:, mi, :, :], False, True,
               reads=[("ident",), kmask], writes=[("ps", bS[wi])])
            P.op("act", lambda e, wi=wi, par=par, b=bS[wi]: e.activation(
                Pm[:, par, wi, :], C.ps[b][:, 0:512], AF.Exp, scale=0.125),
                reads=[("ps", bS[wi])], writes=[kpm[par][wi]])

    def stage2(kh, n, par):
        r0 = par * 64
        bO, bD = 4 * par + 2, 4 * par + 3
        for wi, kt in enumerate((n, n + 1)):
            mm(P, C.ps[bO][:, 0:512], V[:, kt, kh * 128:(kh + 1) * 128], Pm[:, par, wi, :],
               wi == 0, wi == 1, reads=[kv, kpm[par][wi]], writes=[("ps", bO)])
        for wi in range(2):
            mm(P, C.ps[bD][:, 0:512], C.ones[:], Pm[:, par, wi, :],
               wi == 0, wi == 1, reads=[("ones",), kpm[par][wi]], writes=[("ps", bD)])
        for gi in range(4):
            hq = kh * 8 + 2 * gi + par
            P.op("act", lambda e, gi=gi, hq=hq: e.activation(
                rden[:, par, gi * 128:(gi + 1) * 128], C.ps[bD][:, gi * 128:(gi + 1) * 128],
                AF.Identity, bias=esink[:, hq:hq + 1]),
                reads=[("ps", bD), kmask], writes=[krd[par]])
        P.op("dve", lambda e: e.reciprocal(rden[:, par, :], rden[:, par, :]),
             reads=[krd[par]], writes=[krd[par]])
        P.op("dve", lambda e: e.tensor_tensor(
            QA[r0:r0 + 64, 4 * kh:4 * kh + 4, n * 128:(n + 1) * 128],
            C.ps[bO][r0:r0 + 64, 0:512].rearrange("p (a b) -> p a b", a=4),
            rden[r0:r0 + 64, par, :].rearrange("p (a b) -> p a b", a=4), ALU.mult),
            reads=[("ps", bO), krd[par]], writes=[("qa", kh, n, par)])

    its = [(kh, n, par) for kh in range(4) for n in range(8) for par in range(2)]
    stage1(*its[0])
    for i in range(len(its)):
        if i + 1 < len(its):
            stage1(*its[i + 1])
        stage2(*its[i])
    P.barrier()
    A2 = A.off
    A.reset(NT * KC * 4)
    f = A.alloc([128, KC, T], F32)
    A.reset(A2)
    kf = C.key("f")
    sqo = sq2
    extra_slots = [cosT.bitcast(BF16)[:, 0:2048], sinT.bitcast(BF16)[:, 0:2048],
                   t1.rearrange("p a b -> p (a b)").bitcast(BF16)[:, 0:2048],
                   tB.rearrange("p a b -> p (a b)").bitcast(BF16)[:, 0:2048]]
    C.ring = list(C.ring) + extra_slots
    C.ring_keys = list(C.ring_keys) + [C.key("w") for _ in extra_slots]
    for t0 in range(0, NT, T):
        def cons_o(mc, ti, ps, b):
            P.op("act", lambda e: e.activation(f[:, mc, :], ps, AF.Identity, bias=D["bo"][:, mc:mc + 1]),
                 reads=[("ps", b), ("cst",)], writes=[kf])
        linear_fm(P, C, D["wo"], KC, [(t0, T)], lambda k, ti, t0=t0: QA[:, k, t0:t0 + T], [kqa], cons_o, banks)
        emit_postnorm(P, C, f, hT[:, :, t0:t0 + T], g_post, rstd, sqo, T, kf, kkt, krstd)
    P.barrier()


RET_H = 8
GAMMAS = [1.0 - 2.0 ** (-5.0 - h) for h in range(RET_H)]


def ret_tables(P, C, D, cosT, sinT, ang, tmp, posi, ktab):
    kpos, kang, ktmp = C.key("posi"), C.key("ang"), C.key("tmp")
    P.dma("sp", posi, D["posr"], writes=[kpos])
    P.op("dve", lambda e: e.tensor_copy(ang, posi), reads=[kpos], writes=[kang])
    P.op("dve", lambda e: e.tensor_scalar(ang, ang, D["invf_ret"], None, ALU.mult),
         reads=[kang, ("cst",)], writes=[kang])
    emit_sincos(P, C, ang, tmp, sinT, cosT, NT, kang, ktmp, ktab, ktab)


def ret_rotary(P, C, scr, kscr, cosT, sinT, ktab, scale, out, kout):
    x1, x2, t1, t2 = scr[:, 0, :], scr[:, 1, :], scr[:, 2, :], scr[:, 3, :]
    for (a, b, o, op) in ((x1, x2, out[:, 0, :], ALU.subtract), (x2, x1, out[:, 1, :], ALU.add)):
        P.op("dve", lambda e, a=a: e.scalar_tensor_tensor(t1, a, scale, cosT, ALU.mult, ALU.mult),
             reads=[kscr, ktab], writes=[kscr])
        P.op("dve", lambda e, b=b: e.scalar_tensor_tensor(t2, b, scale, sinT, ALU.mult, ALU.mult),
             reads=[kscr, ktab], writes=[kscr])
        P.op("dve", lambda e, o=o, op=op: e.tensor_tensor(o, t1, t2, op), reads=[kscr], writes=[kout])


def ret_head_kv(P, C, D, h, u, ku, scr, kscr, cosT, sinT, ktab, KT, kkt, Ktm, kktm, V, kv, dcol, banks,
                load=None):
    tiles = [(0, 512), (512, 512)]
    W = D["ret_w"]
    if load is not None:
        load(h, KT, kkt, V, kv)
    else:
        def cons_raw(mc, ti, ps, b):
            t0, n = tiles[ti]
            P.op("act", lambda e: e.activation(scr[:, mc, t0:t0 + n], ps, AF.Copy),
                 reads=[("ps", b)], writes=[kscr])
        linear_fm(P, C, [W[h, 1]], KC, tiles, lambda k, ti: u[:, k, tiles[ti][0]:tiles[ti][0] + 512], [ku],
                  cons_raw, banks)
        ret_rotary(P, C, scr, kscr, cosT, sinT, ktab, 256.0 ** -0.5, KT, kkt)
    if load is None:
        def cons_v(ui, ti, ps, b):
            P.op("act", lambda e: e.activation(V[:, ti, ui * 256:(ui + 1) * 256], ps, AF.Copy),
                 reads=[("ps", b)], writes=[kv])
        linear_tm(P, C, [W[h, 2], W[h, 3]], KC, 8, lambda k, ti: u[:, k, ti * 128:(ti + 1) * 128], [ku],
                  cons_v, banks)
    for ti in range(8):
        b = banks.next()
        for a in range(2):
            mm(P, C.ps[b][:, a * 128:(a + 1) * 128], KT[:, a, ti * 128:(ti + 1) * 128], C.ident[:],
               True, True, reads=[kkt, ("ident",)], writes=[("ps", b)], inc=(a == 1))
        P.op("dve", lambda e, ti=ti, b=b: e.tensor_scalar(
            Ktm[:, ti, :], C.ps[b][:, 0:256], dcol(ti), None, ALU.mult),
            reads=[("ps", b), ("cst",)], writes=[kktm])


def phase_retA(P, C, hT, g_pre, D):
    A = C.A
    T = 512
    A.reset(NT * KC * 4)
    u = A.alloc([128, KC, NT], BF16)
    cosT = A.alloc([128, NT], F32)
    sinT = A.alloc([128, NT], F32)
    scr = A.alloc([128, 4, NT], F32)
    KTs = [A.alloc([128, 2, NT], BF16) for _ in range(2)]
    Ktm = A.alloc([128, 8, 256], BF16)
    Vs = [A.alloc([128, 8, 512], BF16) for _ in range(2)]
    Sst = A.alloc([128, 2, 512], F32)
    rstd = A.alloc([128, T], F32)
    sq = A.alloc([128, KC, T], BF16)
    posi = A.alloc([128, NT], I32)
    C.make_ring(2)
    ku, ktab, kscr, kktm, kS, krstd, ksq = (C.key(n) for n in
        ("u", "tab", "scr", "ktm", "S", "rstd", "sq"))
    kkts = [C.key("kt0"), C.key("kt1")]
    kvs = [C.key("v0"), C.key("v1")]
    ret_tables(P, C, D, cosT, sinT, scr[:, 0, :], scr[:, 1, :], posi, ktab)
    for t0 in range(0, NT, T):
        emit_prenorm(P, C, hT[:, :, t0:t0 + T], g_pre, u[:, :, t0:t0 + T], sq, rstd, T, ksq, krstd, ku)
    banks = Banks([0, 1, 2, 3])
    pending_coll = None
    for h in range(RET_H):
        KT, kkt, V, kv = KTs[h % 2], kkts[h % 2], Vs[h % 2], kvs[h % 2]
        ret_head_kv(P, C, D, h, u, ku, scr, kscr, cosT, sinT, ktab, KT, kkt, Ktm, kktm, V, kv,
                    lambda ti, h=h: D["dloc"][:, ti, h:h + 1], banks)
        for a in range(2):
            b = 4 + a
            for ti in range(8):
                mm(P, C.ps[b][:, 0:512], Ktm[:, ti, a * 128:(a + 1) * 128], V[:, ti, :], ti == 0, ti == 7,
                   reads=[kktm, kv], writes=[("ps", b)])
            P.op("act", lambda e, a=a, b=b: e.activation(Sst[:, a, :], C.ps[b][:, 0:512], AF.Copy),
                 reads=[("ps", b)], writes=[kS])
        D["sloc_store"](h, Sst, kS)
        if D.get("kv_store") is not None:
            D["kv_store"](h, KT, kkt, V, kv)
        if pending_coll is not None:
            pending_coll()
        pending_coll = (lambda h=h: D["sloc_coll"](h)) if D.get("sloc_coll") is not None else None
    if pending_coll is not None:
        pending_coll()
    D["retA_state"] = {"u": u, "ku": ku, "cosT": cosT, "sinT": sinT, "ktab": ktab}
    P.barrier(skip_pool=bool(D.get("kv_store")))


def phase_retB(P, C, hT, g_pre, g_post, D, load_h):
    A = C.A
    T = 512
    GOFF = ARENA_BYTES - 2 * KC * NT * 2
    A.reset(GOFF)
    gated_flat = A.alloc([128, 2 * KC * NT], BF16)
    gated = gated_flat.rearrange("p (a b) -> p a b", a=2 * KC)
    hin = gated_flat.bitcast(F32).rearrange("p (a b) -> p a b", a=KC)
    reuse = D.get("reuse")
    if reuse is not None:
        A.reset(NT * KC * 4)
    else:
        A.reset(0)
    u = A.alloc([128, KC, NT], BF16)
    cosT = A.alloc([128, NT], F32)
    sinT = A.alloc([128, NT], F32)
    scr_flat = A.alloc([128, 4 * NT], F32)
    scr = scr_flat.rearrange("p (a b) -> p a b", a=4)
    sq_tmp = scr_flat.bitcast(BF16)[:, 0:KC * T].rearrange("p (a b) -> p a b", a=KC)
    ob = scr_flat.bitcast(BF16)[:, 0:4 * T].rearrange("p (a b) -> p a b", a=4)
    osq = scr_flat.bitcast(BF16)[:, 4 * T:8 * T].rearrange("p (a b) -> p a b", a=4)
    posi_t = scr_flat.bitcast(I32)[:, 2 * NT:3 * NT]
    C.make_ring(2)
    if reuse is not None:
        assert A.off <= GOFF, (A.off, GOFF)
        A.reset(0)
    QT = A.alloc([128, 2, NT], BF16)
    KT = A.alloc([128, 2, NT], BF16)
    Ktm = A.alloc([128, 8, 256], BF16)
    V = A.alloc([128, 8, 512], BF16)
    G = A.alloc([128, 4, NT], BF16)
    oh = A.alloc([128, 4, T], F32)
    S = A.alloc([128, 2, 512], F32)
    Sbf2 = A.alloc([128, 2, 2, 512], BF16)
    dtab = A.alloc([128, 2, 128], F32)
    Sd = A.alloc([128, 2, 128], BF16)
    Qc = A.alloc([128, 2, 2, 128], BF16)
    gnt = A.alloc([128, 4, T], F32)
    kgs = kscr_gn = None
    if reuse is not None:
        ob = A.alloc([128, 4, T], BF16)
        osq = A.alloc([128, 4, T], BF16)
        rstd = None
        assert A.off <= NT * KC * 4, A.off
    else:
        rstd = A.alloc([128, T], F32)
        assert A.off <= GOFF, (A.off, GOFF)
    ku, ktab, kscr, kqt, kkt, kktm, kv, kg, koh, kS, kSb, kdt, kgn, krstd, kgated = (C.key(n) for n in
        ("u", "tab", "scr", "qt", "kt", "ktm", "v", "g", "oh", "S", "Sb", "dt", "gn", "rstd", "gated"))
    ksd = [C.key("sd"), C.key("sd")]
    kqc = [C.key("qc"), C.key("qc")]
    kSbs = [C.key("Sb0"), C.key("Sb1")]
    kob = C.key("ob") if reuse is not None else kscr
    if reuse is None:
        load_h(hin)
        ret_tables(P, C, D, cosT, sinT, scr[:, 0, :], scr[:, 1, :], posi_t, ktab)
        P.barrier()
        for t0 in range(0, NT, T):
            emit_prenorm(P, C, hin[:, :, t0:t0 + T], g_pre, u[:, :, t0:t0 + T], sq_tmp, rstd, T, kscr, krstd, ku)
        P.barrier()
    banks = Banks([0, 1, 2, 3])
    tiles = [(0, 512), (512, 512)]
    W = D["ret_w"]
    stage = scr[:, 2:4, 0:512]
    for h in range(RET_H):
        gam = GAMMAS[h]
        for c in range(4):
            P.dma("sp", stage, D["sall_ap"](c, h), reads=D["sall_keys"](h), writes=[kscr])
            if c == 0:
                P.op("dve", lambda e, c=c, h=h: e.tensor_scalar(
                    S, stage, D["coef"][:, c * 8 + h:c * 8 + h + 1], None, ALU.mult),
                    reads=[kscr, ("cst",)], writes=[kS])
            else:
                P.op("dve", lambda e, c=c, h=h: e.scalar_tensor_tensor(
                    S, stage, D["coef"][:, c * 8 + h:c * 8 + h + 1], S, ALU.mult, ALU.add),
                    reads=[kscr, ("cst",), kS], writes=[kS])
        P.op("act", lambda e: e.activation(Sbf2[:, 0, :, :], S, AF.Copy), reads=[kS], writes=[kSbs[0]])
        P.dma("sp", dtab, D["dtab"][h], writes=[kdt])
        def cons_raw(mc, ti, ps, b):
            t0, n = tiles[ti]
            P.op("act", lambda e: e.activation(scr[:, mc, t0:t0 + n], ps, AF.Copy),
                 reads=[("ps", b)], writes=[kscr])
        linear_fm(P, C, [W[h, 0]], KC, tiles, lambda k, ti: u[:, k, tiles[ti][0]:tiles[ti][0] + 512], [ku],
                  cons_raw, banks)
        ret_rotary(P, C, scr, kscr, cosT, sinT, ktab, 1.0, QT, kqt)
        ret_head_kv(P, C, D, h, u, ku, scr, kscr, cosT, sinT, ktab, KT, kkt, Ktm, kktm, V, kv,
                    lambda ti, h=h: D["sdcol"][:, h:h + 1], banks, load=D.get("kv_load"))

        def cons_g(mc, ti, ps, b):
            t0, n = tiles[ti]
            P.op("act", lambda e: e.activation(G[:, mc, t0:t0 + n], ps, AF.Silu),
                 reads=[("ps", b)], writes=[kg])
        linear_fm(P, C, [W[h, 4], W[h, 5]], KC, tiles, lambda k, ti: u[:, k, tiles[ti][0]:tiles[ti][0] + 512],
                  [ku], cons_g, banks)
        SB = (4, 3)

        def scores(n):
            cs = slice(n * 128, (n + 1) * 128)
            for a in range(2):
                mm(P, C.ps[SB[n % 2]][:, 0:128], KT[:, a, cs], QT[:, a, cs], a == 0, a == 1,
                   reads=[kkt, kqt], writes=[("ps", SB[n % 2])])
        scores(0)
        for n in range(8):
            cs = slice(n * 128, (n + 1) * 128)
            pp = n % 2
            Sbf = Sbf2[:, pp, :, :]
            P.op("dve", lambda e, pp=pp, n=n: e.tensor_tensor(Sd[:, pp, :], C.ps[SB[n % 2]][:, 0:128],
                                                              dtab[:, 0, :], ALU.mult),
                 reads=[("ps", SB[n % 2]), kdt], writes=[ksd[pp]])
            P.op("dve", lambda e, pp=pp, cs=cs: e.tensor_tensor(
                Qc[:, pp, :, :], QT[:, :, cs], dtab[:, 1, :].unsqueeze(1).to_broadcast([128, 2, 128]), ALU.mult),
                reads=[kqt, kdt], writes=[kqc[pp]])
            if n < 7:
                for a in range(2):
                    mm(P, C.ps[6 + a][:, 0:512], Ktm[:, n, a * 128:(a + 1) * 128], V[:, n, :], True, True,
                       reads=[kktm, kv], writes=[("ps", 6 + a)])
                scores(n + 1)
            for m in range(4):
                ms = slice(m * 128, (m + 1) * 128)
                mm(P, C.ps[5][:, ms], V[:, n, ms], Sd[:, pp, :], True, False,
                   reads=[kv, ksd[pp]], writes=[("ps", 5)], inc=False)
                for a in range(2):
                    mm(P, C.ps[5][:, ms], Sbf[:, a, ms], Qc[:, pp, a, :], False, a == 1,
                       reads=[kSbs[pp], kqc[pp]], writes=[("ps", 5)], inc=(a == 1 and m == 3))
            if n < 7:
                for a in range(2):
                    P.op("dve", lambda e, a=a, gam=gam: e.scalar_tensor_tensor(
                        S[:, a, :], S[:, a, :], gam ** 128, C.ps[6 + a][:, 0:512], ALU.mult, ALU.add),
                        reads=[kS, ("ps", 6 + a)], writes=[kS])
                P.op("act", lambda e, pp=pp: e.activation(Sbf2[:, 1 - pp, :, :], S, AF.Copy),
                     reads=[kS], writes=[kSbs[1 - pp]])
            half = n // 4
            P.op("act", lambda e, n=n: e.activation(
                oh[:, :, (n % 4) * 128:(n % 4 + 1) * 128],
                C.ps[5][:, 0:512].rearrange("p (a b) -> p a b", a=4), AF.Copy),
                reads=[("ps", 5)], writes=[koh])
            if n % 4 == 3:
                t0 = half * T
                P.op("act", lambda e: e.activation(ob, oh, AF.Copy), reads=[koh], writes=[kob])
                P.op("act", lambda e: e.activation(osq, oh, AF.Square), reads=[koh], writes=[kob])
                for m in range(4):
                    mm(P, C.ps[6][:, 0:T], C.ones[:], ob[:, m, :], m == 0, m == 3,
                       reads=[kob, ("ones",)], writes=[("ps", 6)])
                for m in range(4):
                    mm(P, C.ps[7][:, 0:T], C.ones[:], osq[:, m, :], m == 0, m == 3,
                       reads=[kob, ("ones",)], writes=[("ps", 7)])
                mean, var, tt_ = gnt[:, 0, :], gnt[:, 1, :], gnt[:, 2, :]
                P.op("dve", lambda e: e.tensor_scalar(mean, C.ps[6][:, 0:T], 1.0 / 512, None, ALU.mult),
                     reads=[("ps", 6)], writes=[kgn])
                P.op("dve", lambda e: e.tensor_tensor(tt_, mean, mean, ALU.mult), reads=[kgn], writes=[kgn])
                P.op("dve", lambda e: e.scalar_tensor_tensor(var, C.ps[7][:, 0:T], 1.0 / 512, tt_,
                                                             ALU.mult, ALU.subtract),
                     reads=[("ps", 7), kgn], writes=[kgn])
                P.op("act", lambda e: e.activation(var, var, AF.Sqrt, bias=C.eps[:, 0:1]),
                     reads=[kgn, ("ones",)], writes=[kgn])
                P.op("dve", lambda e: e.reciprocal(var, var), reads=[kgn], writes=[kgn])
                for m in range(4):
                    c = h * 4 + m
                    P.op("dve", lambda e, m=m: e.tensor_tensor(tt_, oh[:, m, :], mean, ALU.subtract),
                         reads=[koh, kgn], writes=[kgn])
                    P.op("dve", lambda e: e.tensor_tensor(tt_, tt_, var, ALU.mult), reads=[kgn], writes=[kgn])
                    P.op("dve", lambda e, c=c: e.tensor_scalar(
                        tt_, tt_, D["gn_g"][:, c:c + 1], D["gn_b"][:, c:c + 1], ALU.mult, ALU.add),
                        reads=[kgn, ("cst",)], writes=[kgn])
                    P.op("dve", lambda e, m=m, c=c, t0=t0: e.tensor_tensor(
                        gated[:, c, t0:t0 + T], tt_, G[:, m, t0:t0 + T], ALU.mult),
                        reads=[kgn, kg], writes=[kgated])
    P.barrier()
    A.reset(0)
    hT2 = A.alloc([128, KC, NT], F32)
    f = A.alloc([128, KC, T], F32)
    sqo = A.alloc([128, KC, T], BF16)
    rstd2 = A.alloc([128, T], F32)
    C.make_ring(2)
    assert A.off <= GOFF, (A.off, GOFF)
    load_h(hT2)
    kf, ksqo, kr2 = C.key("f"), C.key("sqo"), C.key("r2")
    for t0 in range(0, NT, T):
        def cons_o(mc, ti, ps, b):
            P.op("act", lambda e: e.activation(f[:, mc, :], ps, AF.Copy), reads=[("ps", b)], writes=[kf])
        linear_fm(P, C, D["ret_wo"], 2 * KC, [(t0, T)], lambda k, ti, t0=t0: gated[:, k, t0:t0 + T], [kgated],
                  cons_o, banks)
        emit_postnorm(P, C, f, hT2[:, :, t0:t0 + T], g_post, rstd2, sqo, T, kf, ksqo, kr2)
    P.barrier()
    return hT2


def phase_ffn_seq(P, C, hT, ffns, hook=None, first=False):
    A = C.A
    T = 512
    if not first:
        P.barrier()
    A.reset(NT * KC * 4)
    u = A.alloc([128, KC, T], BF16)
    hid = A.alloc([128, FC, T], BF16)
    f = A.alloc([128, KC, T], F32)
    sg = A.alloc([128, 2, T], F32)
    rstdA = A.alloc([128, T], F32)
    rstdB = A.alloc([128, T], F32)
    sqA = A.alloc([128, 2, 2, T], BF16)
    sqB = A.alloc([128, 2, 2, T], BF16)
    C.make_ring(3)
    ku, khid, krA, krB = C.key("u"), C.key("hid"), C.key("rA"), C.key("rB")
    kf = [C.key("f%d" % c) for c in range(KC)]
    ksg = [C.key("sg0"), C.key("sg1")]
    ksqA = [C.key("sqA0"), C.key("sqA1")]
    ksqB = [C.key("sqB0"), C.key("sqB1")]
    items = [(fi, ti) for fi in range(len(ffns)) for ti in (1, 0)]
    K_ = len(items)
    NB = 7

    def hkey(ti):
        return ("h", ti)

    def hs_of(k):
        return hT[:, :, items[k][1] * T:(items[k][1] + 1) * T]

    def stats_pre_ops(k):
        ops = []
        hs = hs_of(k)
        hk = hkey(items[k][1])
        for r in range(KC // 2):
            pp = r % 2

            def sq_op(r=r, pp=pp):
                P.op("act", lambda e: e.activation(sqA[:, pp, :, :], hs[:, 2 * r:2 * r + 2, :], AF.Square),
                     reads=[hk], writes=[ksqA[pp]])

            def mm_op(r=r, pp=pp):
                for i in range(2):
                    c = 2 * r + i
                    mm(P, C.ps[NB][:, 0:T], C.ones[:], sqA[:, pp, i, :], c == 0, c == KC - 1,
                       reads=[ksqA[pp], ("ones",)], writes=[("ps", NB)], inc=(i == 1))
            ops += [sq_op, mm_op]

        def fin():
            P.op("act", lambda e: e.activation(rstdA, C.ps[NB][:, 0:T], AF.Sqrt, bias=C.eps[:, 0:1],
                                               scale=1.0 / D_MODEL),
                 reads=[("ps", NB), ("ones",)], writes=[krA])
            P.op("dve", lambda e: e.reciprocal(rstdA, rstdA), reads=[krA], writes=[krA])
        ops.append(fin)
        return ops

    def write_u(k):
        hs = hs_of(k)
        g1 = ffns[items[k][0]][0]
        for c in range(KC):
            P.op("dve", lambda e, c=c: e.scalar_tensor_tensor(
                u[:, c, :], hs[:, c, :], g1[:, c:c + 1], rstdA, ALU.mult, ALU.mult),
                reads=[hkey(items[k][1]), krA, ("cst",)], writes=[ku])

    def post_apply_ops(k):
        hs = hs_of(k)
        hk = hkey(items[k][1])
        g2h = ffns[items[k][0]][1]
        ops = []

        def fin():
            P.op("act", lambda e: e.activation(rstdB, C.ps[NB][:, 0:T], AF.Sqrt, bias=C.eps[:, 0:1],
                                               scale=1.0 / D_MODEL),
                 reads=[("ps", NB), ("ones",)], writes=[krB])
            P.op("dve", lambda e: e.reciprocal(rstdB, rstdB), reads=[krB], writes=[krB])
        ops.append(fin)
        for c in range(KC):
            def ap(c=c):
                P.op("dve", lambda e: e.tensor_tensor(f[:, c, :], f[:, c, :], rstdB, ALU.mult),
                     reads=[kf[c], krB], writes=[kf[c]])
                P.op("dve", lambda e: e.scalar_tensor_tensor(
                    hs[:, c, :], f[:, c, :], g2h[:, c:c + 1], hs[:, c, :], ALU.mult, ALU.add),
                    reads=[kf[c], ("cst",), hk], writes=[hk])
            ops.append(ap)
        return ops

    hs0, hk0 = hs_of(0), hkey(items[0][1])
    sqbig = hid[:, 0:KC, :]
    P.op("act", lambda e: e.activation(sqbig, hs0, AF.Square), reads=[hk0], writes=[khid])
    for c in range(KC):
        mm(P, C.ps[NB][:, 0:T], C.ones[:], sqbig[:, c, :], c == 0, c == KC - 1,
           reads=[khid, ("ones",)], writes=[("ps", NB)])
    P.op("act", lambda e: e.activation(rstdA, C.ps[NB][:, 0:T], AF.Sqrt, bias=C.eps[:, 0:1],
                                       scale=1.0 / D_MODEL),
         reads=[("ps", NB), ("ones",)], writes=[krA])
    P.op("dve", lambda e: e.reciprocal(rstdA, rstdA), reads=[krA], writes=[krA])
    write_u(0)
    for k in range(K_):
        fi, ti = items[k]
        g1, g2h, win_d, wout_d = ffns[fi]
        extras = []
        if k >= 1:
            extras += post_apply_ops(k - 1)
            if hook is not None and k == K_ - 1:
                extras.append(lambda: hook(hkey(items[k - 1][1])))
        if k + 1 < K_:
            extras += stats_pre_ops(k + 1)
        ei = 0
        for j in range(FC):
            wk, slot = C.next_slot()
            P.dma("pool", slot[:, 0:KC * 256], win_d[j].rearrange("p k c -> p (k c)"), writes=[wk])
            sv = slot[:, 0:KC * 256].rearrange("p (k c) -> p k c", k=KC)
            pair = j % 2
            bg, bu = 2 * pair, 2 * pair + 1
            for kk in range(KC):
                mm(P, C.ps[bg][:, 0:T], sv[:, kk, 0:128], u[:, kk, :], kk == 0, kk == KC - 1,
                   reads=[wk, ku], writes=[("ps", bg)])
            for kk in range(KC):
                mm(P, C.ps[bu][:, 0:T], sv[:, kk, 128:256], u[:, kk, :], kk == 0, kk == KC - 1,
                   reads=[wk, ku], writes=[("ps", bu)])
            P.op("act", lambda e, bg=bg, pair=pair: e.activation(sg[:, pair, :], C.ps[bg][:, 0:T], AF.Silu),
                 reads=[("ps", bg)], writes=[ksg[pair]])
            P.op("dve", lambda e, bu=bu, pair=pair, j=j: e.tensor_tensor(
                hid[:, j, :], sg[:, pair, :], C.ps[bu][:, 0:T], ALU.mult),
                reads=[ksg[pair], ("ps", bu)], writes=[khid])
            for _ in range(2):
                if ei < len(extras) and j >= 1:
                    extras[ei]()
                    ei += 1
        while ei < len(extras):
            extras[ei]()
            ei += 1
        if k + 1 < K_:
            write_u(k + 1)
        unit = 16
        pend = None
        for dg in range(KC // 2):
            par = dg % 2
            banks = (4, 5) if par == 0 else (6, 3)
            k0 = 0
            first = True
            while k0 < FC:
                nk = min(unit, FC - k0)
                wk, slot = C.next_slot()
                P.dma("pool", slot[:, 0:nk * 256],
                      wout_d[dg, :, k0:k0 + nk, :].rearrange("p k c -> p (k c)"), writes=[wk])
                sv = slot[:, 0:nk * 256].rearrange("p (k c) -> p k c", k=nk)
                for m in range(2):
                    for kk in range(nk):
                        kc_ = k0 + kk
                        mm(P, C.ps[banks[m]][:, 0:T], sv[:, kk, m * 128:(m + 1) * 128], hid[:, kc_, :],
                           kc_ == 0, kc_ == FC - 1, reads=[wk, khid], writes=[("ps", banks[m])],
                           inc=(kk == nk - 1))
                k0 += nk
                if first and pend is not None:
                    pend()
                    pend = None
                first = False
            pp = dg % 2
            for m in range(2):
                c = dg * 2 + m
                P.op("act", lambda e, c=c, b=banks[m]: e.activation(f[:, c, :], C.ps[b][:, 0:T], AF.Copy),
                     reads=[("ps", banks[m])], writes=[kf[c]])
                P.op("act", lambda e, m=m, pp=pp, b=banks[m]: e.activation(sqB[:, pp, m, :], C.ps[b][:, 0:T],
                                                                           AF.Square),
                     reads=[("ps", banks[m])], writes=[ksqB[pp]])

            def ones_mm(dg=dg, pp=pp):
                for m in range(2):
                    c = dg * 2 + m
                    mm(P, C.ps[NB][:, 0:T], C.ones[:], sqB[:, pp, m, :], c == 0, c == KC - 1,
                       reads=[ksqB[pp], ("ones",)], writes=[("ps", NB)], inc=(m == 1))
            pend = ones_mm
        pend()
    for op_ in post_apply_ops(K_ - 1):
        op_()
    P.barrier()


def linear_fm_pieces(P, C, wu, kc, tiles, rhs, rkeys, consume, banks, mc0=0):
    UC = wu.shape[2]
    state = {}

    def load():
        wk, slot = C.next_slot()
        P.dma("pool", slot[:, 0:kc * UC], wu.rearrange("p k c -> p (k c)"), writes=[wk])
        state["wk"] = wk
        state["sv"] = slot[:, 0:kc * UC].rearrange("p (k c) -> p k c", k=kc)
    pieces = []
    first = True
    for m in range(UC // 128):
        for ti, (t0, n) in enumerate(tiles):
            def piece(m=m, ti=ti, n=n, first=first):
                if first:
                    load()
                b = banks.next()
                for k in range(kc):
                    mm(P, C.ps[b][:, 0:n], state["sv"][:, k, m * 128:(m + 1) * 128], rhs(k, ti),
                       k == 0, k == kc - 1, reads=[state["wk"]] + rkeys, writes=[("ps", b)])
                consume(mc0 + m, ti, C.ps[b][:, 0:n], b)
            pieces.append(piece)
            first = False
    return pieces


def phase_retB2(P, C, hT, g_post, D, load_h):
    A = C.A
    T = 512
    A.reset(NT * KC * 4)
    u = A.alloc([128, KC, NT], BF16)
    cosT = A.alloc([128, NT], F32)
    sinT = A.alloc([128, NT], F32)
    scr = A.alloc([128, 4, NT], F32)
    C.make_ring(2)
    GOFF = A.off

    def alloc_set():
        return {"QT": A.alloc([128, 2, NT], BF16), "KT": A.alloc([128, 2, NT], BF16),
                "Ktm": A.alloc([128, 8, 256], BF16), "V": A.alloc([128, 8, 512], BF16),
                "G": A.alloc([128, 4, NT], BF16),
                "kqt": C.key("qt"), "kkt": C.key("kt"), "kktm": C.key("ktm"), "kv": C.key("v"), "kg": C.key("g")}
    sets = [None, alloc_set()]
    gst = A.alloc([128, 2, 4, T], BF16)
    ob = A.alloc([128, 4, T], BF16)
    osq = A.alloc([128, 4, T], BF16)
    A.reset(0)
    sets[0] = alloc_set()
    oh = A.alloc([128, 4, T], F32)
    Ss = [A.alloc([128, 2, 512], F32) for _ in range(2)]
    Sbf2 = A.alloc([128, 2, 2, 512], BF16)
    dtabs = [A.alloc([128, 2, 128], F32) for _ in range(2)]
    Sd = A.alloc([128, 2, 128], BF16)
    Qc = A.alloc([128, 2, 2, 128], BF16)
    gnt = A.alloc([128, 3, T], F32)
    assert A.off <= NT * KC * 4, A.off
    ku, ktab, kscr, koh, kgn, kob = (C.key(n) for n in ("u", "tab", "scr", "oh", "gn", "ob"))
    kSs = [C.key("S0"), C.key("S1")]
    kdts = [C.key("dt0"), C.key("dt1")]
    ksd = [C.key("sd"), C.key("sd")]
    kqc = [C.key("qc"), C.key("qc")]
    kSbs = [C.key("Sb0"), C.key("Sb1")]
    kgst = [C.key("gst0"), C.key("gst1")]
    banks = Banks([0, 1, 2])
    tiles = [(0, 512), (512, 512)]
    W = D["ret_w"]
    stage = scr[:, 2:4, 0:512]
    gated_d = D["gated_d"]
    rhs_u = lambda k, ti: u[:, k, tiles[ti][0]:tiles[ti][0] + 512]

    def state_in(h):
        S, kS, dtab, kdt = Ss[h % 2], kSs[h % 2], dtabs[h % 2], kdts[h % 2]
        for c in range(4):
            P.dma("sp", stage, D["sall_ap"](c, h), reads=D["sall_keys"](h), writes=[kscr])
            if c == 0:
                P.op("dve", lambda e, c=c: e.tensor_scalar(
                    S, stage, D["coef"][:, c * 8 + h:c * 8 + h + 1], None, ALU.mult),
                    reads=[kscr, ("cst",)], writes=[kS])
            else:
                P.op("dve", lambda e, c=c: e.scalar_tensor_tensor(
                    S, stage, D["coef"][:, c * 8 + h:c * 8 + h + 1], S, ALU.mult, ALU.add),
                    reads=[kscr, ("cst",), kS], writes=[kS])
        P.dma("sp", dtab, D["dtab"][h], writes=[kdt])

    def proj_pieces(h):
        st = sets[h % 2]
        pieces = [lambda: state_in(h)]

        def cons_raw(mc, ti, ps, b):
            t0, n = tiles[ti]
            P.op("act", lambda e: e.activation(scr[:, mc, t0:t0 + n], ps, AF.Copy),
                 reads=[("ps", b)], writes=[kscr])
        pieces += linear_fm_pieces(P, C, W[h, 0], KC, tiles, rhs_u, [ku], cons_raw, banks)
        pieces.append(lambda: D["kv_load"](h, st["KT"], st["kkt"], st["V"], st["kv"]))

        def cons_g(mc, ti, ps, b):
            t0, n = tiles[ti]
            P.op("act", lambda e: e.activation(st["G"][:, mc, t0:t0 + n], ps, AF.Silu),
                 reads=[("ps", b)], writes=[st["kg"]])
        pieces += linear_fm_pieces(P, C, W[h, 4], KC, tiles, rhs_u, [ku], cons_g, banks, mc0=0)
        pieces += linear_fm_pieces(P, C, W[h, 5], KC, tiles, rhs_u, [ku], cons_g, banks, mc0=2)

        def ktm_piece(t_lo):
            for ti in range(t_lo, t_lo + 4):
                b = banks.next()
                for a in range(2):
                    mm(P, C.ps[b][:, a * 128:(a + 1) * 128], st["KT"][:, a, ti * 128:(ti + 1) * 128], C.ident[:],
                       True, True, reads=[st["kkt"], ("ident",)], writes=[("ps", b)], inc=(a == 1))
                P.op("dve", lambda e, ti=ti, b=b: e.tensor_scalar(
                    st["Ktm"][:, ti, :], C.ps[b][:, 0:256], D["sdcol"][:, h:h + 1], None, ALU.mult),
                    reads=[("ps", b), ("cst",)], writes=[st["kktm"]])
        pieces.append(lambda: ktm_piece(0))
        pieces.append(lambda: ktm_piece(4))
        pieces.append(lambda: ret_rotary(P, C, scr, kscr, cosT, sinT, ktab, 1.0, st["QT"], st["kqt"]))
        return pieces

    for pc in proj_pieces(0):
        pc()
    SB = (4, 3)
    for h in range(RET_H):
        st = sets[h % 2]
        QT, KT, Ktm, V, G = st["QT"], st["KT"], st["Ktm"], st["V"], st["G"]
        kqt, kkt, kktm, kv, kg = st["kqt"], st["kkt"], st["kktm"], st["kv"], st["kg"]
        gam = GAMMAS[h]
        nxt = proj_pieces(h + 1) if h + 1 < RET_H else []
        pi = [0]

        def emit_piece(cnt=1):
            for _ in range(cnt):
                if pi[0] < len(nxt):
                    nxt[pi[0]]()
                    pi[0] += 1
        S, kS, dtab, kdt = Ss[h % 2], kSs[h % 2], dtabs[h % 2], kdts[h % 2]
        P.op("act", lambda e, S=S: e.activation(Sbf2[:, 0, :, :], S, AF.Copy), reads=[kS], writes=[kSbs[0]])

        def scores(n):
            cs = slice(n * 128, (n + 1) * 128)
            for a in range(2):
                mm(P, C.ps[SB[n % 2]][:, 0:128], KT[:, a, cs], QT[:, a, cs], a == 0, a == 1,
                   reads=[kkt, kqt], writes=[("ps", SB[n % 2])])
        scores(0)
        for n in range(8):
            cs = slice(n * 128, (n + 1) * 128)
            pp = n % 2
            Sbf = Sbf2[:, pp, :, :]
            P.op("dve", lambda e, pp=pp, n=n, dtab=dtab: e.tensor_tensor(Sd[:, pp, :], C.ps[SB[n % 2]][:, 0:128],
                                                              dtab[:, 0, :], ALU.mult),
                 reads=[("ps", SB[n % 2]), kdt], writes=[ksd[pp]])
            P.op("dve", lambda e, pp=pp, cs=cs, QT=QT, dtab=dtab: e.tensor_tensor(
                Qc[:, pp, :, :], QT[:, :, cs], dtab[:, 1, :].unsqueeze(1).to_broadcast([128, 2, 128]), ALU.mult),
                reads=[kqt, kdt], writes=[kqc[pp]])
            if n < 7:
                for a in range(2):
                    mm(P, C.ps[6 + a][:, 0:512], Ktm[:, n, a * 128:(a + 1) * 128], V[:, n, :], True, True,
                       reads=[kktm, kv], writes=[("ps", 6 + a)])
                scores(n + 1)
            emit_piece(2 if n == 0 else 1)
            for m in range(4):
                ms = slice(m * 128, (m + 1) * 128)
                mm(P, C.ps[5][:, ms], V[:, n, ms], Sd[:, pp, :], True, False,
                   reads=[kv, ksd[pp]], writes=[("ps", 5)], inc=False)
                for a in range(2):
                    mm(P, C.ps[5][:, ms], Sbf[:, a, ms], Qc[:, pp, a, :], False, a == 1,
                       reads=[kSbs[pp], kqc[pp]], writes=[("ps", 5)], inc=(a == 1 and m == 3))
            if n < 7:
                for a in range(2):
                    P.op("dve", lambda e, a=a, gam=gam, S=S: e.scalar_tensor_tensor(
                        S[:, a, :], S[:, a, :], gam ** 128, C.ps[6 + a][:, 0:512], ALU.mult, ALU.add),
                        reads=[kS, ("ps", 6 + a)], writes=[kS])
                P.op("act", lambda e, pp=pp, S=S: e.activation(Sbf2[:, 1 - pp, :, :], S, AF.Copy),
                     reads=[kS], writes=[kSbs[1 - pp]])
            P.op("act", lambda e, n=n: e.activation(
                oh[:, :, (n % 4) * 128:(n % 4 + 1) * 128],
                C.ps[5][:, 0:512].rearrange("p (a b) -> p a b", a=4), AF.Copy),
                reads=[("ps", 5)], writes=[koh])
            if n % 4 == 3:
                half = n // 4
                t0 = half * T
                gi = (2 * h + half) % 2
                P.op("act", lambda e: e.activation(ob, oh, AF.Copy), reads=[koh], writes=[kob])
                P.op("act", lambda e: e.activation(osq, oh, AF.Square), reads=[koh], writes=[kob])
                for m in range(4):
                    mm(P, C.ps[6][:, 0:T], C.ones[:], ob[:, m, :], m == 0, m == 3,
                       reads=[kob, ("ones",)], writes=[("ps", 6)])
                for m in range(4):
                    mm(P, C.ps[7][:, 0:T], C.ones[:], osq[:, m, :], m == 0, m == 3,
                       reads=[kob, ("ones",)], writes=[("ps", 7)])
                mean, var, tt_ = gnt[:, 0, :], gnt[:, 1, :], gnt[:, 2, :]
                P.op("dve", lambda e: e.tensor_scalar(mean, C.ps[6][:, 0:T], 1.0 / 512, None, ALU.mult),
                     reads=[("ps", 6)], writes=[kgn])
                P.op("dve", lambda e: e.tensor_tensor(tt_, mean, mean, ALU.mult), reads=[kgn], writes=[kgn])
                P.op("dve", lambda e: e.scalar_tensor_tensor(var, C.ps[7][:, 0:T], 1.0 / 512, tt_,
                                                             ALU.mult, ALU.subtract),
                     reads=[("ps", 7), kgn], writes=[kgn])
                P.op("act", lambda e: e.activation(var, var, AF.Sqrt, bias=C.eps[:, 0:1]),
                     reads=[kgn, ("ones",)], writes=[kgn])
                P.op("dve", lambda e: e.reciprocal(var, var), reads=[kgn], writes=[kgn])
                for m in range(4):
                    c = h * 4 + m
                    P.op("dve", lambda e, m=m: e.tensor_tensor(tt_, oh[:, m, :], mean, ALU.subtract),
                         reads=[koh, kgn], writes=[kgn])
                    P.op("dve", lambda e: e.tensor_tensor(tt_, tt_, var, ALU.mult), reads=[kgn], writes=[kgn])
                    P.op("dve", lambda e, c=c: e.tensor_scalar(
                        tt_, tt_, D["gn_g"][:, c:c + 1], D["gn_b"][:, c:c + 1], ALU.mult, ALU.add),
                        reads=[kgn, ("cst",)], writes=[kgn])
                    P.op("dve", lambda e, m=m, t0=t0, gi=gi, G=G: e.tensor_tensor(
                        gst[:, gi, m, :], tt_, G[:, m, t0:t0 + T], ALU.mult),
                        reads=[kgn, kg], writes=[kgst[gi]])
                    emit_piece()
                P.dma("sp", gated_d[half][:, 4 * h:4 * h + 4, :], gst[:, gi, :, :],
                      reads=[kgst[gi]], writes=[("gd", half)])
        emit_piece(len(nxt))
    P.barrier()
    A.reset(0)
    hT2 = A.alloc([128, KC, NT], F32)
    f = A.alloc([128, KC, T], F32)
    sqo = A.alloc([128, KC, T], BF16)
    rstd2 = A.alloc([128, T], F32)
    C.make_ring(6)
    gt1 = A.alloc([128, 2 * KC, T], BF16)
    gts = [gt1, gt1]
    kf, ksqo, kr2 = C.key("f"), C.key("sqo"), C.key("r2")
    kg1 = C.key("gt")
    kgt = [kg1, kg1]
    banks = Banks([0, 1, 2, 3])
    P.dma("sp", gts[0].rearrange("p a b -> p (a b)"), gated_d[0].rearrange("p a b -> p (a b)"),
          reads=[("gd", 0)], writes=[kgt[0]])
    load_h(hT2)
    for half in range(2):
        t0 = half * T
        if half == 1:
            P.dma("sp", gts[1].rearrange("p a b -> p (a b)"), gated_d[1].rearrange("p a b -> p (a b)"),
                  reads=[("gd", 1)], writes=[kgt[1]])

        def cons_o(mc, ti, ps, b):
            P.op("act", lambda e: e.activation(f[:, mc, :], ps, AF.Copy), reads=[("ps", b)], writes=[kf])
        linear_fm(P, C, D["ret_wo"], 2 * KC, [(0, T)], lambda k, ti, half=half: gts[half][:, k, :], [kgt[half]],
                  cons_o, banks)
        emit_postnorm(P, C, f, hT2[:, :, t0:t0 + T], g_post, rstd2, sqo, T, kf, ksqo, kr2)
    P.barrier()
    return hT2


def phase_pool2(P, C, hT, g_pre, g_post, halo_fill, poolw_d, scale, band_d):
    A = C.A
    T = 512
    NX = NT + 16
    A.reset(NT * KC * 4)
    rstdx = A.alloc([128, NX], F32)
    halo = A.alloc([128, KC, 16], F32)
    stage16 = A.alloc([128, KC, 16], F32)
    ub_flat = A.alloc([128, KC * NX], BF16)
    ub = ub_flat.rearrange("p (a b) -> p a b", a=KC)
    mixed = ub_flat[:, 0:KC * NT].rearrange("p (a b) -> p a b", a=KC)
    utm_flat = A.alloc([128, KC * NT], BF16)
    utm = utm_flat.rearrange("p (c j f) -> p c j f", c=KC, j=8)
    sq = utm_flat[:, 0:KC * T].rearrange("p (a b) -> p a b", a=KC)
    f = utm_flat.bitcast(F32).rearrange("p (a b) -> p a b", a=KC)
    uhtm = A.alloc([128, KC, 128], BF16)
    wp = A.alloc([128, 4, 4, 512], BF16)
    bt = A.alloc([128, 4, 4, 128], BF16)
    rstd = A.alloc([128, T], F32)
    sqp = A.alloc([128, KC, T], BF16)
    khalo, krx, kub, kutm, kuh, kwp, kbt, krstd, ksqp = (C.key(n) for n in
        ("halo", "rstdx", "ub", "utm", "uh", "wp", "bt", "rstd", "sqp"))
    halo_fill(halo, khalo, stage16, C.key("st16"))
    for g in range(4):
        P.dma("pool", wp[:, g, :, :], poolw_d[g], writes=[kwp])
    P.dma("pool", bt.rearrange("p a b c -> p (a b c)"), band_d.rearrange("p a b c -> p (a b c)"), writes=[kbt])
    P.op("act", lambda e: e.activation(sq[:, :, 0:16], halo, AF.Square), reads=[khalo], writes=[kutm])
    emit_rstd(P, C, sq[:, :, 0:16], KC, 16, rstdx[:, 0:16], 7, D_MODEL, kutm, krx)
    for t0 in range(0, NT, T):
        P.op("act", lambda e, t0=t0: e.activation(sq, hT[:, :, t0:t0 + T], AF.Square),
             reads=[("h",)], writes=[kutm])
        emit_rstd(P, C, sq, KC, T, rstdx[:, 16 + t0:16 + t0 + T], 7, D_MODEL, kutm, krx)
    for c in range(KC):
        P.op("dve", lambda e, c=c: e.scalar_tensor_tensor(
            ub[:, c, 0:16], halo[:, c, :], g_pre[:, c:c + 1], rstdx[:, 0:16], ALU.mult, ALU.mult),
            reads=[khalo, krx, ("cst",)], writes=[kub])
        P.op("dve", lambda e, c=c: e.scalar_tensor_tensor(
            ub[:, c, 16:NX], hT[:, c, :], g_pre[:, c:c + 1], rstdx[:, 16:NX], ALU.mult, ALU.mult),
            reads=[("h",), krx, ("cst",)], writes=[kub])
    banks = Banks([0, 1, 2, 3, 4, 5])
    ev = [0]

    def evac(dst, src, reads, writes):
        eng = "act" if ev[0] % 2 == 0 else "dve"
        ev[0] += 1
        if eng == "act":
            P.op("act", lambda e: e.activation(dst, src, AF.Copy), reads=reads, writes=writes)
        else:
            P.op("dve", lambda e: e.tensor_copy(dst, src), reads=reads, writes=writes)
    for cg in range(KC // 4):
        b = banks.next()
        for q in range(4):
            c = cg * 4 + q
            mm(P, C.ps[b][0:16, q * 128:(q + 1) * 128], ub[:, c, 0:16], C.ident[:], True, True,
               reads=[kub, ("ident",)], writes=[("ps", b)], inc=(q == 3))
        evac(uhtm[0:16, cg * 4:cg * 4 + 4, :], C.ps[b][0:16, 0:512].rearrange("p (a b) -> p a b", a=4),
             [("ps", b)], [kuh])
    for c in range(KC):
        for jg in range(2):
            b = banks.next()
            for q in range(4):
                j = jg * 4 + q
                mm(P, C.ps[b][:, q * 128:(q + 1) * 128], ub[:, c, 16 + j * 128:16 + (j + 1) * 128], C.ident[:],
                   True, True, reads=[kub, ("ident",)], writes=[("ps", b)], inc=(q == 3))
            evac(utm[:, c, jg * 4:jg * 4 + 4, :], C.ps[b][:, 0:512].rearrange("p (a b) -> p a b", a=4),
                 [("ps", b)], [kutm])
    for c in range(KC):
        g = c // 4
        for jg in range(2):
            b = banks.next()
            for q in range(4):
                j = jg * 4 + q
                o = C.ps[b][:, q * 128:(q + 1) * 128]
                cur = bt[:, g, 2, :] if j == 0 else bt[:, g, 0, :]
                mm(P, o, utm[:, c, j, :], cur, True, False, reads=[kutm, kbt], writes=[("ps", b)], inc=False)
                if j == 0:
                    mm(P, o, uhtm[0:16, c, :], bt[0:16, g, 3, :], False, True,
                       reads=[kuh, kbt], writes=[("ps", b)], inc=(q == 3))
                else:
                    mm(P, o, utm[:, c, j - 1, :], bt[:, g, 1, :], False, True,
                       reads=[kutm, kbt], writes=[("ps", b)], inc=(q == 3))
            evac(mixed[:, c, jg * 512:(jg + 1) * 512], C.ps[b][:, 0:512], [("ps", b)], [kub])
    banks = Banks([0, 1, 2, 3])
    for t0 in range(0, NT, T):
        for g in range(4):
            for m in range(4):
                b = banks.next()
                c = g * 4 + m
                for k in range(4):
                    mm(P, C.ps[b][:, 0:T], wp[:, g, k, m * 128:(m + 1) * 128], mixed[:, g * 4 + k, t0:t0 + T],
                       k == 0, k == 3, reads=[kwp, kub], writes=[("ps", b)])
                P.op("dve", lambda e, c=c, b=b: e.tensor_scalar(
                    f[:, c, :], C.ps[b][:, 0:T], scale[:, c:c + 1], None, ALU.mult),
                    reads=[("ps", b), ("cst",)], writes=[kutm])
        emit_postnorm(P, C, f, hT[:, :, t0:t0 + T], g_post, rstd, sqp, T, kutm, ksqp, krstd)
    P.barrier()


CST_ITEMS = [
    ("ln", (4, 6, KC)), ("pool_scale", (2, KC)), ("pool_corr", (4, 16)),
    ("bq", (KC,)), ("bqs", (KC,)), ("bk", (4,)), ("bks", (4,)), ("bo", (KC,)),
    ("invf_swa", (1,)), ("sgn", (1,)),
    ("gn_g", (2 * KC,)), ("gn_b", (2 * KC,)), ("invf_ret", (1,)), ("sdcol", (8,)),
    ("dloc", (8, 8)), ("coef", (64,)), ("sel", (4,)),
]
CST_OFF = {}
_o = 0
for _n, _s in CST_ITEMS:
    _sz = int(np.prod(_s))
    CST_OFF[_n] = (_o, _s)
    _o += _sz
NCST = _o


def cst_view(C, name):
    o, shp = CST_OFF[name]
    n = int(np.prod(shp))
    v = C.cst[:, o:o + n]
    if len(shp) == 2:
        v = v.rearrange("p (a b) -> p a b", a=shp[0])
    elif len(shp) == 3:
        v = v.rearrange("p (a b c) -> p a b c", a=shp[0], b=shp[1])
    return v


def prologue(P, C):
    cst_d = P.dram_in("cst", [128, NCST], F32)
    P.dma("sp", C.cst[:], cst_d, writes=[("cst",)])
    ident_d = P.dram_in("ident", [128, 128], F32)
    P.dma("pool", C.ident[:], ident_d, writes=[("ident",)])
    perm_d = P.dram_in("perm", [128, 128], F32)
    P.dma("pool", C.perm[:], perm_d, writes=[("ident",)])
    ln = cst_view(C, "ln")
    for i in range(4):
        for s in (1, 5):
            P.op("dve", lambda e, i=i, s=s: e.tensor_scalar(ln[:, i, s, :], ln[:, i, s, :], 0.5, None, ALU.mult),
                 reads=[("cst",)], writes=[("cst",)])
    return ln


GROUPS = [[0, 1, 2, 3], [4, 5, 6, 7]]


def build_fused():
    P = Prog()
    C = Ctx(P, NCST)
    nc = P.nc
    ln = prologue(P, C)
    hT_d = P.dram_in("hT", [KC * 128, NT], F32)
    out_d = P.dram_out("outT", [KC * 128, NT], F32)
    hT = C.A.t[:, 0:KC * NT].rearrange("p (a b) -> p a b", a=KC)
    sel = cst_view(C, "sel")
    hT_v = hT_d.rearrange("(c p) t -> p c t", p=128)
    out_v = out_d.rearrange("(c p) t -> p c t", p=128)
    for ti in (1, 0):
        P.dma("sp", hT[:, :, ti * 512:(ti + 1) * 512], hT_v[:, :, ti * 512:(ti + 1) * 512], writes=[("h", ti)])

    def ffn_seq(*specs, hook=None, first=False):
        ffns = []
        for (i, s) in specs:
            win_d = P.dram_in("win_%d_%d" % (i, s), [FC, 128, KC, 256], F32)
            wout_d = P.dram_in("wout_%d_%d" % (i, s), [KC // 2, 128, FC, 256], F32)
            base = 0 if s == 0 else 4
            ffns.append((ln[:, i, base, :], ln[:, i, base + 1, :], win_d, wout_d))
        phase_ffn_seq(P, C, hT, ffns, hook=hook, first=first)

    xbuf = {}

    def ex_send(w, tag):
        src = nc.dram_tensor("xs_" + tag, [KC * 128, w], F32).ap()
        dst = nc.dram_tensor("xd_" + tag, [4 * KC * 128, w], F32).ap()
        xbuf[tag] = dst

        def hook(hk):
            P.dma("sp", src.rearrange("(c p) t -> p c t", p=128), hT[:, :, NT - w:NT],
                  reads=[hk], writes=[("xs", tag)])
            P.op("pool", lambda e: e.collective_compute("AllGather", ALU.bypass, replica_groups=GROUPS,
                                                        ins=[src.opt()], outs=[dst.opt()]),
                 reads=[("xs", tag)], writes=[("xd", tag)])
        return hook

    def exchange(w, tag):
        dst = xbuf[tag]

        def fill(halo, kh, stage, kst):
            for r in range(4):
                P.dma("sp", stage, dst[r * KC * 128:(r + 1) * KC * 128, :].rearrange("(c p) t -> p c t", p=128),
                      reads=[("xd", tag)], writes=[kst])
                if r == 0:
                    P.op("dve", lambda e: e.tensor_scalar(halo, stage, sel[:, 0:1], None, ALU.mult),
                         reads=[kst, ("cst",)], writes=[kh])
                else:
                    P.op("dve", lambda e, r=r: e.scalar_tensor_tensor(halo, stage, sel[:, r:r + 1], halo,
                                                                      ALU.mult, ALU.add),
                         reads=[kst, ("cst",), kh], writes=[kh])
        return fill

    def pool(i, j, tag):
        fill = exchange(16, tag)
        pw_d = P.dram_in("poolw%d" % j, [4, 128, 4, 512], F32)
        if "band" not in xbuf:
            xbuf["band"] = P.dram_in("pool_band", [128, 4, 4, 128], F32)
        phase_pool2(P, C, hT, ln[:, i, 2, :], ln[:, i, 3, :], fill, pw_d,
                    cst_view(C, "pool_scale")[:, j, :], xbuf["band"])

    ffn_seq((0, 0), hook=ex_send(16, "a"), first=True)
    pool(0, 0, "a")
    ffn_seq((0, 1), (1, 0))
    posr_d = P.dram_in("posr", [128, NT], I32)
    retw_d = P.dram_in("ret_w", [8, 6, 128, KC, 256], F32)
    sl_src = nc.dram_tensor("sl_src", [8, 256, 512], F32).ap()
    sl_dst = nc.dram_tensor("sl_dst", [8, 4 * 256, 512], F32).ap()

    def sloc_store(hh, Sst, kS):
        P.dma("sp", sl_src[hh].rearrange("(a p) v -> p a v", p=128), Sst, reads=[kS], writes=[("sls", hh)])

    def sloc_coll(hh):
        P.op("pool", lambda e: e.collective_compute("AllGather", ALU.bypass, replica_groups=GROUPS,
                                                    ins=[sl_src[hh].opt()], outs=[sl_dst[hh].opt()]),
             reads=[("sls", hh)], writes=[("sld", hh)])
    kt_s = nc.dram_tensor("kt_scr", [8, 128, 2 * NT], BF16).ap()
    v_s = nc.dram_tensor("v_scr", [8, 128, 8 * 512], BF16).ap()

    def kv_store(hh, KT, kkt, V, kv):
        P.dma("sp", kt_s[hh], KT.rearrange("p a b -> p (a b)"), reads=[kkt], writes=[("kts", hh)])
        P.dma("sp", v_s[hh], V.rearrange("p a b -> p (a b)"), reads=[kv], writes=[("vs", hh)])

    def kv_load(hh, KT, kkt, V, kv):
        P.dma("sp", KT.rearrange("p a b -> p (a b)"), kt_s[hh], reads=[("kts", hh)], writes=[kkt])
        P.dma("sp", V.rearrange("p a b -> p (a b)"), v_s[hh], reads=[("vs", hh)], writes=[kv])
    DA = {"posr": posr_d, "ret_w": retw_d, "sloc_store": sloc_store, "sloc_coll": sloc_coll, "kv_store": kv_store,
          "invf_ret": cst_view(C, "invf_ret"), "dloc": cst_view(C, "dloc")}
    phase_retA(P, C, hT, ln[:, 1, 2, :], DA)
    hsp = nc.dram_tensor("hspill", [KC * 128, NT], F32).ap()
    P.dma("sp", hsp.rearrange("(c p) t -> p c t", p=128), hT, reads=[("h",)], writes=[("hsp",)])
    P.barrier(skip_pool=True)

    def load_h2(dst):
        P.dma("sp", dst, hsp.rearrange("(c p) t -> p c t", p=128), reads=[("hsp",)], writes=[("h",)])
    rwo_d = P.dram_in("ret_wo", [KC, 128, 2 * KC, 128], F32)
    DB = {"posr": posr_d, "ret_w": retw_d, "kv_load": kv_load,
          "dtab": P.dram_in("dtab", [8, 128, 2, 128], F32),
          "ret_wo": [rwo_d[m] for m in range(KC)],
          "invf_ret": cst_view(C, "invf_ret"), "sdcol": cst_view(C, "sdcol"),
          "coef": cst_view(C, "coef"), "gn_g": cst_view(C, "gn_g"), "gn_b": cst_view(C, "gn_b"),
          "sall_ap": (lambda c, hh: sl_dst[hh][c * 256:(c + 1) * 256, :].rearrange("(a p) v -> p a v", p=128)),
          "sall_keys": (lambda hh: [("sld", hh)])}
    DB["reuse"] = DA["retA_state"]
    DB["gated_d"] = nc.dram_tensor("gated_scr", [2, 128, 2 * KC, 512], BF16).ap()
    phase_retB2(P, C, hT, ln[:, 1, 3, :], DB, load_h2)
    ffn_seq((1, 1), (2, 0), hook=ex_send(128, "b"))
    def units(name, n):
        d = P.dram_in(name, [n, 128, KC, 128], F32)
        return [d[m] for m in range(n)]
    DS = {"halo_fill": exchange(128, "b"),
          "posx": P.dram_in("posx", [128, NT + 128], I32),
          "tab": P.dram_in("swa_tab", [128, 928], F32),
          "wq": units("swa_wq", 16), "wk": units("swa_wk", 4),
          "wv": units("swa_wv", 4), "wo": units("swa_wo", 16),
          "bq": cst_view(C, "bq"), "bqs": cst_view(C, "bqs"), "bk": cst_view(C, "bk"),
          "bks": cst_view(C, "bks"), "bo": cst_view(C, "bo"),
          "invf": cst_view(C, "invf_swa"), "sgn": cst_view(C, "sgn"), "perm": C.perm[:]}
    phase_swa(P, C, hT, ln[:, 2, 2, :], ln[:, 2, 3, :], DS)
    ffn_seq((2, 1), (3, 0), hook=ex_send(16, "c"))
    pool(3, 1, "c")
    ffn_seq((3, 1), hook=lambda hk: P.dma("sp", out_v[:, :, 512:1024], hT[:, :, 512:1024], reads=[hk]))
    P.dma("sp", out_v[:, :, 0:512], hT[:, :, 0:512], reads=[("h",)])
    print("sem counts", P.ecnt, max(P.dcnt))
    return P.finish()


FUSED_INPUTS = (["ident", "perm"] + ["win_%d_%d" % (i, s) for i in range(4) for s in range(2)]
                + ["wout_%d_%d" % (i, s) for i in range(4) for s in range(2)]
                + ["poolw0", "poolw1", "pool_band", "posr", "ret_w", "dtab", "ret_wo", "posx", "swa_tab",
                   "swa_wq", "swa_wk", "swa_wv", "swa_wo"])


def kernel_fused(**inputs):
    host = Host(inputs)
    x = host.inp["x"]
    nc = build_fused()
    in_maps = []
    for c in range(NCORES):
        b, s0 = host.core_info(c)
        m = {"hT": np.ascontiguousarray(x[b, s0:s0 + NT, :].T), "cst": host.cst(c)}
        for n in FUSED_INPUTS:
            m[n] = host.const(n, c if n in PER_CORE else None)
        in_maps.append(m)
    res = run_bass_kernel_spmd(nc, in_maps, core_ids=list(range(NCORES)))
    out = np.empty((BATCH, SEQ, D_MODEL), np.float32)
    for c in range(NCORES):
        b, s0 = host.core_info(c)
        out[b, s0:s0 + NT, :] = np.asarray(res.results[c]["outT"]).T
    return out


def build_launch(phases):
    P = Prog()
    C = Ctx(P, NCST)
    ln = prologue(P, C)
    hT_d = P.dram_in("hT", [KC * 128, NT], F32)
    out_d = P.dram_out("outT", [KC * 128, NT], F32)
    hT = C.A.t[:, 0:KC * NT].rearrange("p (a b) -> p a b", a=KC)

    def load_h(dst):
        P.dma("sp", dst, hT_d.rearrange("(c p) t -> p c t", p=128), writes=[("h",)])

    for pi, ph in enumerate(phases):
        kind = ph[0]
        if pi == 0 and kind != "retB":
            load_h(hT)
        if kind == "ffn":
            i, s = ph[1], ph[2]
            win_d = P.dram_in("win_%d_%d" % (i, s), [FC, 128, KC, 256], F32)
            wout_d = P.dram_in("wout_%d_%d" % (i, s), [KC // 2, 128, FC, 256], F32)
            base = 0 if s == 0 else 4
            phase_ffn(P, C, hT, ln[:, i, base, :], ln[:, i, base + 1, :], win_d, wout_d)
        elif kind == "pool":
            i, j = ph[1], ph[2]
            halo_d = P.dram_in("halo16", [128, KC, 16], F32)
            pw_d = P.dram_in("poolw", [4, 128, 4, 512], F32)
            band_d = P.dram_in("pool_band", [128, 4, 4, 128], F32)
            phase_pool2(P, C, hT, ln[:, i, 2, :], ln[:, i, 3, :],
                        (lambda halo, kh, st, kst, halo_d=halo_d: P.dma("sp", halo, halo_d, writes=[kh])), pw_d,
                        cst_view(C, "pool_scale")[:, j, :], band_d)
        elif kind == "retA":
            i = ph[1]
            D = {"posr": P.dram_in("posr", [128, NT], I32),
                 "ret_w": P.dram_in("ret_w", [8, 6, 128, KC, 256], F32),
                 "invf_ret": cst_view(C, "invf_ret"), "dloc": cst_view(C, "dloc")}
            sloc_d = P.dram_out("sloc", [8, 2, 128, 512], F32)
            D["sloc_store"] = (lambda hh, Sst, kS, sloc_d=sloc_d:
                               P.dma("sp", sloc_d[hh].rearrange("a p v -> p a v"), Sst, reads=[kS]))
            phase_retA(P, C, hT, ln[:, i, 2, :], D)
        elif kind == "retB":
            i = ph[1]
            rwo_d = P.dram_in("ret_wo", [KC, 128, 2 * KC, 128], F32)
            D = {"posr": P.dram_in("posr", [128, NT], I32),
                 "ret_w": P.dram_in("ret_w", [8, 6, 128, KC, 256], F32),
                 "sall": P.dram_in("sall", [4, 8, 2, 128, 512], F32),
                 "dtab": P.dram_in("dtab", [8, 128, 2, 128], F32),
                 "ret_wo": [rwo_d[m] for m in range(KC)],
                 "invf_ret": cst_view(C, "invf_ret"), "sdcol": cst_view(C, "sdcol"),
                 "coef": cst_view(C, "coef"), "gn_g": cst_view(C, "gn_g"), "gn_b": cst_view(C, "gn_b")}
            D["sall_ap"] = (lambda c, hh, D=D: D["sall"][c, hh].rearrange("a p v -> p a v"))
            D["sall_keys"] = (lambda hh: [])
            phase_retB(P, C, hT, ln[:, i, 2, :], ln[:, i, 3, :], D, load_h)
        elif kind == "swa":
            i = ph[1]
            def units(name, n):
                d = P.dram_in(name, [n, 128, KC, 128], F32)
                return [d[m] for m in range(n)]
            halo_d = P.dram_in("halo128", [128, KC, 128], F32)
            D = {"halo_fill": (lambda halo, kh, st, kst, halo_d=halo_d: P.dma("sp", halo, halo_d, writes=[kh])),
                 "posx": P.dram_in("posx", [128, NT + 128], I32),
                 "tab": P.dram_in("swa_tab", [128, 928], F32),
                 "wq": units("swa_wq", 16), "wk": units("swa_wk", 4),
                 "wv": units("swa_wv", 4), "wo": units("swa_wo", 16),
                 "bq": cst_view(C, "bq"), "bqs": cst_view(C, "bqs"), "bk": cst_view(C, "bk"),
                 "bks": cst_view(C, "bks"), "bo": cst_view(C, "bo"),
                 "invf": cst_view(C, "invf_swa"), "sgn": cst_view(C, "sgn"), "perm": C.perm[:]}
            phase_swa(P, C, hT, ln[:, i, 2, :], ln[:, i, 3, :], D)
        else:
            raise ValueError(kind)
    P.dma("sp", out_d.rearrange("(c p) t -> p c t", p=128), hT, reads=[("h",)])
    return P.finish()


def fm_vec(v):
    v = np.asarray(v, np.float32)
    return np.ascontiguousarray(v.reshape(-1, 128).T)


def tile_w(w, UC):
    K_, N_ = w.shape
    return np.ascontiguousarray(w.reshape(K_ // 128, 128, N_ // UC, UC).transpose(2, 1, 0, 3))


def tile_win(w_in):
    g = w_in[:, :D_FF].reshape(KC, 128, FC, 128)
    u = w_in[:, D_FF:].reshape(KC, 128, FC, 128)
    out = np.empty((FC, 128, KC, 256), np.float32)
    out[:, :, :, :128] = g.transpose(2, 1, 0, 3)
    out[:, :, :, 128:] = u.transpose(2, 1, 0, 3)
    return out


def tile_wout(w_out):
    w = w_out.reshape(FC, 128, KC // 2, 256)
    return np.ascontiguousarray(w.transpose(2, 1, 0, 3))


def swap_halves(x, hd=64):
    s = x.shape
    y = x.reshape(s[:-1] + (s[-1] // hd, 2, hd // 2))[..., ::-1, :]
    return np.ascontiguousarray(y).reshape(s)


def dup_heads(x, hd=64):
    s = x.shape
    y = x.reshape(s[:-1] + (s[-1] // hd, 1, hd))
    y = np.repeat(y, 2, axis=-2)
    return np.ascontiguousarray(y).reshape(s[:-1] + (2 * s[-1],))


def hT_to_halo(hT, w):
    return np.ascontiguousarray(hT[:, NT - w:].reshape(KC, 128, w).transpose(1, 0, 2))


class Host:
    def __init__(self, inp):
        self.inp = {k: np.asarray(v) for k, v in inp.items()}
        self.cache = {}

    def core_info(self, c):
        return c // 4, (c % 4) * NT

    def cst(self, c):
        I = self.inp
        b, s0 = self.core_info(c)
        out = np.zeros((128, NCST), np.float32)

        def put(name, arr):
            o, shp = CST_OFF[name]
            n = int(np.prod(shp))
            out[:, o:o + n] = np.asarray(arr, np.float32).reshape(128, n)
        ln = np.zeros((128, 4, 6, KC), np.float32)
        for i in range(4):
            for k, nm in enumerate(("ln_ffn1", "ln_mix", "ln_ffn2")):
                for s in range(2):
                    ln[:, i, 2 * k + s, :] = fm_vec(I[nm][i, s])
        put("ln", ln)
        put("pool_scale", np.stack([fm_vec(I["pool_scale"][j]) for j in range(2)], axis=1))
        t = np.arange(16) + s0 + 1
        corr = np.stack([1.0 / np.minimum(t, w) for w in (2, 4, 8, 16)], axis=0)
        put("pool_corr", np.broadcast_to(corr[None], (128, 4, 16)))
        bi = I["swa_b_in"][0]
        put("bq", fm_vec(bi[:2048]))
        put("bqs", fm_vec(swap_halves(bi[:2048])))
        put("bk", fm_vec(dup_heads(bi[2048:2304])))
        put("bks", fm_vec(dup_heads(swap_halves(bi[2048:2304]))))
        put("bo", fm_vec(I["swa_b_out"][0]))
        p = np.arange(128)
        put("invf_swa", (10000.0 ** (-(2.0 * (p % 32)) / 64.0)).astype(np.float32)[:, None])
        put("sgn", np.where((p % 64) < 32, -1.0, 1.0)[:, None])
        put("gn_g", fm_vec(I["ret_gn_g"][0]))
        put("gn_b", fm_vec(I["ret_gn_b"][0]))
        put("invf_ret", (10000.0 ** (-np.linspace(0.0, 1.0, 128, dtype=np.float32))).astype(np.float32)[:, None])
        gam = np.array(GAMMAS, np.float64)
        put("sdcol", gam[None, :] ** (127.0 - p[:, None]))
        tt = (np.arange(8)[None, :, None] * 128 + p[:, None, None])
        put("dloc", gam[None, None, :] ** (1023.0 - tt))
        coef = np.zeros((8, 8))
        r = c % 4
        for r2 in range(4):
            if r2 < r:
                coef[r2] = gam ** (1024.0 * (r - 1 - r2))
        put("coef", np.broadcast_to(coef.reshape(1, 64), (128, 64)))
        sel = np.zeros(4)
        if r > 0:
            sel[r - 1] = 1.0
        put("sel", np.broadcast_to(sel.reshape(1, 4), (128, 4)))
        return out

    def const(self, name, c=None):
        I = self.inp
        key = (name, c)
        if key in self.cache:
            return self.cache[key]
        if name == "ident":
            v = np.eye(128, dtype=np.float32)
        elif name == "perm":
            v = np.zeros((128, 128), np.float32)
            pp = np.arange(128)
            v[pp, pp ^ 32] = 1.0
        elif name.startswith("win_"):
            _, i, s = name.split("_")
            v = tile_win(I["ffn_w_in"][int(i), int(s)])
        elif name.startswith("wout_"):
            _, i, s = name.split("_")
            v = tile_wout(I["ffn_w_out"][int(i), int(s)])
        elif name.startswith("poolw"):
            j = int(name[5:])
            v = np.ascontiguousarray(I["pool_w"][j].reshape(4, 4, 128, 512).transpose(0, 2, 1, 3))
        elif name == "pool_band":
            b, s0 = self.core_info(c)
            v = np.zeros((128, 4, 4, 128), np.float32)
            s_ = np.arange(128)[:, None]
            t_ = np.arange(128)[None, :]
            for g, w in enumerate((2, 4, 8, 16)):
                cur = ((t_ - s_ >= 0) & (t_ - s_ < w)) / float(w)
                prev = ((t_ - (s_ - 128)) < w) / float(w)
                if s0 == 0:
                    cur0 = ((t_ - s_ >= 0) & (t_ - s_ < w)) / np.minimum(t_ + 1.0, float(w))
                else:
                    cur0 = cur
                v[:, g, 0, :] = cur - np.eye(128)
                v[:, g, 1, :] = prev
                v[:, g, 2, :] = cur0 - np.eye(128)
                v[0:16, g, 3, :] = prev[112:128, :]
        elif name == "ret_w":
            w = I["ret_w_in"][0]
            v = np.empty((8, 6, 128, KC, 256), np.float32)
            for h in range(8):
                cols = [w[:, h * 256:(h + 1) * 256], w[:, 2048 + h * 256:2048 + (h + 1) * 256],
                        w[:, 4096 + h * 512:4096 + h * 512 + 256], w[:, 4096 + h * 512 + 256:4096 + (h + 1) * 512],
                        w[:, 8192 + h * 512:8192 + h * 512 + 256], w[:, 8192 + h * 512 + 256:8192 + (h + 1) * 512]]
                for ui, x in enumerate(cols):
                    v[h, ui] = x.reshape(KC, 128, 256).transpose(1, 0, 2)
        elif name == "ret_wo":
            v = tile_w(I["ret_w_out"][0], 128)
        elif name == "dtab":
            gam = np.array(GAMMAS, np.float64)
            i_ = np.arange(128)
            rel = i_[None, :] - i_[:, None]
            v = np.zeros((8, 128, 2, 128), np.float32)
            for h in range(8):
                v[h, :, 0, :] = np.where(rel >= 0, gam[h] ** np.maximum(rel, 0), 0.0)
                v[h, :, 1, :] = (gam[h] ** (i_ + 1.0))[None, :]
        elif name in ("swa_wq", "swa_wqs", "swa_wk", "swa_wks", "swa_wv", "swa_wo"):
            w = I["swa_w_in"][0]
            if name == "swa_wq":
                v = tile_w(w[:, :2048], 128)
            elif name == "swa_wqs":
                v = tile_w(swap_halves(w[:, :2048]), 128)
            elif name == "swa_wk":
                v = tile_w(dup_heads(w[:, 2048:2304]), 128)
            elif name == "swa_wks":
                v = tile_w(dup_heads(swap_halves(w[:, 2048:2304])), 128)
            elif name == "swa_wv":
                v = tile_w(dup_heads(w[:, 2304:2560]), 128)
            else:
                v = tile_w(I["swa_w_out"][0], 128)
        elif name == "posr":
            b, s0 = self.core_info(c)
            v = np.ascontiguousarray(np.broadcast_to(I["positions"][b, s0:s0 + NT].astype(np.int32)[None], (128, NT)))
        elif name == "posx":
            b, s0 = self.core_info(c)
            pos = np.zeros(NT + 128, np.int32)
            pos[128:] = I["positions"][b, s0:s0 + NT]
            if s0 > 0:
                pos[:128] = I["positions"][b, s0 - 128:s0]
            v = np.ascontiguousarray(np.broadcast_to(pos[None], (128, NT + 128)))
        elif name == "swa_tab":
            b, s0 = self.core_info(c)
            tab = np.zeros((128, 928), np.float32)
            tab[:, 0:512] = dup_heads(I["swa_b_in"][0][2304:2560])[None, :]
            j = np.arange(128)[:, None]
            i_ = np.arange(128)[None, :]
            tab[:, 512:640] = (j <= i_)
            tab[:, 640:768] = (j > i_)
            tab[:, 768:896] = (j > i_) if s0 > 0 else 0.0
            tab[:, 896:928] = I["swa_sinks"][0][None, :]
            v = tab
        else:
            raise KeyError(name)
        self.cache[key] = v
        return v


PER_CORE = ("posr", "posx", "swa_tab", "pool_band")


def run_launch(host, phases, hTs, extra):
    nc = build_launch(phases)
    names = ["ident", "perm"]
    rename = {}
    for ph in phases:
        if ph[0] == "ffn":
            names += ["win_%d_%d" % (ph[1], ph[2]), "wout_%d_%d" % (ph[1], ph[2])]
        elif ph[0] == "pool":
            rename["poolw"] = "poolw%d" % ph[2]
            names += ["poolw", "pool_band"]
        elif ph[0] == "retA":
            names += ["posr", "ret_w"]
        elif ph[0] == "retB":
            names += ["posr", "ret_w", "dtab", "ret_wo"]
        elif ph[0] == "swa":
            names += ["posx", "swa_tab", "swa_wq", "swa_wk", "swa_wv", "swa_wo"]
    in_maps = []
    for c in range(NCORES):
        m = {"hT": hTs[c], "cst": host.cst(c)}
        for n in names:
            src = rename.get(n, n)
            m[n] = host.const(src, c if src in PER_CORE else None)
        for k, v in extra.items():
            m[k] = v[c]
        in_maps.append(m)
    res = run_bass_kernel_spmd(nc, in_maps, core_ids=list(range(NCORES)))
    return res.results


LAUNCHES = [
    [("ffn", 0, 0)],
    [("pool", 0, 0), ("ffn", 0, 1), ("ffn", 1, 0), ("retA", 1)],
    [("retB", 1), ("ffn", 1, 1), ("ffn", 2, 0)],
    [("swa", 2), ("ffn", 2, 1), ("ffn", 3, 0)],
    [("pool", 3, 1), ("ffn", 3, 1)],
]


def halos(hTs, w):
    out = []
    for c in range(NCORES):
        if c % 4 == 0:
            out.append(np.zeros((128, KC, w), np.float32))
        else:
            out.append(hT_to_halo(hTs[c - 1], w))
    return out


def kernel(**inputs):
    return kernel_fused(**inputs)


def kernel_unfused(**inputs):
    host = Host(inputs)
    x = host.inp["x"]
    hTs = []
    for c in range(NCORES):
        b, s0 = host.core_info(c)
        hTs.append(np.ascontiguousarray(x[b, s0:s0 + NT, :].T))
    sall = None
    for li, phases in enumerate(LAUNCHES):
        extra = {}
        kinds = [p[0] for p in phases]
        if "pool" in kinds:
            extra["halo16"] = halos(hTs, 16)
        if "swa" in kinds:
            extra["halo128"] = halos(hTs, 128)
        if "retB" in kinds:
            extra["sall"] = [np.ascontiguousarray(sall[4 * (c // 4):4 * (c // 4) + 4]) for c in range(NCORES)]
        res = run_launch(host, phases, hTs, extra)
        hTs = [np.asarray(r["outT"]) for r in res]
        if "retA" in kinds:
            sall = np.ascontiguousarray(np.stack([np.asarray(r["sloc"]) for r in res], axis=0))
    out = np.empty((BATCH, SEQ, D_MODEL), np.float32)
    for c in range(NCORES):
        b, s0 = host.core_info(c)
        out[b, s0:s0 + NT, :] = hTs[c].T
    return out
```

```python
import numpy as np
import concourse.bass as bass
import concourse.mybir as mybir
from concourse.bass_utils import run_bass_kernel_spmd

F32 = mybir.dt.float32
BF16 = mybir.dt.bfloat16
I32 = mybir.dt.int32
ALU = mybir.AluOpType
AF = mybir.ActivationFunctionType

D_MODEL = 2048
D_FF = 5504
SEQ = 4096
BATCH = 2
NCORES = 8
NT = 1024
EPS = 1e-6


class Ev:
    __slots__ = ("sem", "val", "eng")

    def __init__(self, sem, val, eng):
        self.sem, self.val, self.eng = sem, val, eng


class Prog:
    ENGS = ("pe", "act", "dve", "pool", "sp")

    def __init__(self, n_dma_sems=32):
        self.nc = bass.Bass("TRN2", target_bir_lowering=False)
        nc = self.nc
        self.ops = {e: [] for e in self.ENGS}
        self.esem = {e: nc.alloc_semaphore("es_" + e) for e in ("pe", "act", "dve", "pool")}
        self.ecnt = {e: 0 for e in self.esem}
        self.pending = {e: [] for e in self.ENGS}
        self.dsems = [nc.alloc_semaphore("ds%d" % i) for i in range(n_dma_sems)]
        self.dcnt = [0] * n_dma_sems
        self.dnext = 0
        self.waited = {}
        self.res = {}
        self.n_sb = 0

    def sbuf(self, name, shape, dtype):
        return self.nc.alloc_sbuf_tensor(name, list(shape), dtype)

    def psum(self, name, shape, dtype=F32):
        return self.nc.alloc_psum_tensor(name, list(shape), dtype)

    def dram_in(self, name, shape, dtype):
        return self.nc.dram_tensor(name, list(shape), dtype, kind="ExternalInput").ap()

    def dram_out(self, name, shape, dtype):
        return self.nc.dram_tensor(name, list(shape), dtype, kind="ExternalOutput").ap()

    def _flush(self, eng):
        pend = self.pending[eng]
        if not pend:
            return
        last = self.ops[eng][-1]
        if not last["inc"]:
            last["inc"] = True
            self.ecnt[eng] += 1
        ev = Ev(self.esem[eng], self.ecnt[eng], eng)
        for key, is_w in pend:
            self._record(key, is_w, ev)
        self.pending[eng] = []

    def _record(self, key, is_w, ev):
        st = self.res.get(key)
        if st is None:
            st = self.res[key] = [None, []]
        if is_w:
            st[0] = ev
            st[1] = []
        else:
            rl = st[1]
            for i, o in enumerate(rl):
                if o.sem is ev.sem:
                    if o.val < ev.val:
                        rl[i] = ev
                    break
            else:
                rl.append(ev)

    def _deps(self, eng, reads, writes):
        pe_pend = self.pending["pe"]
        if pe_pend:
            keys = set(k for k, _ in pe_pend)
            if any(k in keys for k in reads) or any(k in keys for k in writes):
                if eng != "pe":
                    self._flush("pe")
        evs = []
        for r in reads:
            st = self.res.get(r)
            if st is not None and st[0] is not None:
                evs.append(st[0])
        for w in writes:
            st = self.res.get(w)
            if st is not None:
                if st[0] is not None:
                    evs.append(st[0])
                evs.extend(st[1])
        best = {}
        for ev in evs:
            if ev.eng == eng and eng == "pe":
                continue
            k = (eng, id(ev.sem))
            if self.waited.get(k, 0) >= ev.val:
                continue
            cur = best.get(id(ev.sem))
            if cur is None or cur[1] < ev.val:
                best[id(ev.sem)] = (ev.sem, ev.val)
        waits = []
        for sem, val in best.values():
            self.waited[(eng, id(sem))] = val
            waits.append((sem, val))
        return waits

    def op(self, eng, fn, reads=(), writes=(), inc=True):
        waits = self._deps(eng, reads, writes)
        rec = {"fn": fn, "waits": waits, "inc": False, "dma": None}
        self.ops[eng].append(rec)
        pend = self.pending[eng]
        for r in reads:
            pend.append((r, False))
        for w in writes:
            pend.append((w, True))
        if inc:
            self._flush(eng)
        return rec

    def dma(self, q, out_ap, in_ap, reads=(), writes=(), fn=None):
        i = self.dnext
        self.dnext = (i + 1) % len(self.dsems)
        sem = self.dsems[i]
        prev = self.dcnt[i]
        waits = self._deps(q, reads, writes)
        if prev > 0 and self.waited.get((q, id(sem)), 0) < prev:
            self.waited[(q, id(sem))] = prev
            waits = [w for w in waits if w[0] is not sem] + [(sem, prev)]
        self.dcnt[i] = prev + 16
        ev = Ev(sem, prev + 16, "dma")
        if fn is None:
            fn = (lambda e, o=out_ap, s=in_ap: e.dma_start(out=o, in_=s))
        rec = {"fn": fn, "waits": waits, "inc": False, "dma": sem}
        self.ops[q].append(rec)
        for r in reads:
            self._record(r, False, ev)
        for w in writes:
            self._record(w, True, ev)
        return ev

    def barrier(self, skip_pool=False):
        for e in self.esem:
            if self.pending[e]:
                self._flush(e)
        targets = [(self.esem[e], self.ecnt[e]) for e in self.esem
                   if self.ecnt[e] > 0 and not (skip_pool and e == "pool")]
        targets += [(s, c) for s, c in zip(self.dsems, self.dcnt) if c > 0]
        for e in self.ENGS:
            waits = []
            for sem, val in targets:
                if e in self.esem and sem is self.esem[e]:
                    continue
                if self.waited.get((e, id(sem)), 0) >= val:
                    continue
                self.waited[(e, id(sem))] = val
                waits.append((sem, val))
            if waits:
                self.ops[e].append({"fn": None, "waits": waits, "inc": False, "dma": None})

    def finish(self):
        self.barrier()
        nc = self.nc
        prog = self

        def mk(name):
            def body(eng):
                for rec in prog.ops[name]:
                    for sem, val in rec["waits"]:
                        eng.wait_ge(sem, val)
                    if rec["fn"] is None:
                        continue
                    ins = rec["fn"](eng)
                    if rec["dma"] is not None:
                        ins.then_inc(rec["dma"], 16)
                    elif rec["inc"]:
                        ins.then_inc(prog.esem[name], 1)
            return body

        with nc.Block() as block:
            block.tensor(mk("pe"))
            block.scalar(mk("act"))
            block.vector(mk("dve"))
            block.gpsimd(mk("pool"))
            block.sync(mk("sp"))
        return nc


import math

TWO_PI = 2.0 * math.pi
MAGIC = 12582912.0
PI_LO = 3.1415920
ARENA_BYTES = 200 * 1024
KC = D_MODEL // 128
FC = D_FF // 128


class Banks:
    def __init__(self, ids):
        self.ids = list(ids)
        self.i = 0

    def next(self):
        b = self.ids[self.i]
        self.i = (self.i + 1) % len(self.ids)
        return b


class Arena:
    def __init__(self, P, nbytes):
        self.t = P.sbuf("arena", [128, nbytes // 4], F32)
        self.cap = nbytes
        self.off = 0

    def reset(self, off=0):
        self.off = off

    def alloc(self, shape, dtype):
        esz = 2 if dtype is BF16 else 4
        n = 1
        for d in shape[1:]:
            n *= d
        nbytes = (n * esz + 31) // 32 * 32
        o = self.off
        self.off += nbytes
        assert self.off <= self.cap, ("arena overflow", self.off, self.cap)
        v = self.t[:, o // 4:(o + nbytes) // 4]
        if dtype is not F32:
            v = v.bitcast(dtype)
        v = v[:, 0:n]
        if len(shape) == 3:
            v = v.rearrange("p (a b) -> p a b", a=shape[1])
        elif len(shape) == 4:
            v = v.rearrange("p (a b c) -> p a b c", a=shape[1], b=shape[2])
        return v


class Ctx:
    def __init__(self, P, cst_cols):
        self.P = P
        self.ps = [P.psum("ps%d" % i, [128, 512]) for i in range(8)]
        self.ones = P.sbuf("ones_bf", [128, 128], BF16)
        self.ident = P.sbuf("ident_bf", [128, 128], BF16)
        self.perm = P.sbuf("perm_bf", [128, 128], BF16)
        self.eps = P.sbuf("eps_c", [128, 1], F32)
        self.cst = P.sbuf("cst_sb", [128, cst_cols], F32)
        P.op("dve", lambda e: e.memset(self.ones[:], 1.0), writes=[("ones",)])
        P.op("dve", lambda e: e.memset(self.eps[:], EPS), writes=[("ones",)])
        self.A = Arena(P, ARENA_BYTES)
        self.ring = []
        self.ring_i = 0
        self.uid = 0

    def key(self, name):
        self.uid += 1
        return (name, self.uid)

    def make_ring(self, nslots, elems=4096):
        self.ring = [self.A.alloc([128, elems], BF16) for _ in range(nslots)]
        self.ring_keys = [self.key("w") for _ in range(nslots)]
        self.ring_i = 0

    def next_slot(self):
        i = self.ring_i
        self.ring_i = (i + 1) % len(self.ring)
        return self.ring_keys[i], self.ring[i]


def mm(P, out, lhsT, rhs, start, stop, reads, writes, inc=None):
    if inc is None:
        inc = stop
    P.op("pe", lambda e: e.matmul(out, lhsT, rhs, start=start, stop=stop),
         reads=reads, writes=writes, inc=inc)


def emit_rstd(P, C, sq, nchunks, n, rstd, bank, dim, key_sq, key_rstd, extra=None):
    ps = C.ps[bank]
    for c in range(nchunks):
        mm(P, ps[:, 0:n], C.ones[:], sq[:, c, :], c == 0, c == nchunks - 1,
           reads=[key_sq, ("ones",)], writes=[("ps", bank)])
    P.op("act", lambda e: e.activation(rstd, ps[:, 0:n], AF.Sqrt, bias=C.eps[:, 0:1], scale=1.0 / dim),
         reads=[("ps", bank), ("ones",)], writes=[key_rstd])
    P.op("dve", lambda e: e.reciprocal(rstd, rstd), reads=[key_rstd], writes=[key_rstd])


def emit_postnorm(P, C, f, hs, g, rstd, sq, n, kf, ksq, krstd):
    P.op("act", lambda e: e.activation(sq, f, AF.Square), reads=[kf], writes=[ksq])
    emit_rstd(P, C, sq, KC, n, rstd, 7, D_MODEL, ksq, krstd)
    kfc = [C.key("pnf") for _ in range(KC)]
    for c in range(KC):
        P.op("dve", lambda e, c=c: e.tensor_tensor(f[:, c, :], f[:, c, :], rstd, ALU.mult),
             reads=[kf, krstd], writes=[kf if c == 0 else kfc[c]])
    for c in range(KC):
        edge = (c == 0 or c == KC - 1)
        P.op("dve", lambda e, c=c: e.scalar_tensor_tensor(
            hs[:, c, :], f[:, c, :], g[:, c:c + 1], hs[:, c, :], ALU.mult, ALU.add),
            reads=[kf, kfc[c], ("cst",), ("h",)], writes=[("h",)] if edge else [C.key("pnh")])


def emit_prenorm(P, C, hs, g, u, sq, rstd, n, ksq, krstd, ku):
    P.op("act", lambda e: e.activation(sq, hs, AF.Square), reads=[("h",)], writes=[ksq])
    emit_rstd(P, C, sq, KC, n, rstd, 7, D_MODEL, ksq, krstd)
    for c in range(KC):
        edge = (c == 0 or c == KC - 1)
        P.op("dve", lambda e, c=c: e.scalar_tensor_tensor(
            u[:, c, :], hs[:, c, :], g[:, c:c + 1], rstd, ALU.mult, ALU.mult),
            reads=[("h",), krstd, ("cst",)], writes=[ku] if edge else [C.key("pnu")])


def linear_fm(P, C, w_units, kc, tiles, rhs, rkeys, consume, banks):
    mc = 0
    for wu in w_units:
        UC = wu.shape[2]
        wk, slot = C.next_slot()
        P.dma("pool", slot[:, 0:kc * UC], wu.rearrange("p k c -> p (k c)"), writes=[wk])
        sv = slot[:, 0:kc * UC].rearrange("p (k c) -> p k c", k=kc)
        for m in range(UC // 128):
            for ti, (t0, n) in enumerate(tiles):
                b = banks.next()
                for k in range(kc):
                    mm(P, C.ps[b][:, 0:n], sv[:, k, m * 128:(m + 1) * 128], rhs(k, ti),
                       k == 0, k == kc - 1, reads=[wk] + rkeys, writes=[("ps", b)])
                consume(mc, ti, C.ps[b][:, 0:n], b)
            mc += 1


def linear_tm(P, C, w_units, kc, ntiles, lhs, lkeys, consume, banks):
    for ui, wu in enumerate(w_units):
        UC = wu.shape[2]
        wk, slot = C.next_slot()
        P.dma("pool", slot[:, 0:kc * UC], wu.rearrange("p k c -> p (k c)"), writes=[wk])
        sv = slot[:, 0:kc * UC].rearrange("p (k c) -> p k c", k=kc)
        for ti in range(ntiles):
            b = banks.next()
            for k in range(kc):
                mm(P, C.ps[b][:, 0:UC], lhs(k, ti), sv[:, k, :], k == 0, k == kc - 1,
                   reads=[wk] + lkeys, writes=[("ps", b)])
            consume(ui, ti, C.ps[b][:, 0:UC], b)


def emit_sincos(P, C, ang, tmp, sin_out, cos_out, n, kang, ktmp, ksin, kcos):
    for shift, dst, kd in ((0.0, sin_out, ksin), (0.5 * math.pi, cos_out, kcos)):
        if dst is None:
            continue
        src, ks = ang, kang
        if shift != 0.0:
            P.op("dve", lambda e, dst=dst, shift=shift: e.tensor_scalar(dst, ang, shift, None, ALU.add),
                 reads=[kang], writes=[kd])
            src, ks = dst, kd
        P.op("dve", lambda e, src=src: e.tensor_scalar(tmp, src, 1.0 / TWO_PI, MAGIC, ALU.mult, ALU.add),
             reads=[ks], writes=[ktmp])
        P.op("dve", lambda e: e.tensor_scalar(tmp, tmp, -MAGIC, None, ALU.add),
             reads=[ktmp], writes=[ktmp])
        P.op("dve", lambda e, dst=dst, src=src: e.scalar_tensor_tensor(dst, tmp, -TWO_PI, src, ALU.mult, ALU.add),
             reads=[ktmp, ks], writes=[kd])
        P.op("dve", lambda e, dst=dst: e.tensor_scalar(dst, dst, PI_LO, -PI_LO, ALU.min, ALU.max),
             reads=[kd], writes=[kd])
        P.op("act", lambda e, dst=dst: e.activation(dst, dst, AF.Sin), reads=[kd], writes=[kd])


def phase_ffn(P, C, hT, g1, g2h, win_d, wout_d):
    A = C.A
    T = 512
    A.reset(NT * KC * 4)
    u = A.alloc([128, KC, T], BF16)
    hid = A.alloc([128, FC, T], BF16)
    f = A.alloc([128, KC, T], F32)
    sg = A.alloc([128, 2, T], F32)
    rstd = A.alloc([128, T], F32)
    C.make_ring(3)
    ku, khid, kf, krstd = C.key("u"), C.key("hid"), C.key("f"), C.key("rstd")
    ksg = [C.key("sg0"), C.key("sg1")]
    for t0 in range(0, NT, T):
        hs = hT[:, :, t0:t0 + T]
        emit_prenorm(P, C, hs, g1, u, hid[:, 0:KC, :], rstd, T, khid, krstd, ku)
        for j in range(FC):
            wk, slot = C.next_slot()
            P.dma("pool", slot[:, 0:KC * 256], win_d[j].rearrange("p k c -> p (k c)"), writes=[wk])
            sv = slot[:, 0:KC * 256].rearrange("p (k c) -> p k c", k=KC)
            pair = j % 2
            bg, bu = 2 * pair, 2 * pair + 1
            for k in range(KC):
                mm(P, C.ps[bg][:, 0:T], sv[:, k, 0:128], u[:, k, :], k == 0, k == KC - 1,
                   reads=[wk, ku], writes=[("ps", bg)])
            for k in range(KC):
                mm(P, C.ps[bu][:, 0:T], sv[:, k, 128:256], u[:, k, :], k == 0, k == KC - 1,
                   reads=[wk, ku], writes=[("ps", bu)])
            P.op("act", lambda e, bg=bg, pair=pair: e.activation(sg[:, pair, :], C.ps[bg][:, 0:T], AF.Silu),
                 reads=[("ps", bg)], writes=[ksg[pair]])
            P.op("dve", lambda e, bu=bu, pair=pair, j=j: e.tensor_tensor(
                hid[:, j, :], sg[:, pair, :], C.ps[bu][:, 0:T], ALU.mult),
                reads=[ksg[pair], ("ps", bu)], writes=[khid])
        unit = 16
        for dg in range(KC // 2):
            par = dg % 2
            banks = (4 + 2 * par, 5 + 2 * par)
            k0 = 0
            while k0 < FC:
                nk = min(unit, FC - k0)
                wk, slot = C.next_slot()
                P.dma("pool", slot[:, 0:nk * 256],
                      wout_d[dg, :, k0:k0 + nk, :].rearrange("p k c -> p (k c)"), writes=[wk])
                sv = slot[:, 0:nk * 256].rearrange("p (k c) -> p k c", k=nk)
                for m in range(2):
                    for kk in range(nk):
                        kc_ = k0 + kk
                        mm(P, C.ps[banks[m]][:, 0:T], sv[:, kk, m * 128:(m + 1) * 128], hid[:, kc_, :],
                           kc_ == 0, kc_ == FC - 1, reads=[wk, khid], writes=[("ps", banks[m])],
                           inc=(kk == nk - 1))
                k0 += nk
            for m in range(2):
                c = dg * 2 + m
                P.op("act", lambda e, c=c, b=banks[m]: e.activation(f[:, c, :], C.ps[b][:, 0:T], AF.Copy),
                     reads=[("ps", banks[m])], writes=[kf])
        emit_postnorm(P, C, f, hs, g2h, rstd, u, T, kf, ku, krstd)
    P.barrier()


def phase_pool(P, C, hT, g_pre, g_post, halo_fill, poolw_d, scale, corr):
    A = C.A
    T = 512
    NX = NT + 16
    A.reset(NT * KC * 4)
    rstdx = A.alloc([128, NX], F32)
    halo = A.alloc([128, KC, 16], F32)
    mixed = A.alloc([128, KC, NT], BF16)
    f = A.alloc([128, KC, T], F32)
    sq = A.alloc([128, KC, T], BF16)
    bufsets = [[A.alloc([128, NX], F32) for _ in range(3)] for _ in range(2)]
    smalls = [A.alloc([128, 16], F32) for _ in range(2)]
    wp = A.alloc([128, 4, 4, 512], BF16)
    rstd = A.alloc([128, T], F32)
    khalo, ksq, krx, kf, kwp, krstd = (C.key(n) for n in ("halo", "sq", "rstdx", "f", "wp", "rstd"))
    kmixs = [C.key("mixed%d" % c) for c in range(KC)]
    kbs = [[C.key("b%d" % i) for i in range(3)] for _ in range(2)]
    ksms = [C.key("small0"), C.key("small1")]
    stage16 = A.alloc([128, KC, 16], F32)
    halo_fill(halo, khalo, stage16, C.key("st16"))
    for g in range(4):
        P.dma("pool", wp[:, g, :, :], poolw_d[g], writes=[kwp])
    P.op("act", lambda e: e.activation(sq[:, :, 0:16], halo, AF.Square), reads=[khalo], writes=[ksq])
    emit_rstd(P, C, sq[:, :, 0:16], KC, 16, rstdx[:, 0:16], 7, D_MODEL, ksq, krx)
    for t0 in range(0, NT, T):
        P.op("act", lambda e, t0=t0: e.activation(sq, hT[:, :, t0:t0 + T], AF.Square),
             reads=[("h",)], writes=[ksq])
        emit_rstd(P, C, sq, KC, T, rstdx[:, 16 + t0:16 + t0 + T], 7, D_MODEL, ksq, krx)
    def chunk_ops(c, ei):
        g = c // 4
        w = 2 ** (g + 1)
        a, b1, b2 = bufsets[ei]
        ka, k1, k2 = kbs[ei]
        small, ksm = smalls[ei], ksms[ei]
        kmix = kmixs[c]
        ops = []
        ops.append(lambda: P.op("dve", lambda e: e.scalar_tensor_tensor(
            a[:, 0:16], halo[:, c, :], g_pre[:, c:c + 1], rstdx[:, 0:16], ALU.mult, ALU.mult),
            reads=[khalo, krx, ("cst",)], writes=[ka]))
        ops.append(lambda: P.op("dve", lambda e: e.scalar_tensor_tensor(
            a[:, 16:NX], hT[:, c, :], g_pre[:, c:c + 1], rstdx[:, 16:NX], ALU.mult, ALU.mult),
            reads=[("h",), krx, ("cst",)], writes=[ka]))
        cur, kcur = a, ka
        nxt = [(b1, k1), (b2, k2)]
        for s_ in range(g + 1):
            sh = 2 ** s_
            lo = 2 ** (s_ + 1) - 1
            dst, kd = nxt[s_ % 2]
            ops.append(lambda cur=cur, dst=dst, sh=sh, lo=lo, kcur=kcur, kd=kd: P.op(
                "dve", lambda e: e.tensor_tensor(dst[:, lo:NX], cur[:, lo:NX], cur[:, lo - sh:NX - sh], ALU.add),
                reads=[kcur], writes=[kd]))
            cur, kcur = dst, kd
        ops.append(lambda cur=cur, kcur=kcur: P.op("dve", lambda e: e.scalar_tensor_tensor(
            mixed[:, c, 16:NT], cur[:, 32:NX], 1.0 / w, a[:, 32:NX], ALU.mult, ALU.subtract),
            reads=[kcur, ka], writes=[kmix]))
        ops.append(lambda cur=cur, kcur=kcur: P.op("dve", lambda e: e.tensor_tensor(
            small, cur[:, 16:32], corr[:, g, :], ALU.mult), reads=[kcur, ("cst",)], writes=[ksm]))
        ops.append(lambda: P.op("dve", lambda e: e.tensor_tensor(
            mixed[:, c, 0:16], small, a[:, 16:32], ALU.subtract), reads=[ksm, ka], writes=[kmix]))
        return ops
    for c in range(0, KC, 2):
        oa, ob_ = chunk_ops(c, 0), chunk_ops(c + 1, 1)
        for i in range(max(len(oa), len(ob_))):
            if i < len(oa):
                oa[i]()
            if i < len(ob_):
                ob_[i]()
    banks = Banks([0, 1, 2, 3])
    for t0 in range(0, NT, T):
        for g in range(4):
            for m in range(4):
                b = banks.next()
                c = g * 4 + m
                for k in range(4):
                    mm(P, C.ps[b][:, 0:T], wp[:, g, k, m * 128:(m + 1) * 128], mixed[:, g * 4 + k, t0:t0 + T],
                       k == 0, k == 3, reads=[kwp, kmixs[g * 4 + k]], writes=[("ps", b)])
                P.op("dve", lambda e, c=c, b=b: e.tensor_scalar(
                    f[:, c, :], C.ps[b][:, 0:T], scale[:, c:c + 1], None, ALU.mult),
                    reads=[("ps", b), ("cst",)], writes=[kf])
        emit_postnorm(P, C, f, hT[:, :, t0:t0 + T], g_post, rstd, sq, T, kf, ksq, krstd)
    P.barrier()


def rot_linear(P, C, w_units, ws_units, kc, tiles, rhs, rkeys, bias, bias_s, cosT, sinT, tmp, ktmp,
               tB, ktB, out, kout, banks, ktab, col0=0):
    mc0 = 0
    cnt = [0]
    for wu, wsu in zip(w_units, ws_units):
        nm = wu.shape[2] // 128

        def cons_a(mc, ti, ps, b, mc0=mc0):
            t0, n = tiles[ti]
            P.op("dve", lambda e: e.scalar_tensor_tensor(
                tmp[:, mc, t0:t0 + n], ps, bias[:, mc0 + mc:mc0 + mc + 1],
                cosT[:, col0 + t0:col0 + t0 + n], ALU.add, ALU.mult),
                reads=[("ps", b), ("cst",), ktab], writes=[ktmp])

        def cons_b(mc, ti, ps, b, mc0=mc0):
            t0, n = tiles[ti]
            o = out(mc0 + mc)
            bi = cnt[0] % 2
            cnt[0] += 1
            P.op("dve", lambda e: e.scalar_tensor_tensor(
                tB[:, bi, 0:n], ps, bias_s[:, mc0 + mc:mc0 + mc + 1],
                sinT[:, col0 + t0:col0 + t0 + n], ALU.add, ALU.mult),
                reads=[("ps", b), ("cst",), ktab], writes=[ktB[bi]])
            P.op("dve", lambda e: e.tensor_tensor(
                o[:, t0:t0 + n], tmp[:, mc, t0:t0 + n], tB[:, bi, 0:n], ALU.add),
                reads=[ktmp, ktB[bi]], writes=[kout])

        linear_fm(P, C, [wu], kc, tiles, rhs, rkeys, cons_a, banks)
        linear_fm(P, C, [wsu], kc, tiles, rhs, rkeys, cons_b, banks)
        mc0 += nm


def rot_linear_perm(P, C, w_units, kc, tiles, rhs, rkeys, bias, cosT, sinT, t1, kt1, qb, kqb, tB, ktB,
                    out, kout, banks, ktab, perm, col0=0):
    cnt = [0]

    def cons(mc, ti, ps, b):
        t0, n = tiles[ti]
        bi = cnt[0] % 2
        cnt[0] += 1
        o = out(mc)
        P.op("dve", lambda e: e.tensor_scalar(qb[:, bi, 0:n], ps, bias[:, mc:mc + 1], None, ALU.add),
             reads=[("ps", b), ("cst",)], writes=[kqb[bi]])
        P.op("dve", lambda e: e.scalar_tensor_tensor(
            t1[:, bi, 0:n], ps, bias[:, mc:mc + 1], cosT[:, col0 + t0:col0 + t0 + n], ALU.add, ALU.mult),
            reads=[("ps", b), ("cst",), ktab], writes=[kt1[bi]])
        b2 = banks.next()
        mm(P, C.ps[b2][:, 0:n], perm, qb[:, bi, 0:n], True, True, reads=[kqb[bi], ("ident",)],
           writes=[("ps", b2)])
        P.op("dve", lambda e: e.tensor_tensor(tB[:, bi, 0:n], C.ps[b2][:, 0:n],
                                              sinT[:, col0 + t0:col0 + t0 + n], ALU.mult),
             reads=[("ps", b2), ktab], writes=[ktB[bi]])
        P.op("dve", lambda e: e.tensor_tensor(o[:, t0:t0 + n], t1[:, bi, 0:n], tB[:, bi, 0:n], ALU.add),
             reads=[kt1[bi], ktB[bi]], writes=[kout])
    linear_fm(P, C, w_units, kc, tiles, rhs, rkeys, cons, banks)


def phase_swa(P, C, hT, g_pre, g_post, D):
    A = C.A
    T = 512
    NX = NT + 128
    tiles_x = [(0, 512), (512, 512), (1024, 128)]
    tiles_o = [(0, 512), (512, 512)]
    A.reset(NT * KC * 4)
    u = A.alloc([128, KC, NX], BF16)
    QA = A.alloc([128, KC, NT], BF16)
    cosT = A.alloc([128, NX], F32)
    sinT = A.alloc([128, NX], F32)
    kv_flat = A.alloc([128, 4 * NX + 9 * 512], BF16)
    KT = kv_flat[:, 0:4 * NX].rearrange("p (a b) -> p a b", a=4)
    V = kv_flat[:, 4 * NX:4 * NX + 9 * 512].rearrange("p (a b) -> p a b", a=9)
    sq2 = kv_flat[:, 0:KC * T].rearrange("p (a b) -> p a b", a=KC)
    Pm = A.alloc([128, 2, 2, 512], BF16)
    rden = A.alloc([128, 2, 512], F32)
    masks = A.alloc([128, 3, 4, 128], BF16)
    esink = A.alloc([128, 32], F32)
    tab = A.alloc([128, 512 + 384 + 32], F32)
    rstd = A.alloc([128, T], F32)
    t1 = A.alloc([128, 2, 512], F32)
    qb = A.alloc([128, 2, 512], BF16)
    tB = A.alloc([128, 2, 512], F32)
    ktB = [C.key("tB"), C.key("tB")]
    kt1 = [C.key("t1"), C.key("t1")]
    kqb = [C.key("qb"), C.key("qb")]
    C.make_ring(2, 2048)
    ku, kqa, ktab, kkt, kv, krstd, ktq, kmask, ktb = (C.key(n) for n in
        ("u", "qa", "tab", "kt", "v", "rstd", "tq", "mask", "tb"))
    kpm = [[C.key("pm"), C.key("pm")], [C.key("pm"), C.key("pm")]]
    krd = [C.key("rd"), C.key("rd")]
    mark = A.off
    A.reset(NT * KC * 4 + NX * KC * 2)
    halo = A.alloc([128, KC, 128], F32)
    posi = A.alloc([128, NX], I32)
    ang = A.alloc([128, NX], F32)
    tmp = A.alloc([128, NX], F32)
    sq = A.alloc([128, KC, 128], BF16)
    A.reset(mark)
    khalo, kpos, kang, ktmp, ksq = (C.key(n) for n in ("halo", "posi", "ang", "tmp", "sq"))
    stage128 = kv_flat.bitcast(F32)[:, 0:KC * 128].rearrange("p (a b) -> p a b", a=KC)
    D["halo_fill"](halo, khalo, stage128, kkt)
    P.dma("sp", posi, D["posx"], writes=[kpos])
    P.dma("sp", tab, D["tab"], writes=[ktb])
    P.op("dve", lambda e: e.tensor_copy(ang, posi), reads=[kpos], writes=[kang])
    P.op("dve", lambda e: e.tensor_scalar(ang, ang, D["invf"], None, ALU.mult),
         reads=[kang, ("cst",)], writes=[kang])
    emit_sincos(P, C, ang, tmp, sinT, cosT, NX, kang, ktmp, ktab, ktab)
    P.op("dve", lambda e: e.tensor_scalar(sinT, sinT, D["sgn"], None, ALU.mult),
         reads=[ktab, ("cst",)], writes=[ktab])
    for mi_ in range(3):
        P.op("dve", lambda e, mi_=mi_: e.tensor_scalar(
            masks[:, mi_, :, :], tab[:, 512 + 128 * mi_:640 + 128 * mi_].unsqueeze(1).to_broadcast([128, 4, 128]),
            -1.0, 30000.0, ALU.add, ALU.mult), reads=[ktb], writes=[kmask])
    P.op("act", lambda e: e.activation(esink, tab[:, 896:928], AF.Exp), reads=[ktb], writes=[kmask])
    P.op("act", lambda e: e.activation(sq, halo, AF.Square), reads=[khalo], writes=[ksq])
    emit_rstd(P, C, sq, KC, 128, rstd[:, 0:128], 7, D_MODEL, ksq, krstd)
    for c in range(KC):
        P.op("dve", lambda e, c=c: e.scalar_tensor_tensor(
            u[:, c, 0:128], halo[:, c, :], g_pre[:, c:c + 1], rstd[:, 0:128], ALU.mult, ALU.mult),
            reads=[khalo, krstd, ("cst",)], writes=[ku])
    for t0 in range(0, NT, T):
        emit_prenorm(P, C, hT[:, :, t0:t0 + T], g_pre, u[:, :, 128 + t0:128 + t0 + T], sq2, rstd, T,
                     kkt, krstd, ku)
    banks = Banks([0, 1, 2, 3])
    base_ring, base_keys = list(C.ring), list(C.ring_keys)
    proj_slots = [Pm.rearrange("p a b c -> p (a b c)")[:, 0:2048],
                  rden.rearrange("p a b -> p (a b)").bitcast(BF16)[:, 0:2048]]
    C.ring = base_ring + proj_slots
    C.ring_keys = base_keys + [C.key("w") for _ in proj_slots]
    rot_linear_perm(P, C, D["wk"], KC, tiles_x,
                    lambda k, ti: u[:, k, tiles_x[ti][0]:tiles_x[ti][0] + tiles_x[ti][1]],
                    [ku], D["bk"], cosT, sinT, t1, kt1, qb, kqb, tB, ktB, lambda mc: KT[:, mc, :], kkt, banks, ktab,
                    D["perm"])
    def cons_v(ui, ti, ps, b):
        P.op("dve", lambda e: e.tensor_tensor(V[:, ti, ui * 128:(ui + 1) * 128], ps,
                                              tab[:, ui * 128:(ui + 1) * 128], ALU.add),
             reads=[("ps", b), ktb], writes=[kv])
    linear_tm(P, C, D["wv"], KC, 9, lambda k, ti: u[:, k, ti * 128:(ti + 1) * 128], [ku], cons_v, banks)
    P.barrier()
    rot_linear_perm(P, C, D["wq"], KC, tiles_o,
                    lambda k, ti: u[:, k, 128 + tiles_o[ti][0]:128 + tiles_o[ti][0] + 512],
                    [ku], D["bq"], cosT, sinT, t1, kt1, qb, kqb, tB, ktB, lambda mc: QA[:, mc, :], kqa, banks, ktab,
                    D["perm"], col0=128)
    C.ring, C.ring_keys, C.ring_i = base_ring, base_keys, 0
    P.barrier()
    def stage1(kh, n, par):
        r0 = par * 64
        bS = [4 * par, 4 * par + 1]
        for wi, kt in enumerate((n, n + 1)):
            mi = (2 if n == 0 else 1) if wi == 0 else 0
            mm(P, C.ps[bS[wi]][:, 0:512], KT[r0:r0 + 64, kh, kt * 128:(kt + 1) * 128],
               QA[r0:r0 + 64, 4 * kh:4 * kh + 4, n * 128:(n + 1) * 128], True, False,
               reads=[kkt, kqa, ("qa", kh, n, par)], writes=[("ps", bS[wi])], inc=False)
            mm(P, C.ps[bS[wi]][:, 0:512], C.ident[:], masks[:, mi, :, :], False, True,
               reads=[("ident",), kmask], writes=[("ps", bS[wi])])
            P.op("act", lambda e, wi=wi, par=par, b=bS[wi]: e.activation(
                Pm[:, par, wi, :], C.ps[b][:, 0:512], AF.Exp, scale=0.125),
                reads=[("ps", bS[wi])], writes=[kpm[par][wi]])

    def stage2(kh, n, par):
        r0 = par * 64
        bO, bD = 4 * par + 2, 4 * par + 3
        for wi, kt in enumerate((n, n + 1)):
            mm(P, C.ps[bO][:, 0:512], V[:, kt, kh * 128:(kh + 1) * 128], Pm[:, par, wi, :],
               wi == 0, wi == 1, reads=[kv, kpm[par][wi]], writes=[("ps", bO)])
        for wi in range(2):
            mm(P, C.ps[bD][:, 0:512], C.ones[:], Pm[:, par, wi, :],
               wi == 0, wi == 1, reads=[("ones",), kpm[par][wi]], writes=[("ps", bD)])
        for gi in range(4):
            hq = kh * 8 + 2 * gi + par
            P.op("act", lambda e, gi=gi, hq=hq: e.activation(
                rden[:, par, gi * 128:(gi + 1) * 128], C.ps[bD][:, gi * 128:(gi + 1) * 128],
                AF.Identity, bias=esink[:, hq:hq + 1]),
                reads=[("ps", bD), kmask], writes=[krd[par]])
        P.op("dve", lambda e: e.reciprocal(rden[:, par, :], rden[:, par, :]),
             reads=[krd[par]], writes=[krd[par]])
        P.op("dve", lambda e: e.tensor_tensor(
            QA[r0:r0 + 64, 4 * kh:4 * kh + 4, n * 128:(n + 1) * 128],
            C.ps[bO][r0:r0 + 64, 0:512].rearrange("p (a b) -> p a b", a=4),
            rden[r0:r0 + 64, par, :].rearrange("p (a b) -> p a b", a=4), ALU.mult),
            reads=[("ps", bO), krd[par]], writes=[("qa", kh, n, par)])

    its = [(kh, n, par) for kh in range(4) for n in range(8) for par in range(2)]
    stage1(*its[0])
    for i in range(len(its)):
        if i + 1 < len(its):
            stage1(*its[i + 1])
        stage2(*its[i])
    P.barrier()
    A2 = A.off
    A.reset(NT * KC * 4)
    f = A.alloc([128, KC, T], F32)
    A.reset(A2)
    kf = C.key("f")
    sqo = sq2
    extra_slots = [cosT.bitcast(BF16)[:, 0:2048], sinT.bitcast(BF16)[:, 0:2048],
                   t1.rearrange("p a b -> p (a b)").bitcast(BF16)[:, 0:2048],
                   tB.rearrange("p a b -> p (a b)").bitcast(BF16)[:, 0:2048]]
    C.ring = list(C.ring) + extra_slots
    C.ring_keys = list(C.ring_keys) + [C.key("w") for _ in extra_slots]
    for t0 in range(0, NT, T):
        def cons_o(mc, ti, ps, b):
            P.op("act", lambda e: e.activation(f[:, mc, :], ps, AF.Identity, bias=D["bo"][:, mc:mc + 1]),
                 reads=[("ps", b), ("cst",)], writes=[kf])
        linear_fm(P, C, D["wo"], KC, [(t0, T)], lambda k, ti, t0=t0: QA[:, k, t0:t0 + T], [kqa], cons_o, banks)
        emit_postnorm(P, C, f, hT[:, :, t0:t0 + T], g_post, rstd, sqo, T, kf, kkt, krstd)
    P.barrier()


RET_H = 8
GAMMAS = [1.0 - 2.0 ** (-5.0 - h) for h in range(RET_H)]


def ret_tables(P, C, D, cosT, sinT, ang, tmp, posi, ktab):
    kpos, kang, ktmp = C.key("posi"), C.key("ang"), C.key("tmp")
    P.dma("sp", posi, D["posr"], writes=[kpos])
    P.op("dve", lambda e: e.tensor_copy(ang, posi), reads=[kpos], writes=[kang])
    P.op("dve", lambda e: e.tensor_scalar(ang, ang, D["invf_ret"], None, ALU.mult),
         reads=[kang, ("cst",)], writes=[kang])
    emit_sincos(P, C, ang, tmp, sinT, cosT, NT, kang, ktmp, ktab, ktab)


def ret_rotary(P, C, scr, kscr, cosT, sinT, ktab, scale, out, kout):
    x1, x2, t1, t2 = scr[:, 0, :], scr[:, 1, :], scr[:, 2, :], scr[:, 3, :]
    for (a, b, o, op) in ((x1, x2, out[:, 0, :], ALU.subtract), (x2, x1, out[:, 1, :], ALU.add)):
        P.op("dve", lambda e, a=a: e.scalar_tensor_tensor(t1, a, scale, cosT, ALU.mult, ALU.mult),
             reads=[kscr, ktab], writes=[kscr])
        P.op("dve", lambda e, b=b: e.scalar_tensor_tensor(t2, b, scale, sinT, ALU.mult, ALU.mult),
             reads=[kscr, ktab], writes=[kscr])
        P.op("dve", lambda e, o=o, op=op: e.tensor_tensor(o, t1, t2, op), reads=[kscr], writes=[kout])


def ret_head_kv(P, C, D, h, u, ku, scr, kscr, cosT, sinT, ktab, KT, kkt, Ktm, kktm, V, kv, dcol, banks,
                load=None):
    tiles = [(0, 512), (512, 512)]
    W = D["ret_w"]
    if load is not None:
        load(h, KT, kkt, V, kv)
    else:
        def cons_raw(mc, ti, ps, b):
            t0, n = tiles[ti]
            P.op("act", lambda e: e.activation(scr[:, mc, t0:t0 + n], ps, AF.Copy),
                 reads=[("ps", b)], writes=[kscr])
        linear_fm(P, C, [W[h, 1]], KC, tiles, lambda k, ti: u[:, k, tiles[ti][0]:tiles[ti][0] + 512], [ku],
                  cons_raw, banks)
        ret_rotary(P, C, scr, kscr, cosT, sinT, ktab, 256.0 ** -0.5, KT, kkt)
    if load is None:
        def cons_v(ui, ti, ps, b):
            P.op("act", lambda e: e.activation(V[:, ti, ui * 256:(ui + 1) * 256], ps, AF.Copy),
                 reads=[("ps", b)], writes=[kv])
        linear_tm(P, C, [W[h, 2], W[h, 3]], KC, 8, lambda k, ti: u[:, k, ti * 128:(ti + 1) * 128], [ku],
                  cons_v, banks)
    for ti in range(8):
        b = banks.next()
        for a in range(2):
            mm(P, C.ps[b][:, a * 128:(a + 1) * 128], KT[:, a, ti * 128:(ti + 1) * 128], C.ident[:],
               True, True, reads=[kkt, ("ident",)], writes=[("ps", b)], inc=(a == 1))
        P.op("dve", lambda e, ti=ti, b=b: e.tensor_scalar(
            Ktm[:, ti, :], C.ps[b][:, 0:256], dcol(ti), None, ALU.mult),
            reads=[("ps", b), ("cst",)], writes=[kktm])


def phase_retA(P, C, hT, g_pre, D):
    A = C.A
    T = 512
    A.reset(NT * KC * 4)
    u = A.alloc([128, KC, NT], BF16)
    cosT = A.alloc([128, NT], F32)
    sinT = A.alloc([128, NT], F32)
    scr = A.alloc([128, 4, NT], F32)
    KTs = [A.alloc([128, 2, NT], BF16) for _ in range(2)]
    Ktm = A.alloc([128, 8, 256], BF16)
    Vs = [A.alloc([128, 8, 512], BF16) for _ in range(2)]
    Sst = A.alloc([128, 2, 512], F32)
    rstd = A.alloc([128, T], F32)
    sq = A.alloc([128, KC, T], BF16)
    posi = A.alloc([128, NT], I32)
    C.make_ring(2)
    ku, ktab, kscr, kktm, kS, krstd, ksq = (C.key(n) for n in
        ("u", "tab", "scr", "ktm", "S", "rstd", "sq"))
    kkts = [C.key("kt0"), C.key("kt1")]
    kvs = [C.key("v0"), C.key("v1")]
    ret_tables(P, C, D, cosT, sinT, scr[:, 0, :], scr[:, 1, :], posi, ktab)
    for t0 in range(0, NT, T):
        emit_prenorm(P, C, hT[:, :, t0:t0 + T], g_pre, u[:, :, t0:t0 + T], sq, rstd, T, ksq, krstd, ku)
    banks = Banks([0, 1, 2, 3])
    pending_coll = None
    for h in range(RET_H):
        KT, kkt, V, kv = KTs[h % 2], kkts[h % 2], Vs[h % 2], kvs[h % 2]
        ret_head_kv(P, C, D, h, u, ku, scr, kscr, cosT, sinT, ktab, KT, kkt, Ktm, kktm, V, kv,
                    lambda ti, h=h: D["dloc"][:, ti, h:h + 1], banks)
        for a in range(2):
            b = 4 + a
            for ti in range(8):
                mm(P, C.ps[b][:, 0:512], Ktm[:, ti, a * 128:(a + 1) * 128], V[:, ti, :], ti == 0, ti == 7,
                   reads=[kktm, kv], writes=[("ps", b)])
            P.op("act", lambda e, a=a, b=b: e.activation(Sst[:, a, :], C.ps[b][:, 0:512], AF.Copy),
                 reads=[("ps", b)], writes=[kS])
        D["sloc_store"](h, Sst, kS)
        if D.get("kv_store") is not None:
            D["kv_store"](h, KT, kkt, V, kv)
        if pending_coll is not None:
            pending_coll()
        pending_coll = (lambda h=h: D["sloc_coll"](h)) if D.get("sloc_coll") is not None else None
    if pending_coll is not None:
        pending_coll()
    D["retA_state"] = {"u": u, "ku": ku, "cosT": cosT, "sinT": sinT, "ktab": ktab}
    P.barrier(skip_pool=bool(D.get("kv_store")))


def phase_retB(P, C, hT, g_pre, g_post, D, load_h):
    A = C.A
    T = 512
    GOFF = ARENA_BYTES - 2 * KC * NT * 2
    A.reset(GOFF)
    gated_flat = A.alloc([128, 2 * KC * NT], BF16)
    gated = gated_flat.rearrange("p (a b) -> p a b", a=2 * KC)
    hin = gated_flat.bitcast(F32).rearrange("p (a b) -> p a b", a=KC)
    reuse = D.get("reuse")
    if reuse is not None:
        A.reset(NT * KC * 4)
    else:
        A.reset(0)
    u = A.alloc([128, KC, NT], BF16)
    cosT = A.alloc([128, NT], F32)
    sinT = A.alloc([128, NT], F32)
    scr_flat = A.alloc([128, 4 * NT], F32)
    scr = scr_flat.rearrange("p (a b) -> p a b", a=4)
    sq_tmp = scr_flat.bitcast(BF16)[:, 0:KC * T].rearrange("p (a b) -> p a b", a=KC)
    ob = scr_flat.bitcast(BF16)[:, 0:4 * T].rearrange("p (a b) -> p a b", a=4)
    osq = scr_flat.bitcast(BF16)[:, 4 * T:8 * T].rearrange("p (a b) -> p a b", a=4)
    posi_t = scr_flat.bitcast(I32)[:, 2 * NT:3 * NT]
    C.make_ring(2)
    if reuse is not None:
        assert A.off <= GOFF, (A.off, GOFF)
        A.reset(0)
    QT = A.alloc([128, 2, NT], BF16)
    KT = A.alloc([128, 2, NT], BF16)
    Ktm = A.alloc([128, 8, 256], BF16)
    V = A.alloc([128, 8, 512], BF16)
    G = A.alloc([128, 4, NT], BF16)
    oh = A.alloc([128, 4, T], F32)
    S = A.alloc([128, 2, 512], F32)
    Sbf2 = A.alloc([128, 2, 2, 512], BF16)
    dtab = A.alloc([128, 2, 128], F32)
    Sd = A.alloc([128, 2, 128], BF16)
    Qc = A.alloc([128, 2, 2, 128], BF16)
    gnt = A.alloc([128, 4, T], F32)
    kgs = kscr_gn = None
    if reuse is not None:
        ob = A.alloc([128, 4, T], BF16)
        osq = A.alloc([128, 4, T], BF16)
        rstd = None
        assert A.off <= NT * KC * 4, A.off
    else:
        rstd = A.alloc([128, T], F32)
        assert A.off <= GOFF, (A.off, GOFF)
    ku, ktab, kscr, kqt, kkt, kktm, kv, kg, koh, kS, kSb, kdt, kgn, krstd, kgated = (C.key(n) for n in
        ("u", "tab", "scr", "qt", "kt", "ktm", "v", "g", "oh", "S", "Sb", "dt", "gn", "rstd", "gated"))
    ksd = [C.key("sd"), C.key("sd")]
    kqc = [C.key("qc"), C.key("qc")]
    kSbs = [C.key("Sb0"), C.key("Sb1")]
    kob = C.key("ob") if reuse is not None else kscr
    if reuse is None:
        load_h(hin)
        ret_tables(P, C, D, cosT, sinT, scr[:, 0, :], scr[:, 1, :], posi_t, ktab)
        P.barrier()
        for t0 in range(0, NT, T):
            emit_prenorm(P, C, hin[:, :, t0:t0 + T], g_pre, u[:, :, t0:t0 + T], sq_tmp, rstd, T, kscr, krstd, ku)
        P.barrier()
    banks = Banks([0, 1, 2, 3])
    tiles = [(0, 512), (512, 512)]
    W = D["ret_w"]
    stage = scr[:, 2:4, 0:512]
    for h in range(RET_H):
        gam = GAMMAS[h]
        for c in range(4):
            P.dma("sp", stage, D["sall_ap"](c, h), reads=D["sall_keys"](h), writes=[kscr])
            if c == 0:
                P.op("dve", lambda e, c=c, h=h: e.tensor_scalar(
                    S, stage, D["coef"][:, c * 8 + h:c * 8 + h + 1], None, ALU.mult),
                    reads=[kscr, ("cst",)], writes=[kS])
            else:
                P.op("dve", lambda e, c=c, h=h: e.scalar_tensor_tensor(
                    S, stage, D["coef"][:, c * 8 + h:c * 8 + h + 1], S, ALU.mult, ALU.add),
                    reads=[kscr, ("cst",), kS], writes=[kS])
        P.op("act", lambda e: e.activation(Sbf2[:, 0, :, :], S, AF.Copy), reads=[kS], writes=[kSbs[0]])
        P.dma("sp", dtab, D["dtab"][h], writes=[kdt])
        def cons_raw(mc, ti, ps, b):
            t0, n = tiles[ti]
            P.op("act", lambda e: e.activation(scr[:, mc, t0:t0 + n], ps, AF.Copy),
                 reads=[("ps", b)], writes=[kscr])
        linear_fm(P, C, [W[h, 0]], KC, tiles, lambda k, ti: u[:, k, tiles[ti][0]:tiles[ti][0] + 512], [ku],
                  cons_raw, banks)
        ret_rotary(P, C, scr, kscr, cosT, sinT, ktab, 1.0, QT, kqt)
        ret_head_kv(P, C, D, h, u, ku, scr, kscr, cosT, sinT, ktab, KT, kkt, Ktm, kktm, V, kv,
                    lambda ti, h=h: D["sdcol"][:, h:h + 1], banks, load=D.get("kv_load"))

        def cons_g(mc, ti, ps, b):
            t0, n = tiles[ti]
            P.op("act", lambda e: e.activation(G[:, mc, t0:t0 + n], ps, AF.Silu),
                 reads=[("ps", b)], writes=[kg])
        linear_fm(P, C, [W[h, 4], W[h, 5]], KC, tiles, lambda k, ti: u[:, k, tiles[ti][0]:tiles[ti][0] + 512],
                  [ku], cons_g, banks)
        SB = (4, 3)

        def scores(n):
            cs = slice(n * 128, (n + 1) * 128)
            for a in range(2):
                mm(P, C.ps[SB[n % 2]][:, 0:128], KT[:, a, cs], QT[:, a, cs], a == 0, a == 1,
                   reads=[kkt, kqt], writes=[("ps", SB[n % 2])])
        scores(0)
        for n in range(8):
            cs = slice(n * 128, (n + 1) * 128)
            pp = n % 2
            Sbf = Sbf2[:, pp, :, :]
            P.op("dve", lambda e, pp=pp, n=n: e.tensor_tensor(Sd[:, pp, :], C.ps[SB[n % 2]][:, 0:128],
                                                              dtab[:, 0, :], ALU.mult),
                 reads=[("ps", SB[n % 2]), kdt], writes=[ksd[pp]])
            P.op("dve", lambda e, pp=pp, cs=cs: e.tensor_tensor(
                Qc[:, pp, :, :], QT[:, :, cs], dtab[:, 1, :].unsqueeze(1).to_broadcast([128, 2, 128]), ALU.mult),
                reads=[kqt, kdt], writes=[kqc[pp]])
            if n < 7:
                for a in range(2):
                    mm(P, C.ps[6 + a][:, 0:512], Ktm[:, n, a * 128:(a + 1) * 128], V[:, n, :], True, True,
                       reads=[kktm, kv], writes=[("ps", 6 + a)])
                scores(n + 1)
            for m in range(4):
                ms = slice(m * 128, (m + 1) * 128)
                mm(P, C.ps[5][:, ms], V[:, n, ms], Sd[:, pp, :], True, False,
                   reads=[kv, ksd[pp]], writes=[("ps", 5)], inc=False)
                for a in range(2):
                    mm(P, C.ps[5][:, ms], Sbf[:, a, ms], Qc[:, pp, a, :], False, a == 1,
                       reads=[kSbs[pp], kqc[pp]], writes=[("ps", 5)], inc=(a == 1 and m == 3))
            if n < 7:
                for a in range(2):
                    P.op("dve", lambda e, a=a, gam=gam: e.scalar_tensor_tensor(
                        S[:, a, :], S[:, a, :], gam ** 128, C.ps[6 + a][:, 0:512], ALU.mult, ALU.add),
                        reads=[kS, ("ps", 6 + a)], writes=[kS])
                P.op("act", lambda e, pp=pp: e.activation(Sbf2[:, 1 - pp, :, :], S, AF.Copy),
                     reads=[kS], writes=[kSbs[1 - pp]])
            half = n // 4
            P.op("act", lambda e, n=n: e.activation(
                oh[:, :, (n % 4) * 128:(n % 4 + 1) * 128],
                C.ps[5][:, 0:512].rearrange("p (a b) -> p a b", a=4), AF.Copy),
                reads=[("ps", 5)], writes=[koh])
            if n % 4 == 3:
                t0 = half * T
                P.op("act", lambda e: e.activation(ob, oh, AF.Copy), reads=[koh], writes=[kob])
                P.op("act", lambda e: e.activation(osq, oh, AF.Square), reads=[koh], writes=[kob])
                for m in range(4):
                    mm(P, C.ps[6][:, 0:T], C.ones[:], ob[:, m, :], m == 0, m == 3,
                       reads=[kob, ("ones",)], writes=[("ps", 6)])
                for m in range(4):
                    mm(P, C.ps[7][:, 0:T], C.ones[:], osq[:, m, :], m == 0, m == 3,
                       reads=[kob, ("ones",)], writes=[("ps", 7)])
                mean, var, tt_ = gnt[:, 0, :], gnt[:, 1, :], gnt[:, 2, :]
                P.op("dve", lambda e: e.tensor_scalar(mean, C.ps[6][:, 0:T], 1.0 / 512, None, ALU.mult),
                     reads=[("ps", 6)], writes=[kgn])
                P.op("dve", lambda e: e.tensor_tensor(tt_, mean, mean, ALU.mult), reads=[kgn], writes=[kgn])
                P.op("dve", lambda e: e.scalar_tensor_tensor(var, C.ps[7][:, 0:T], 1.0 / 512, tt_,
                                                             ALU.mult, ALU.subtract),
                     reads=[("ps", 7), kgn], writes=[kgn])
                P.op("act", lambda e: e.activation(var, var, AF.Sqrt, bias=C.eps[:, 0:1]),
                     reads=[kgn, ("ones",)], writes=[kgn])
                P.op("dve", lambda e: e.reciprocal(var, var), reads=[kgn], writes=[kgn])
                for m in range(4):
                    c = h * 4 + m
                    P.op("dve", lambda e, m=m: e.tensor_tensor(tt_, oh[:, m, :], mean, ALU.subtract),
                         reads=[koh, kgn], writes=[kgn])
                    P.op("dve", lambda e: e.tensor_tensor(tt_, tt_, var, ALU.mult), reads=[kgn], writes=[kgn])
                    P.op("dve", lambda e, c=c: e.tensor_scalar(
                        tt_, tt_, D["gn_g"][:, c:c + 1], D["gn_b"][:, c:c + 1], ALU.mult, ALU.add),
                        reads=[kgn, ("cst",)], writes=[kgn])
                    P.op("dve", lambda e, m=m, c=c, t0=t0: e.tensor_tensor(
                        gated[:, c, t0:t0 + T], tt_, G[:, m, t0:t0 + T], ALU.mult),
                        reads=[kgn, kg], writes=[kgated])
    P.barrier()
    A.reset(0)
    hT2 = A.alloc([128, KC, NT], F32)
    f = A.alloc([128, KC, T], F32)
    sqo = A.alloc([128, KC, T], BF16)
    rstd2 = A.alloc([128, T], F32)
    C.make_ring(2)
    assert A.off <= GOFF, (A.off, GOFF)
    load_h(hT2)
    kf, ksqo, kr2 = C.key("f"), C.key("sqo"), C.key("r2")
    for t0 in range(0, NT, T):
        def cons_o(mc, ti, ps, b):
            P.op("act", lambda e: e.activation(f[:, mc, :], ps, AF.Copy), reads=[("ps", b)], writes=[kf])
        linear_fm(P, C, D["ret_wo"], 2 * KC, [(t0, T)], lambda k, ti, t0=t0: gated[:, k, t0:t0 + T], [kgated],
                  cons_o, banks)
        emit_postnorm(P, C, f, hT2[:, :, t0:t0 + T], g_post, rstd2, sqo, T, kf, ksqo, kr2)
    P.barrier()
    return hT2


def phase_ffn_seq(P, C, hT, ffns, hook=None, first=False):
    A = C.A
    T = 512
    if not first:
        P.barrier()
    A.reset(NT * KC * 4)
    u = A.alloc([128, KC, T], BF16)
    hid = A.alloc([128, FC, T], BF16)
    f = A.alloc([128, KC, T], F32)
    sg = A.alloc([128, 2, T], F32)
    rstdA = A.alloc([128, T], F32)
    rstdB = A.alloc([128, T], F32)
    sqA = A.alloc([128, 2, 2, T], BF16)
    sqB = A.alloc([128, 2, 2, T], BF16)
    C.make_ring(3)
    ku, khid, krA, krB = C.key("u"), C.key("hid"), C.key("rA"), C.key("rB")
    kf = [C.key("f%d" % c) for c in range(KC)]
    ksg = [C.key("sg0"), C.key("sg1")]
    ksqA = [C.key("sqA0"), C.key("sqA1")]
    ksqB = [C.key("sqB0"), C.key("sqB1")]
    items = [(fi, ti) for fi in range(len(ffns)) for ti in (1, 0)]
    K_ = len(items)
    NB = 7

    def hkey(ti):
        return ("h", ti)

    def hs_of(k):
        return hT[:, :, items[k][1] * T:(items[k][1] + 1) * T]

    def stats_pre_ops(k):
        ops = []
        hs = hs_of(k)
        hk = hkey(items[k][1])
        for r in range(KC // 2):
            pp = r % 2

            def sq_op(r=r, pp=pp):
                P.op("act", lambda e: e.activation(sqA[:, pp, :, :], hs[:, 2 * r:2 * r + 2, :], AF.Square),
                     reads=[hk], writes=[ksqA[pp]])

            def mm_op(r=r, pp=pp):
                for i in range(2):
                    c = 2 * r + i
                    mm(P, C.ps[NB][:, 0:T], C.ones[:], sqA[:, pp, i, :], c == 0, c == KC - 1,
                       reads=[ksqA[pp], ("ones",)], writes=[("ps", NB)], inc=(i == 1))
            ops += [sq_op, mm_op]

        def fin():
            P.op("act", lambda e: e.activation(rstdA, C.ps[NB][:, 0:T], AF.Sqrt, bias=C.eps[:, 0:1],
                                               scale=1.0 / D_MODEL),
                 reads=[("ps", NB), ("ones",)], writes=[krA])
            P.op("dve", lambda e: e.reciprocal(rstdA, rstdA), reads=[krA], writes=[krA])
        ops.append(fin)
        return ops

    def write_u(k):
        hs = hs_of(k)
        g1 = ffns[items[k][0]][0]
        for c in range(KC):
            P.op("dve", lambda e, c=c: e.scalar_tensor_tensor(
                u[:, c, :], hs[:, c, :], g1[:, c:c + 1], rstdA, ALU.mult, ALU.mult),
                reads=[hkey(items[k][1]), krA, ("cst",)], writes=[ku])

    def post_apply_ops(k):
        hs = hs_of(k)
        hk = hkey(items[k][1])
        g2h = ffns[items[k][0]][1]
        ops = []

        def fin():
            P.op("act", lambda e: e.activation(rstdB, C.ps[NB][:, 0:T], AF.Sqrt, bias=C.eps[:, 0:1],
                                               scale=1.0 / D_MODEL),
                 reads=[("ps", NB), ("ones",)], writes=[krB])
            P.op("dve", lambda e: e.reciprocal(rstdB, rstdB), reads=[krB], writes=[krB])
        ops.append(fin)
        for c in range(KC):
            def ap(c=c):
                P.op("dve", lambda e: e.tensor_tensor(f[:, c, :], f[:, c, :], rstdB, ALU.mult),
                     reads=[kf[c], krB], writes=[kf[c]])
                P.op("dve", lambda e: e.scalar_tensor_tensor(
                    hs[:, c, :], f[:, c, :], g2h[:, c:c + 1], hs[:, c, :], ALU.mult, ALU.add),
                    reads=[kf[c], ("cst",), hk], writes=[hk])
            ops.append(ap)
        return ops

    hs0, hk0 = hs_of(0), hkey(items[0][1])
    sqbig = hid[:, 0:KC, :]
    P.op("act", lambda e: e.activation(sqbig, hs0, AF.Square), reads=[hk0], writes=[khid])
    for c in range(KC):
        mm(P, C.ps[NB][:, 0:T], C.ones[:], sqbig[:, c, :], c == 0, c == KC - 1,
           reads=[khid, ("ones",)], writes=[("ps", NB)])
    P.op("act", lambda e: e.activation(rstdA, C.ps[NB][:, 0:T], AF.Sqrt, bias=C.eps[:, 0:1],
                                       scale=1.0 / D_MODEL),
         reads=[("ps", NB), ("ones",)], writes=[krA])
    P.op("dve", lambda e: e.reciprocal(rstdA, rstdA), reads=[krA], writes=[krA])
    write_u(0)
    for k in range(K_):
        fi, ti = items[k]
        g1, g2h, win_d, wout_d = ffns[fi]
        extras = []
        if k >= 1:
            extras += post_apply_ops(k - 1)
            if hook is not None and k == K_ - 1:
                extras.append(lambda: hook(hkey(items[k - 1][1])))
        if k + 1 < K_:
            extras += stats_pre_ops(k + 1)
        ei = 0
        for j in range(FC):
            wk, slot = C.next_slot()
            P.dma("pool", slot[:, 0:KC * 256], win_d[j].rearrange("p k c -> p (k c)"), writes=[wk])
            sv = slot[:, 0:KC * 256].rearrange("p (k c) -> p k c", k=KC)
            pair = j % 2
            bg, bu = 2 * pair, 2 * pair + 1
            for kk in range(KC):
                mm(P, C.ps[bg][:, 0:T], sv[:, kk, 0:128], u[:, kk, :], kk == 0, kk == KC - 1,
                   reads=[wk, ku], writes=[("ps", bg)])
            for kk in range(KC):
                mm(P, C.ps[bu][:, 0:T], sv[:, kk, 128:256], u[:, kk, :], kk == 0, kk == KC - 1,
                   reads=[wk, ku], writes=[("ps", bu)])
            P.op("act", lambda e, bg=bg, pair=pair: e.activation(sg[:, pair, :], C.ps[bg][:, 0:T], AF.Silu),
                 reads=[("ps", bg)], writes=[ksg[pair]])
            P.op("dve", lambda e, bu=bu, pair=pair, j=j: e.tensor_tensor(
                hid[:, j, :], sg[:, pair, :], C.ps[bu][:, 0:T], ALU.mult),
                reads=[ksg[pair], ("ps", bu)], writes=[khid])
            for _ in range(2):
                if ei < len(extras) and j >= 1:
                    extras[ei]()
                    ei += 1
        while ei < len(extras):
            extras[ei]()
            ei += 1
        if k + 1 < K_:
            write_u(k + 1)
        unit = 16
        pend = None
        for dg in range(KC // 2):
            par = dg % 2
            banks = (4, 5) if par == 0 else (6, 3)
            k0 = 0
            first = True
            while k0 < FC:
                nk = min(unit, FC - k0)
                wk, slot = C.next_slot()
                P.dma("pool", slot[:, 0:nk * 256],
                      wout_d[dg, :, k0:k0 + nk, :].rearrange("p k c -> p (k c)"), writes=[wk])
                sv = slot[:, 0:nk * 256].rearrange("p (k c) -> p k c", k=nk)
                for m in range(2):
                    for kk in range(nk):
                        kc_ = k0 + kk
                        mm(P, C.ps[banks[m]][:, 0:T], sv[:, kk, m * 128:(m + 1) * 128], hid[:, kc_, :],
                           kc_ == 0, kc_ == FC - 1, reads=[wk, khid], writes=[("ps", banks[m])],
                           inc=(kk == nk - 1))
                k0 += nk
                if first and pend is not None:
                    pend()
                    pend = None
                first = False
            pp = dg % 2
            for m in range(2):
                c = dg * 2 + m
                P.op("act", lambda e, c=c, b=banks[m]: e.activation(f[:, c, :], C.ps[b][:, 0:T], AF.Copy),
                     reads=[("ps", banks[m])], writes=[kf[c]])
                P.op("act", lambda e, m=m, pp=pp, b=banks[m]: e.activation(sqB[:, pp, m, :], C.ps[b][:, 0:T],
                                                                           AF.Square),
                     reads=[("ps", banks[m])], writes=[ksqB[pp]])

            def ones_mm(dg=dg, pp=pp):
                for m in range(2):
                    c = dg * 2 + m
                    mm(P, C.ps[NB][:, 0:T], C.ones[:], sqB[:, pp, m, :], c == 0, c == KC - 1,
                       reads=[ksqB[pp], ("ones",)], writes=[("ps", NB)], inc=(m == 1))
            pend = ones_mm
        pend()
    for op_ in post_apply_ops(K_ - 1):
        op_()
    P.barrier()


def linear_fm_pieces(P, C, wu, kc, tiles, rhs, rkeys, consume, banks, mc0=0):
    UC = wu.shape[2]
    state = {}

    def load():
        wk, slot = C.next_slot()
        P.dma("pool", slot[:, 0:kc * UC], wu.rearrange("p k c -> p (k c)"), writes=[wk])
        state["wk"] = wk
        state["sv"] = slot[:, 0:kc * UC].rearrange("p (k c) -> p k c", k=kc)
    pieces = []
    first = True
    for m in range(UC // 128):
        for ti, (t0, n) in enumerate(tiles):
            def piece(m=m, ti=ti, n=n, first=first):
                if first:
                    load()
                b = banks.next()
                for k in range(kc):
                    mm(P, C.ps[b][:, 0:n], state["sv"][:, k, m * 128:(m + 1) * 128], rhs(k, ti),
                       k == 0, k == kc - 1, reads=[state["wk"]] + rkeys, writes=[("ps", b)])
                consume(mc0 + m, ti, C.ps[b][:, 0:n], b)
            pieces.append(piece)
            first = False
    return pieces


def phase_retB2(P, C, hT, g_post, D, load_h):
    A = C.A
    T = 512
    A.reset(NT * KC * 4)
    u = A.alloc([128, KC, NT], BF16)
    cosT = A.alloc([128, NT], F32)
    sinT = A.alloc([128, NT], F32)
    scr = A.alloc([128, 4, NT], F32)
    C.make_ring(2)
    GOFF = A.off

    def alloc_set():
        return {"QT": A.alloc([128, 2, NT], BF16), "KT": A.alloc([128, 2, NT], BF16),
                "Ktm": A.alloc([128, 8, 256], BF16), "V": A.alloc([128, 8, 512], BF16),
                "G": A.alloc([128, 4, NT], BF16),
                "kqt": C.key("qt"), "kkt": C.key("kt"), "kktm": C.key("ktm"), "kv": C.key("v"), "kg": C.key("g")}
    sets = [None, alloc_set()]
    gst = A.alloc([128, 2, 4, T], BF16)
    ob = A.alloc([128, 4, T], BF16)
    osq = A.alloc([128, 4, T], BF16)
    A.reset(0)
    sets[0] = alloc_set()
    oh = A.alloc([128, 4, T], F32)
    Ss = [A.alloc([128, 2, 512], F32) for _ in range(2)]
    Sbf2 = A.alloc([128, 2, 2, 512], BF16)
    dtabs = [A.alloc([128, 2, 128], F32) for _ in range(2)]
    Sd = A.alloc([128, 2, 128], BF16)
    Qc = A.alloc([128, 2, 2, 128], BF16)
    gnt = A.alloc([128, 4, T], F32)
    assert A.off <= NT * KC * 4, A.off
    ku, ktab, kscr, koh, kgn, kob = (C.key(n) for n in ("u", "tab", "scr", "oh", "gn", "ob"))
    kSs = [C.key("S0"), C.key("S1")]
    kdts = [C.key("dt0"), C.key("dt1")]
    ksd = [C.key("sd"), C.key("sd")]
    kqc = [C.key("qc"), C.key("qc")]
    kSbs = [C.key("Sb0"), C.key("Sb1")]
    kgst = [C.key("gst0"), C.key("gst1")]
    ktt1 = C.key("tt1")
    banks = Banks([0, 1, 2])
    tiles = [(0, 512), (512, 512)]
    W = D["ret_w"]
    stage = scr[:, 2:4, 0:512]
    gated_d = D["gated_d"]
    rhs_u = lambda k, ti: u[:, k, tiles[ti][0]:tiles[ti][0] + 512]

    def state_in(h):
        S, kS, dtab, kdt = Ss[h % 2], kSs[h % 2], dtabs[h % 2], kdts[h % 2]
        for c in range(4):
            P.dma("sp", stage, D["sall_ap"](c, h), reads=D["sall_keys"](h), writes=[kscr])
            if c == 0:
                P.op("dve", lambda e, c=c: e.tensor_scalar(
                    S, stage, D["coef"][:, c * 8 + h:c * 8 + h + 1], None, ALU.mult),
                    reads=[kscr, ("cst",)], writes=[kS])
            else:
                P.op("dve", lambda e, c=c: e.scalar_tensor_tensor(
                    S, stage, D["coef"][:, c * 8 + h:c * 8 + h + 1], S, ALU.mult, ALU.add),
                    reads=[kscr, ("cst",), kS], writes=[kS])
        P.dma("sp", dtab, D["dtab"][h], writes=[kdt])

    def proj_pieces(h):
        st = sets[h % 2]
        pieces = [lambda: state_in(h)]

        def cons_raw(mc, ti, ps, b):
            t0, n = tiles[ti]
            P.op("act", lambda e: e.activation(scr[:, mc, t0:t0 + n], ps, AF.Copy),
                 reads=[("ps", b)], writes=[kscr])
        pieces += linear_fm_pieces(P, C, W[h, 0], KC, tiles, rhs_u, [ku], cons_raw, banks)
        pieces.append(lambda: D["kv_load"](h, st["KT"], st["kkt"], st["V"], st["kv"]))

        def cons_g(mc, ti, ps, b):
            t0, n = tiles[ti]
            P.op("act", lambda e: e.activation(st["G"][:, mc, t0:t0 + n], ps, AF.Silu),
                 reads=[("ps", b)], writes=[st["kg"]])
        pieces += linear_fm_pieces(P, C, W[h, 4], KC, tiles, rhs_u, [ku], cons_g, banks, mc0=0)
        pieces += linear_fm_pieces(P, C, W[h, 5], KC, tiles, rhs_u, [ku], cons_g, banks, mc0=2)

        def ktm_piece(t_lo):
            for ti in range(t_lo, t_lo + 4):
                b = banks.next()
                for a in range(2):
                    mm(P, C.ps[b][:, a * 128:(a + 1) * 128], st["KT"][:, a, ti * 128:(ti + 1) * 128], C.ident[:],
                       True, True, reads=[st["kkt"], ("ident",)], writes=[("ps", b)], inc=(a == 1))
                P.op("dve", lambda e, ti=ti, b=b: e.tensor_scalar(
                    st["Ktm"][:, ti, :], C.ps[b][:, 0:256], D["sdcol"][:, h:h + 1], None, ALU.mult),
                    reads=[("ps", b), ("cst",)], writes=[st["kktm"]])
        pieces.append(lambda: ktm_piece(0))
        pieces.append(lambda: ktm_piece(4))
        pieces.append(lambda: ret_rotary(P, C, scr, kscr, cosT, sinT, ktab, 1.0, st["QT"], st["kqt"]))
        return pieces

    for pc in proj_pieces(0):
        pc()
    SB = (4, 3)
    for h in range(RET_H):
        st = sets[h % 2]
        QT, KT, Ktm, V, G = st["QT"], st["KT"], st["Ktm"], st["V"], st["G"]
        kqt, kkt, kktm, kv, kg = st["kqt"], st["kkt"], st["kktm"], st["kv"], st["kg"]
        gam = GAMMAS[h]
        nxt = proj_pieces(h + 1) if h + 1 < RET_H else []
        pi = [0]

        def emit_piece(cnt=1):
            for _ in range(cnt):
                if pi[0] < len(nxt):
                    nxt[pi[0]]()
                    pi[0] += 1
        S, kS, dtab, kdt = Ss[h % 2], kSs[h % 2], dtabs[h % 2], kdts[h % 2]
        P.op("act", lambda e, S=S: e.activation(Sbf2[:, 0, :, :], S, AF.Copy), reads=[kS], writes=[kSbs[0]])

        def scores(n):
            cs = slice(n * 128, (n + 1) * 128)
            for a in range(2):
                mm(P, C.ps[SB[n % 2]][:, 0:128], KT[:, a, cs], QT[:, a, cs], a == 0, a == 1,
                   reads=[kkt, kqt], writes=[("ps", SB[n % 2])])
        scores(0)
        for n in range(8):
            cs = slice(n * 128, (n + 1) * 128)
            pp = n % 2
            Sbf = Sbf2[:, pp, :, :]
            P.op("dve", lambda e, pp=pp, n=n, dtab=dtab: e.tensor_tensor(Sd[:, pp, :], C.ps[SB[n % 2]][:, 0:128],
                                                              dtab[:, 0, :], ALU.mult),
                 reads=[("ps", SB[n % 2]), kdt], writes=[ksd[pp]])
            P.op("dve", lambda e, pp=pp, cs=cs, QT=QT, dtab=dtab: e.tensor_tensor(
                Qc[:, pp, :, :], QT[:, :, cs], dtab[:, 1, :].unsqueeze(1).to_broadcast([128, 2, 128]), ALU.mult),
                reads=[kqt, kdt], writes=[kqc[pp]])
            if n < 7:
                for a in range(2):
                    mm(P, C.ps[6 + a][:, 0:512], Ktm[:, n, a * 128:(a + 1) * 128], V[:, n, :], True, True,
                       reads=[kktm, kv], writes=[("ps", 6 + a)])
                scores(n + 1)
            emit_piece(2 if n == 0 else 1)
            for m in range(4):
                ms = slice(m * 128, (m + 1) * 128)
                mm(P, C.ps[5][:, ms], V[:, n, ms], Sd[:, pp, :], True, False,
                   reads=[kv, ksd[pp]], writes=[("ps", 5)], inc=False)
                for a in range(2):
                    mm(P, C.ps[5][:, ms], Sbf[:, a, ms], Qc[:, pp, a, :], False, a == 1,
                       reads=[kSbs[pp], kqc[pp]], writes=[("ps", 5)], inc=(a == 1 and m == 3))
            if n < 7:
                for a in range(2):
                    P.op("dve", lambda e, a=a, gam=gam, S=S: e.scalar_tensor_tensor(
                        S[:, a, :], S[:, a, :], gam ** 128, C.ps[6 + a][:, 0:512], ALU.mult, ALU.add),
                        reads=[kS, ("ps", 6 + a)], writes=[kS])
                P.op("act", lambda e, pp=pp, S=S: e.activation(Sbf2[:, 1 - pp, :, :], S, AF.Copy),
                     reads=[kS], writes=[kSbs[1 - pp]])
            P.op("act", lambda e, n=n: e.activation(
                oh[:, :, (n % 4) * 128:(n % 4 + 1) * 128],
                C.ps[5][:, 0:512].rearrange("p (a b) -> p a b", a=4), AF.Copy),
                reads=[("ps", 5)], writes=[koh])
            if n % 4 == 3:
                half = n // 4
                t0 = half * T
                gi = (2 * h + half) % 2
                P.op("act", lambda e: e.activation(ob, oh, AF.Copy), reads=[koh], writes=[kob])
                P.op("act", lambda e: e.activation(osq, oh, AF.Square), reads=[koh], writes=[kob])
                for m in range(4):
                    mm(P, C.ps[6][:, 0:T], C.ones[:], ob[:, m, :], m == 0, m == 3,
                       reads=[kob, ("ones",)], writes=[("ps", 6)])
                for m in range(4):
                    mm(P, C.ps[7][:, 0:T], C.ones[:], osq[:, m, :], m == 0, m == 3,
                       reads=[kob, ("ones",)], writes=[("ps", 7)])
                mean, var, tt_ = gnt[:, 0, :], gnt[:, 1, :], gnt[:, 2, :]
                P.op("dve", lambda e: e.tensor_scalar(mean, C.ps[6][:, 0:T], 1.0 / 512, None, ALU.mult),
                     reads=[("ps", 6)], writes=[kgn])
                P.op("dve", lambda e: e.tensor_tensor(tt_, mean, mean, ALU.mult), reads=[kgn], writes=[kgn])
                P.op("dve", lambda e: e.scalar_tensor_tensor(var, C.ps[7][:, 0:T], 1.0 / 512, tt_,
                                                             ALU.mult, ALU.subtract),
                     reads=[("ps", 7), kgn], writes=[kgn])
                P.op("act", lambda e: e.activation(var, var, AF.Sqrt, bias=C.eps[:, 0:1]),
                     reads=[kgn, ("ones",)], writes=[kgn])
                P.op("dve", lambda e: e.reciprocal(var, var), reads=[kgn], writes=[kgn])
                tts = [gnt[:, 2, :], gnt[:, 3, :]]
                ktt = [kgn, ktt1]

                def gn_ops(m, ti_, h=h, G=G, t0=t0, gi=gi, mean=mean, var=var):
                    c = h * 4 + m
                    t_, kt_ = tts[ti_], ktt[ti_]
                    kw = kgst[gi] if m in (0, 3) else C.key("gstf")
                    return [
                        lambda: P.op("dve", lambda e: e.tensor_tensor(t_, oh[:, m, :], mean, ALU.subtract),
                                     reads=[koh, kgn], writes=[kt_]),
                        lambda: P.op("dve", lambda e: e.tensor_tensor(t_, t_, var, ALU.mult),
                                     reads=[kt_, kgn], writes=[kt_]),
                        lambda: P.op("dve", lambda e: e.tensor_scalar(
                            t_, t_, D["gn_g"][:, c:c + 1], D["gn_b"][:, c:c + 1], ALU.mult, ALU.add),
                            reads=[kt_, ("cst",)], writes=[kt_]),
                        lambda: P.op("dve", lambda e: e.tensor_tensor(
                            gst[:, gi, m, :], t_, G[:, m, t0:t0 + T], ALU.mult),
                            reads=[kt_, kg, kgst[gi]], writes=[kw]),
                    ]
                for m0 in (0, 2):
                    oa, ob_ = gn_ops(m0, 0), gn_ops(m0 + 1, 1)
                    for i in range(4):
                        oa[i]()
                        ob_[i]()
                        if i % 2 == 1:
                            emit_piece()
                P.dma("sp", gated_d[half][:, 4 * h:4 * h + 4, :], gst[:, gi, :, :],
                      reads=[kgst[gi]], writes=[("gd", half)])
        emit_piece(len(nxt))
    P.barrier()
    A.reset(0)
    hT2 = A.alloc([128, KC, NT], F32)
    f = A.alloc([128, KC, T], F32)
    sqo = A.alloc([128, KC, T], BF16)
    rstd2 = A.alloc([128, T], F32)
    C.make_ring(6)
    gt1 = A.alloc([128, 2 * KC, T], BF16)
    gts = [gt1, gt1]
    kf, ksqo, kr2 = C.key("f"), C.key("sqo"), C.key("r2")
    kg1 = C.key("gt")
    kgt = [kg1, kg1]
    banks = Banks([0, 1, 2, 3])
    P.dma("sp", gts[0].rearrange("p a b -> p (a b)"), gated_d[0].rearrange("p a b -> p (a b)"),
          reads=[("gd", 0)], writes=[kgt[0]])
    load_h(hT2)
    for half in range(2):
        t0 = half * T
        if half == 1:
            P.dma("sp", gts[1].rearrange("p a b -> p (a b)"), gated_d[1].rearrange("p a b -> p (a b)"),
                  reads=[("gd", 1)], writes=[kgt[1]])

        def cons_o(mc, ti, ps, b):
            P.op("act", lambda e: e.activation(f[:, mc, :], ps, AF.Copy), reads=[("ps", b)], writes=[kf])
        linear_fm(P, C, D["ret_wo"], 2 * KC, [(0, T)], lambda k, ti, half=half: gts[half][:, k, :], [kgt[half]],
                  cons_o, banks)
        emit_postnorm(P, C, f, hT2[:, :, t0:t0 + T], g_post, rstd2, sqo, T, kf, ksqo, kr2)
    P.barrier()
    return hT2


def phase_pool2(P, C, hT, g_pre, g_post, halo_fill, poolw_d, scale, band_d):
    A = C.A
    T = 512
    NX = NT + 16
    A.reset(NT * KC * 4)
    rstdx = A.alloc([128, NX], F32)
    halo = A.alloc([128, KC, 16], F32)
    stage16 = A.alloc([128, KC, 16], F32)
    ub_flat = A.alloc([128, KC * NX], BF16)
    ub = ub_flat.rearrange("p (a b) -> p a b", a=KC)
    mixed = ub_flat[:, 0:KC * NT].rearrange("p (a b) -> p a b", a=KC)
    utm_flat = A.alloc([128, KC * NT], BF16)
    utm = utm_flat.rearrange("p (c j f) -> p c j f", c=KC, j=8)
    sq = utm_flat[:, 0:KC * T].rearrange("p (a b) -> p a b", a=KC)
    f = utm_flat.bitcast(F32).rearrange("p (a b) -> p a b", a=KC)
    uhtm = A.alloc([128, KC, 128], BF16)
    wp = A.alloc([128, 4, 4, 512], BF16)
    bt = A.alloc([128, 4, 4, 128], BF16)
    rstd = A.alloc([128, T], F32)
    sqp = A.alloc([128, KC, T], BF16)
    khalo, krx, kub, kutm, kuh, kwp, kbt, krstd, ksqp = (C.key(n) for n in
        ("halo", "rstdx", "ub", "utm", "uh", "wp", "bt", "rstd", "sqp"))
    halo_fill(halo, khalo, stage16, C.key("st16"))
    for g in range(4):
        P.dma("pool", wp[:, g, :, :], poolw_d[g], writes=[kwp])
    P.dma("pool", bt.rearrange("p a b c -> p (a b c)"), band_d.rearrange("p a b c -> p (a b c)"), writes=[kbt])
    P.op("act", lambda e: e.activation(sq[:, :, 0:16], halo, AF.Square), reads=[khalo], writes=[kutm])
    emit_rstd(P, C, sq[:, :, 0:16], KC, 16, rstdx[:, 0:16], 7, D_MODEL, kutm, krx)
    for t0 in range(0, NT, T):
        P.op("act", lambda e, t0=t0: e.activation(sq, hT[:, :, t0:t0 + T], AF.Square),
             reads=[("h",)], writes=[kutm])
        emit_rstd(P, C, sq, KC, T, rstdx[:, 16 + t0:16 + t0 + T], 7, D_MODEL, kutm, krx)
    for c in range(KC):
        P.op("dve", lambda e, c=c: e.scalar_tensor_tensor(
            ub[:, c, 0:16], halo[:, c, :], g_pre[:, c:c + 1], rstdx[:, 0:16], ALU.mult, ALU.mult),
            reads=[khalo, krx, ("cst",)], writes=[kub])
        P.op("dve", lambda e, c=c: e.scalar_tensor_tensor(
            ub[:, c, 16:NX], hT[:, c, :], g_pre[:, c:c + 1], rstdx[:, 16:NX], ALU.mult, ALU.mult),
            reads=[("h",), krx, ("cst",)], writes=[kub])
    banks = Banks([0, 1, 2, 3, 4, 5])
    ev = [0]

    def evac(dst, src, reads, writes):
        eng = "act" if ev[0] % 2 == 0 else "dve"
        ev[0] += 1
        if eng == "act":
            P.op("act", lambda e: e.activation(dst, src, AF.Copy), reads=reads, writes=writes)
        else:
            P.op("dve", lambda e: e.tensor_copy(dst, src), reads=reads, writes=writes)
    for cg in range(KC // 4):
        b = banks.next()
        for q in range(4):
            c = cg * 4 + q
            mm(P, C.ps[b][0:16, q * 128:(q + 1) * 128], ub[:, c, 0:16], C.ident[:], True, True,
               reads=[kub, ("ident",)], writes=[("ps", b)], inc=(q == 3))
        evac(uhtm[0:16, cg * 4:cg * 4 + 4, :], C.ps[b][0:16, 0:512].rearrange("p (a b) -> p a b", a=4),
             [("ps", b)], [kuh])
    for c in range(KC):
        for jg in range(2):
            b = banks.next()
            for q in range(4):
                j = jg * 4 + q
                mm(P, C.ps[b][:, q * 128:(q + 1) * 128], ub[:, c, 16 + j * 128:16 + (j + 1) * 128], C.ident[:],
                   True, True, reads=[kub, ("ident",)], writes=[("ps", b)], inc=(q == 3))
            evac(utm[:, c, jg * 4:jg * 4 + 4, :], C.ps[b][:, 0:512].rearrange("p (a b) -> p a b", a=4),
                 [("ps", b)], [kutm])
    for c in range(KC):
        g = c // 4
        for jg in range(2):
            b = banks.next()
            for q in range(4):
                j = jg * 4 + q
                o = C.ps[b][:, q * 128:(q + 1) * 128]
                cur = bt[:, g, 2, :] if j == 0 else bt[:, g, 0, :]
                mm(P, o, utm[:, c, j, :], cur, True, False, reads=[kutm, kbt], writes=[("ps", b)], inc=False)
                if j == 0:
                    mm(P, o, uhtm[0:16, c, :], bt[0:16, g, 3, :], False, True,
                       reads=[kuh, kbt], writes=[("ps", b)], inc=(q == 3))
                else:
                    mm(P, o, utm[:, c, j - 1, :], bt[:, g, 1, :], False, True,
                       reads=[kutm, kbt], writes=[("ps", b)], inc=(q == 3))
            evac(mixed[:, c, jg * 512:(jg + 1) * 512], C.ps[b][:, 0:512], [("ps", b)], [kub])
    banks = Banks([0, 1, 2, 3])
    for t0 in range(0, NT, T):
        for g in range(4):
            for m in range(4):
                b = banks.next()
                c = g * 4 + m
                for k in range(4):
                    mm(P, C.ps[b][:, 0:T], wp[:, g, k, m * 128:(m + 1) * 128], mixed[:, g * 4 + k, t0:t0 + T],
                       k == 0, k == 3, reads=[kwp, kub], writes=[("ps", b)])
                P.op("dve", lambda e, c=c, b=b: e.tensor_scalar(
                    f[:, c, :], C.ps[b][:, 0:T], scale[:, c:c + 1], None, ALU.mult),
                    reads=[("ps", b), ("cst",)], writes=[kutm])
        emit_postnorm(P, C, f, hT[:, :, t0:t0 + T], g_post, rstd, sqp, T, kutm, ksqp, krstd)
    P.barrier()


CST_ITEMS = [
    ("ln", (4, 6, KC)), ("pool_scale", (2, KC)), ("pool_corr", (4, 16)),
    ("bq", (KC,)), ("bqs", (KC,)), ("bk", (4,)), ("bks", (4,)), ("bo", (KC,)),
    ("invf_swa", (1,)), ("sgn", (1,)),
    ("gn_g", (2 * KC,)), ("gn_b", (2 * KC,)), ("invf_ret", (1,)), ("sdcol", (8,)),
    ("dloc", (8, 8)), ("coef", (64,)), ("sel", (4,)),
]
CST_OFF = {}
_o = 0
for _n, _s in CST_ITEMS:
    _sz = int(np.prod(_s))
    CST_OFF[_n] = (_o, _s)
    _o += _sz
NCST = _o


def cst_view(C, name):
    o, shp = CST_OFF[name]
    n = int(np.prod(shp))
    v = C.cst[:, o:o + n]
    if len(shp) == 2:
        v = v.rearrange("p (a b) -> p a b", a=shp[0])
    elif len(shp) == 3:
        v = v.rearrange("p (a b c) -> p a b c", a=shp[0], b=shp[1])
    return v


def prologue(P, C):
    cst_d = P.dram_in("cst", [128, NCST], F32)
    P.dma("sp", C.cst[:], cst_d, writes=[("cst",)])
    ident_d = P.dram_in("ident", [128, 128], F32)
    P.dma("pool", C.ident[:], ident_d, writes=[("ident",)])
    perm_d = P.dram_in("perm", [128, 128], F32)
    P.dma("pool", C.perm[:], perm_d, writes=[("ident",)])
    ln = cst_view(C, "ln")
    for i in range(4):
        for s in (1, 5):
            P.op("dve", lambda e, i=i, s=s: e.tensor_scalar(ln[:, i, s, :], ln[:, i, s, :], 0.5, None, ALU.mult),
                 reads=[("cst",)], writes=[("cst",)])
    return ln


GROUPS = [[0, 1, 2, 3], [4, 5, 6, 7]]


def build_fused():
    P = Prog()
    C = Ctx(P, NCST)
    nc = P.nc
    ln = prologue(P, C)
    hT_d = P.dram_in("hT", [KC * 128, NT], F32)
    out_d = P.dram_out("outT", [KC * 128, NT], F32)
    hT = C.A.t[:, 0:KC * NT].rearrange("p (a b) -> p a b", a=KC)
    sel = cst_view(C, "sel")
    hT_v = hT_d.rearrange("(c p) t -> p c t", p=128)
    out_v = out_d.rearrange("(c p) t -> p c t", p=128)
    for ti in (1, 0):
        P.dma("sp", hT[:, :, ti * 512:(ti + 1) * 512], hT_v[:, :, ti * 512:(ti + 1) * 512], writes=[("h", ti)])

    def ffn_seq(*specs, hook=None, first=False):
        ffns = []
        for (i, s) in specs:
            win_d = P.dram_in("win_%d_%d" % (i, s), [FC, 128, KC, 256], F32)
            wout_d = P.dram_in("wout_%d_%d" % (i, s), [KC // 2, 128, FC, 256], F32)
            base = 0 if s == 0 else 4
            ffns.append((ln[:, i, base, :], ln[:, i, base + 1, :], win_d, wout_d))
        phase_ffn_seq(P, C, hT, ffns, hook=hook, first=first)

    xbuf = {}

    def ex_send(w, tag):
        src = nc.dram_tensor("xs_" + tag, [KC * 128, w], F32).ap()
        dst = nc.dram_tensor("xd_" + tag, [4 * KC * 128, w], F32).ap()
        xbuf[tag] = dst

        def hook(hk):
            P.dma("sp", src.rearrange("(c p) t -> p c t", p=128), hT[:, :, NT - w:NT],
                  reads=[hk], writes=[("xs", tag)])
            P.op("pool", lambda e: e.collective_compute("AllGather", ALU.bypass, replica_groups=GROUPS,
                                                        ins=[src.opt()], outs=[dst.opt()]),
                 reads=[("xs", tag)], writes=[("xd", tag)])
        return hook

    def exchange(w, tag):
        dst = xbuf[tag]

        def fill(halo, kh, stage, kst):
            for r in range(4):
                P.dma("sp", stage, dst[r * KC * 128:(r + 1) * KC * 128, :].rearrange("(c p) t -> p c t", p=128),
                      reads=[("xd", tag)], writes=[kst])
                if r == 0:
                    P.op("dve", lambda e: e.tensor_scalar(halo, stage, sel[:, 0:1], None, ALU.mult),
                         reads=[kst, ("cst",)], writes=[kh])
                else:
                    P.op("dve", lambda e, r=r: e.scalar_tensor_tensor(halo, stage, sel[:, r:r + 1], halo,
                                                                      ALU.mult, ALU.add),
                         reads=[kst, ("cst",), kh], writes=[kh])
        return fill

    def pool(i, j, tag):
        fill = exchange(16, tag)
        pw_d = P.dram_in("poolw%d" % j, [4, 128, 4, 512], F32)
        if "band" not in xbuf:
            xbuf["band"] = P.dram_in("pool_band", [128, 4, 4, 128], F32)
        phase_pool2(P, C, hT, ln[:, i, 2, :], ln[:, i, 3, :], fill, pw_d,
                    cst_view(C, "pool_scale")[:, j, :], xbuf["band"])

    ffn_seq((0, 0), hook=ex_send(16, "a"), first=True)
    pool(0, 0, "a")
    ffn_seq((0, 1), (1, 0))
    posr_d = P.dram_in("posr", [128, NT], I32)
    retw_d = P.dram_in("ret_w", [8, 6, 128, KC, 256], F32)
    sl_src = nc.dram_tensor("sl_src", [8, 256, 512], F32).ap()
    sl_dst = nc.dram_tensor("sl_dst", [8, 4 * 256, 512], F32).ap()

    def sloc_store(hh, Sst, kS):
        P.dma("sp", sl_src[hh].rearrange("(a p) v -> p a v", p=128), Sst, reads=[kS], writes=[("sls", hh)])

    def sloc_coll(hh):
        P.op("pool", lambda e: e.collective_compute("AllGather", ALU.bypass, replica_groups=GROUPS,
                                                    ins=[sl_src[hh].opt()], outs=[sl_dst[hh].opt()]),
             reads=[("sls", hh)], writes=[("sld", hh)])
    kt_s = nc.dram_tensor("kt_scr", [8, 128, 2 * NT], BF16).ap()
    v_s = nc.dram_tensor("v_scr", [8, 128, 8 * 512], BF16).ap()

    def kv_store(hh, KT, kkt, V, kv):
        P.dma("sp", kt_s[hh], KT.rearrange("p a b -> p (a b)"), reads=[kkt], writes=[("kts", hh)])
        P.dma("sp", v_s[hh], V.rearrange("p a b -> p (a b)"), reads=[kv], writes=[("vs", hh)])

    def kv_load(hh, KT, kkt, V, kv):
        P.dma("sp", KT.rearrange("p a b -> p (a b)"), kt_s[hh], reads=[("kts", hh)], writes=[kkt])
        P.dma("sp", V.rearrange("p a b -> p (a b)"), v_s[hh], reads=[("vs", hh)], writes=[kv])
    DA = {"posr": posr_d, "ret_w": retw_d, "sloc_store": sloc_store, "sloc_coll": sloc_coll, "kv_store": kv_store,
          "invf_ret": cst_view(C, "invf_ret"), "dloc": cst_view(C, "dloc")}
    phase_retA(P, C, hT, ln[:, 1, 2, :], DA)
    hsp = nc.dram_tensor("hspill", [KC * 128, NT], F32).ap()
    P.dma("sp", hsp.rearrange("(c p) t -> p c t", p=128), hT, reads=[("h",)], writes=[("hsp",)])
    P.barrier(skip_pool=True)

    def load_h2(dst):
        P.dma("sp", dst, hsp.rearrange("(c p) t -> p c t", p=128), reads=[("hsp",)], writes=[("h",)])
    rwo_d = P.dram_in("ret_wo", [KC, 128, 2 * KC, 128], F32)
    DB = {"posr": posr_d, "ret_w": retw_d, "kv_load": kv_load,
          "dtab": P.dram_in("dtab", [8, 128, 2, 128], F32),
          "ret_wo": [rwo_d[m] for m in range(KC)],
          "invf_ret": cst_view(C, "invf_ret"), "sdcol": cst_view(C, "sdcol"),
          "coef": cst_view(C, "coef"), "gn_g": cst_view(C, "gn_g"), "gn_b": cst_view(C, "gn_b"),
          "sall_ap": (lambda c, hh: sl_dst[hh][c * 256:(c + 1) * 256, :].rearrange("(a p) v -> p a v", p=128)),
          "sall_keys": (lambda hh: [("sld", hh)])}
    DB["reuse"] = DA["retA_state"]
    DB["gated_d"] = nc.dram_tensor("gated_scr", [2, 128, 2 * KC, 512], BF16).ap()
    phase_retB2(P, C, hT, ln[:, 1, 3, :], DB, load_h2)
    ffn_seq((1, 1), (2, 0), hook=ex_send(128, "b"))
    def units(name, n):
        d = P.dram_in(name, [n, 128, KC, 128], F32)
        return [d[m] for m in range(n)]
    DS = {"halo_fill": exchange(128, "b"),
          "posx": P.dram_in("posx", [128, NT + 128], I32),
          "tab": P.dram_in("swa_tab", [128, 928], F32),
          "wq": units("swa_wq", 16), "wk": units("swa_wk", 4),
          "wv": units("swa_wv", 4), "wo": units("swa_wo", 16),
          "bq": cst_view(C, "bq"), "bqs": cst_view(C, "bqs"), "bk": cst_view(C, "bk"),
          "bks": cst_view(C, "bks"), "bo": cst_view(C, "bo"),
          "invf": cst_view(C, "invf_swa"), "sgn": cst_view(C, "sgn"), "perm": C.perm[:]}
    phase_swa(P, C, hT, ln[:, 2, 2, :], ln[:, 2, 3, :], DS)
    ffn_seq((2, 1), (3, 0), hook=ex_send(16, "c"))
    pool(3, 1, "c")
    ffn_seq((3, 1), hook=lambda hk: P.dma("sp", out_v[:, :, 512:1024], hT[:, :, 512:1024], reads=[hk]))
    P.dma("sp", out_v[:, :, 0:512], hT[:, :, 0:512], reads=[("h",)])
    print("sem counts", P.ecnt, max(P.dcnt))
    return P.finish()


FUSED_INPUTS = (["ident", "perm"] + ["win_%d_%d" % (i, s) for i in range(4) for s in range(2)]
                + ["wout_%d_%d" % (i, s) for i in range(4) for s in range(2)]
                + ["poolw0", "poolw1", "pool_band", "posr", "ret_w", "dtab", "ret_wo", "posx", "swa_tab",
                   "swa_wq", "swa_wk", "swa_wv", "swa_wo"])


def kernel_fused(**inputs):
    host = Host(inputs)
    x = host.inp["x"]
    nc = build_fused()
    in_maps = []
    for c in range(NCORES):
        b, s0 = host.core_info(c)
        m = {"hT": np.ascontiguousarray(x[b, s0:s0 + NT, :].T), "cst": host.cst(c)}
        for n in FUSED_INPUTS:
            m[n] = host.const(n, c if n in PER_CORE else None)
        in_maps.append(m)
    res = run_bass_kernel_spmd(nc, in_maps, core_ids=list(range(NCORES)))
    out = np.empty((BATCH, SEQ, D_MODEL), np.float32)
    for c in range(NCORES):
        b, s0 = host.core_info(c)
        out[b, s0:s0 + NT, :] = np.asarray(res.results[c]["outT"]).T
    return out


def build_launch(phases):
    P = Prog()
    C = Ctx(P, NCST)
    ln = prologue(P, C)
    hT_d = P.dram_in("hT", [KC * 128, NT], F32)
    out_d = P.dram_out("outT", [KC * 128, NT], F32)
    hT = C.A.t[:, 0:KC * NT].rearrange("p (a b) -> p a b", a=KC)

    def load_h(dst):
        P.dma("sp", dst, hT_d.rearrange("(c p) t -> p c t", p=128), writes=[("h",)])

    for pi, ph in enumerate(phases):
        kind = ph[0]
        if pi == 0 and kind != "retB":
            load_h(hT)
        if kind == "ffn":
            i, s = ph[1], ph[2]
            win_d = P.dram_in("win_%d_%d" % (i, s), [FC, 128, KC, 256], F32)
            wout_d = P.dram_in("wout_%d_%d" % (i, s), [KC // 2, 128, FC, 256], F32)
            base = 0 if s == 0 else 4
            phase_ffn(P, C, hT, ln[:, i, base, :], ln[:, i, base + 1, :], win_d, wout_d)
        elif kind == "pool":
            i, j = ph[1], ph[2]
            halo_d = P.dram_in("halo16", [128, KC, 16], F32)
            pw_d = P.dram_in("poolw", [4, 128, 4, 512], F32)
            band_d = P.dram_in("pool_band", [128, 4, 4, 128], F32)
            phase_pool2(P, C, hT, ln[:, i, 2, :], ln[:, i, 3, :],
                        (lambda halo, kh, st, kst, halo_d=halo_d: P.dma("sp", halo, halo_d, writes=[kh])), pw_d,
                        cst_view(C, "pool_scale")[:, j, :], band_d)
        elif kind == "retA":
            i = ph[1]
            D = {"posr": P.dram_in("posr", [128, NT], I32),
                 "ret_w": P.dram_in("ret_w", [8, 6, 128, KC, 256], F32),
                 "invf_ret": cst_view(C, "invf_ret"), "dloc": cst_view(C, "dloc")}
            sloc_d = P.dram_out("sloc", [8, 2, 128, 512], F32)
            D["sloc_store"] = (lambda hh, Sst, kS, sloc_d=sloc_d:
                               P.dma("sp", sloc_d[hh].rearrange("a p v -> p a v"), Sst, reads=[kS]))
            phase_retA(P, C, hT, ln[:, i, 2, :], D)
        elif kind == "retB":
            i = ph[1]
            rwo_d = P.dram_in("ret_wo", [KC, 128, 2 * KC, 128], F32)
            D = {"posr": P.dram_in("posr", [128, NT], I32),
                 "ret_w": P.dram_in("ret_w", [8, 6, 128, KC, 256], F32),
                 "sall": P.dram_in("sall", [4, 8, 2, 128, 512], F32),
                 "dtab": P.dram_in("dtab", [8, 128, 2, 128], F32),
                 "ret_wo": [rwo_d[m] for m in range(KC)],
                 "invf_ret": cst_view(C, "invf_ret"), "sdcol": cst_view(C, "sdcol"),
                 "coef": cst_view(C, "coef"), "gn_g": cst_view(C, "gn_g"), "gn_b": cst_view(C, "gn_b")}
            D["sall_ap"] = (lambda c, hh, D=D: D["sall"][c, hh].rearrange("a p v -> p a v"))
            D["sall_keys"] = (lambda hh: [])
            phase_retB(P, C, hT, ln[:, i, 2, :], ln[:, i, 3, :], D, load_h)
        elif kind == "swa":
            i = ph[1]
            def units(name, n):
                d = P.dram_in(name, [n, 128, KC, 128], F32)
                return [d[m] for m in range(n)]
            halo_d = P.dram_in("halo128", [128, KC, 128], F32)
            D = {"halo_fill": (lambda halo, kh, st, kst, halo_d=halo_d: P.dma("sp", halo, halo_d, writes=[kh])),
                 "posx": P.dram_in("posx", [128, NT + 128], I32),
                 "tab": P.dram_in("swa_tab", [128, 928], F32),
                 "wq": units("swa_wq", 16), "wk": units("swa_wk", 4),
                 "wv": units("swa_wv", 4), "wo": units("swa_wo", 16),
                 "bq": cst_view(C, "bq"), "bqs": cst_view(C, "bqs"), "bk": cst_view(C, "bk"),
                 "bks": cst_view(C, "bks"), "bo": cst_view(C, "bo"),
                 "invf": cst_view(C, "invf_swa"), "sgn": cst_view(C, "sgn"), "perm": C.perm[:]}
            phase_swa(P, C, hT, ln[:, i, 2, :], ln[:, i, 3, :], D)
        else:
            raise ValueError(kind)
    P.dma("sp", out_d.rearrange("(c p) t -> p c t", p=128), hT, reads=[("h",)])
    return P.finish()


def fm_vec(v):
    v = np.asarray(v, np.float32)
    return np.ascontiguousarray(v.reshape(-1, 128).T)


def tile_w(w, UC):
    K_, N_ = w.shape
    return np.ascontiguousarray(w.reshape(K_ // 128, 128, N_ // UC, UC).transpose(2, 1, 0, 3))


def tile_win(w_in):
    g = w_in[:, :D_FF].reshape(KC, 128, FC, 128)
    u = w_in[:, D_FF:].reshape(KC, 128, FC, 128)
    out = np.empty((FC, 128, KC, 256), np.float32)
    out[:, :, :, :128] = g.transpose(2, 1, 0, 3)
    out[:, :, :, 128:] = u.transpose(2, 1, 0, 3)
    return out


def tile_wout(w_out):
    w = w_out.reshape(FC, 128, KC // 2, 256)
    return np.ascontiguousarray(w.transpose(2, 1, 0, 3))


def swap_halves(x, hd=64):
    s = x.shape
    y = x.reshape(s[:-1] + (s[-1] // hd, 2, hd // 2))[..., ::-1, :]
    return np.ascontiguousarray(y).reshape(s)


def dup_heads(x, hd=64):
    s = x.shape
    y = x.reshape(s[:-1] + (s[-1] // hd, 1, hd))
    y = np.repeat(y, 2, axis=-2)
    return np.ascontiguousarray(y).reshape(s[:-1] + (2 * s[-1],))


def hT_to_halo(hT, w):
    return np.ascontiguousarray(hT[:, NT - w:].reshape(KC, 128, w).transpose(1, 0, 2))


class Host:
    def __init__(self, inp):
        self.inp = {k: np.asarray(v) for k, v in inp.items()}
        self.cache = {}

    def core_info(self, c):
        return c // 4, (c % 4) * NT

    def cst(self, c):
        I = self.inp
        b, s0 = self.core_info(c)
        out = np.zeros((128, NCST), np.float32)

        def put(name, arr):
            o, shp = CST_OFF[name]
            n = int(np.prod(shp))
            out[:, o:o + n] = np.asarray(arr, np.float32).reshape(128, n)
        ln = np.zeros((128, 4, 6, KC), np.float32)
        for i in range(4):
            for k, nm in enumerate(("ln_ffn1", "ln_mix", "ln_ffn2")):
                for s in range(2):
                    ln[:, i, 2 * k + s, :] = fm_vec(I[nm][i, s])
        put("ln", ln)
        put("pool_scale", np.stack([fm_vec(I["pool_scale"][j]) for j in range(2)], axis=1))
        t = np.arange(16) + s0 + 1
        corr = np.stack([1.0 / np.minimum(t, w) for w in (2, 4, 8, 16)], axis=0)
        put("pool_corr", np.broadcast_to(corr[None], (128, 4, 16)))
        bi = I["swa_b_in"][0]
        put("bq", fm_vec(bi[:2048]))
        put("bqs", fm_vec(swap_halves(bi[:2048])))
        put("bk", fm_vec(dup_heads(bi[2048:2304])))
        put("bks", fm_vec(dup_heads(swap_halves(bi[2048:2304]))))
        put("bo", fm_vec(I["swa_b_out"][0]))
        p = np.arange(128)
        put("invf_swa", (10000.0 ** (-(2.0 * (p % 32)) / 64.0)).astype(np.float32)[:, None])
        put("sgn", np.where((p % 64) < 32, -1.0, 1.0)[:, None])
        put("gn_g", fm_vec(I["ret_gn_g"][0]))
        put("gn_b", fm_vec(I["ret_gn_b"][0]))
        put("invf_ret", (10000.0 ** (-np.linspace(0.0, 1.0, 128, dtype=np.float32))).astype(np.float32)[:, None])
        gam = np.array(GAMMAS, np.float64)
        put("sdcol", gam[None, :] ** (127.0 - p[:, None]))
        tt = (np.arange(8)[None, :, None] * 128 + p[:, None, None])
        put("dloc", gam[None, None, :] ** (1023.0 - tt))
        coef = np.zeros((8, 8))
        r = c % 4
        for r2 in range(4):
            if r2 < r:
                coef[r2] = gam ** (1024.0 * (r - 1 - r2))
        put("coef", np.broadcast_to(coef.reshape(1, 64), (128, 64)))
        sel = np.zeros(4)
        if r > 0:
            sel[r - 1] = 1.0
        put("sel", np.broadcast_to(sel.reshape(1, 4), (128, 4)))
        return out

    def const(self, name, c=None):
        I = self.inp
        key = (name, c)
        if key in self.cache:
            return self.cache[key]
        if name == "ident":
            v = np.eye(128, dtype=np.float32)
        elif name == "perm":
            v = np.zeros((128, 128), np.float32)
            pp = np.arange(128)
            v[pp, pp ^ 32] = 1.0
        elif name.startswith("win_"):
            _, i, s = name.split("_")
            v = tile_win(I["ffn_w_in"][int(i), int(s)])
        elif name.startswith("wout_"):
            _, i, s = name.split("_")
            v = tile_wout(I["ffn_w_out"][int(i), int(s)])
        elif name.startswith("poolw"):
            j = int(name[5:])
            v = np.ascontiguousarray(I["pool_w"][j].reshape(4, 4, 128, 512).transpose(0, 2, 1, 3))
        elif name == "pool_band":
            b, s0 = self.core_info(c)
            v = np.zeros((128, 4, 4, 128), np.float32)
            s_ = np.arange(128)[:, None]
            t_ = np.arange(128)[None, :]
            for g, w in enumerate((2, 4, 8, 16)):
                cur = ((t_ - s_ >= 0) & (t_ - s_ < w)) / float(w)
                prev = ((t_ - (s_ - 128)) < w) / float(w)
                if s0 == 0:
                    cur0 = ((t_ - s_ >= 0) & (t_ - s_ < w)) / np.minimum(t_ + 1.0, float(w))
                else:
                    cur0 = cur
                v[:, g, 0, :] = cur - np.eye(128)
                v[:, g, 1, :] = prev
                v[:, g, 2, :] = cur0 - np.eye(128)
                v[0:16, g, 3, :] = prev[112:128, :]
        elif name == "ret_w":
            w = I["ret_w_in"][0]
            v = np.empty((8, 6, 128, KC, 256), np.float32)
            for h in range(8):
                cols = [w[:, h * 256:(h + 1) * 256], w[:, 2048 + h * 256:2048 + (h + 1) * 256],
                        w[:, 4096 + h * 512:4096 + h * 512 + 256], w[:, 4096 + h * 512 + 256:4096 + (h + 1) * 512],
                        w[:, 8192 + h * 512:8192 + h * 512 + 256], w[:, 8192 + h * 512 + 256:8192 + (h + 1) * 512]]
                for ui, x in enumerate(cols):
                    v[h, ui] = x.reshape(KC, 128, 256).transpose(1, 0, 2)
        elif name == "ret_wo":
            v = tile_w(I["ret_w_out"][0], 128)
        elif name == "dtab":
            gam = np.array(GAMMAS, np.float64)
            i_ = np.arange(128)
            rel = i_[None, :] - i_[:, None]
            v = np.zeros((8, 128, 2, 128), np.float32)
            for h in range(8):
                v[h, :, 0, :] = np.where(rel >= 0, gam[h] ** np.maximum(rel, 0), 0.0)
                v[h, :, 1, :] = (gam[h] ** (i_ + 1.0))[None, :]
        elif name in ("swa_wq", "swa_wqs", "swa_wk", "swa_wks", "swa_wv", "swa_wo"):
            w = I["swa_w_in"][0]
            if name == "swa_wq":
                v = tile_w(w[:, :2048], 128)
            elif name == "swa_wqs":
                v = tile_w(swap_halves(w[:, :2048]), 128)
            elif name == "swa_wk":
                v = tile_w(dup_heads(w[:, 2048:2304]), 128)
            elif name == "swa_wks":
                v = tile_w(dup_heads(swap_halves(w[:, 2048:2304])), 128)
            elif name == "swa_wv":
                v = tile_w(dup_heads(w[:, 2304:2560]), 128)
            else:
                v = tile_w(I["swa_w_out"][0], 128)
        elif name == "posr":
            b, s0 = self.core_info(c)
            v = np.ascontiguousarray(np.broadcast_to(I["positions"][b, s0:s0 + NT].astype(np.int32)[None], (128, NT)))
        elif name == "posx":
            b, s0 = self.core_info(c)
            pos = np.zeros(NT + 128, np.int32)
            pos[128:] = I["positions"][b, s0:s0 + NT]
            if s0 > 0:
                pos[:128] = I["positions"][b, s0 - 128:s0]
            v = np.ascontiguousarray(np.broadcast_to(pos[None], (128, NT + 128)))
        elif name == "swa_tab":
            b, s0 = self.core_info(c)
            tab = np.zeros((128, 928), np.float32)
            tab[:, 0:512] = dup_heads(I["swa_b_in"][0][2304:2560])[None, :]
            j = np.arange(128)[:, None]
            i_ = np.arange(128)[None, :]
            tab[:, 512:640] = (j <= i_)
            tab[:, 640:768] = (j > i_)
            tab[:, 768:896] = (j > i_) if s0 > 0 else 0.0
            tab[:, 896:928] = I["swa_sinks"][0][None, :]
            v = tab
        else:
            raise KeyError(name)
        self.cache[key] = v
        return v


PER_CORE = ("posr", "posx", "swa_tab", "pool_band")


def run_launch(host, phases, hTs, extra):
    nc = build_launch(phases)
    names = ["ident", "perm"]
    rename = {}
    for ph in phases:
        if ph[0] == "ffn":
            names += ["win_%d_%d" % (ph[1], ph[2]), "wout_%d_%d" % (ph[1], ph[2])]
        elif ph[0] == "pool":
            rename["poolw"] = "poolw%d" % ph[2]
            names += ["poolw", "pool_band"]
        elif ph[0] == "retA":
            names += ["posr", "ret_w"]
        elif ph[0] == "retB":
            names += ["posr", "ret_w", "dtab", "ret_wo"]
        elif ph[0] == "swa":
            names += ["posx", "swa_tab", "swa_wq", "swa_wk", "swa_wv", "swa_wo"]
    in_maps = []
    for c in range(NCORES):
        m = {"hT": hTs[c], "cst": host.cst(c)}
        for n in names:
            src = rename.get(n, n)
            m[n] = host.const(src, c if src in PER_CORE else None)
        for k, v in extra.items():
            m[k] = v[c]
        in_maps.append(m)
    res = run_bass_kernel_spmd(nc, in_maps, core_ids=list(range(NCORES)))
    return res.results


LAUNCHES = [
    [("ffn", 0, 0)],
    [("pool", 0, 0), ("ffn", 0, 1), ("ffn", 1, 0), ("retA", 1)],
    [("retB", 1), ("ffn", 1, 1), ("ffn", 2, 0)],
    [("swa", 2), ("ffn", 2, 1), ("ffn", 3, 0)],
    [("pool", 3, 1), ("ffn", 3, 1)],
]


def halos(hTs, w):
    out = []
    for c in range(NCORES):
        if c % 4 == 0:
            out.append(np.zeros((128, KC, w), np.float32))
        else:
            out.append(hT_to_halo(hTs[c - 1], w))
    return out


def kernel(**inputs):
    return kernel_fused(**inputs)


def kernel_unfused(**inputs):
    host = Host(inputs)
    x = host.inp["x"]
    hTs = []
    for c in range(NCORES):
        b, s0 = host.core_info(c)
        hTs.append(np.ascontiguousarray(x[b, s0:s0 + NT, :].T))
    sall = None
    for li, phases in enumerate(LAUNCHES):
        extra = {}
        kinds = [p[0] for p in phases]
        if "pool" in kinds:
            extra["halo16"] = halos(hTs, 16)
        if "swa" in kinds:
            extra["halo128"] = halos(hTs, 128)
        if "retB" in kinds:
            extra["sall"] = [np.ascontiguousarray(sall[4 * (c // 4):4 * (c // 4) + 4]) for c in range(NCORES)]
        res = run_launch(host, phases, hTs, extra)
        hTs = [np.asarray(r["outT"]) for r in res]
        if "retA" in kinds:
            sall = np.ascontiguousarray(np.stack([np.asarray(r["sloc"]) for r in res], axis=0))
    out = np.empty((BATCH, SEQ, D_MODEL), np.float32)
    for c in range(NCORES):
        b, s0 = host.core_info(c)
        out[b, s0:s0 + NT, :] = hTs[c].T
    return out
```
